# Optimizing a Trainium2 kernel written in Bass

```python
import functools
import jax, jax.numpy as jnp
from jax import lax
import numpy as np

D_MODEL = 1024
BATCH = 2
SEQ = 8192
DEPTH = 1
DEC_BATCH = 32
DEC_SEQ = 8
PAST_LEN = 8192
PAGE_SIZE = 128

A_HEADS = 8
A_HEAD_DIM = 64
A_WIDTH = A_HEADS * A_HEAD_DIM
Q_BLOCK = 128
B_HEADS = 4
B_KEY_DIM = 128
B_VAL_DIM = 256
B_KEY_WIDTH = B_HEADS * B_KEY_DIM
B_VAL_WIDTH = B_HEADS * B_VAL_DIM
GATE_RANK = 16
GATE_TEMP = 16.0
GLA_CHUNK = 64
D_FF = 2816
CONV_WIDTH = 3
EPS = 1e-6

IN_SPLITS = (A_WIDTH, A_WIDTH, A_WIDTH, A_HEADS, B_KEY_WIDTH, B_KEY_WIDTH, B_VAL_WIDTH, GATE_RANK, B_VAL_WIDTH, D_MODEL, D_MODEL)
D_IN = 3 * A_WIDTH + A_HEADS + 2 * B_KEY_WIDTH + 2 * B_VAL_WIDTH + GATE_RANK + 2 * D_MODEL

kernel_name = 'fox_gla_parallel_convffn_step'


def rmsnorm(x, g):
    xf = x.astype(jnp.float32)
    xf = xf * lax.rsqrt(jnp.mean(xf * xf, axis=-1, keepdims=True) + EPS)
    return xf.astype(x.dtype) * g


def project_mixer_inputs(hn, w_in, b_f, w_alpha2, b_alpha):
    b, t, _ = hn.shape
    u = hn @ w_in
    offs = np.cumsum(IN_SPLITS)[:-1].tolist()
    q_a, k_a, v_a, f_a, q_b, k_b, v_b, a_low, r_b, gate_a, gate_b = jnp.split(u, offs, axis=-1)
    logf = jax.nn.log_sigmoid((f_a + b_f).astype(jnp.float32))
    log_alpha = jax.nn.log_sigmoid((a_low @ w_alpha2 + b_alpha).astype(jnp.float32)) / GATE_TEMP
    return (q_a.reshape(b, t, A_HEADS, A_HEAD_DIM) * A_HEAD_DIM ** -0.5,
            k_a.reshape(b, t, A_HEADS, A_HEAD_DIM),
            v_a.reshape(b, t, A_HEADS, A_HEAD_DIM),
            logf,
            q_b.reshape(b, t, B_HEADS, B_KEY_DIM) * B_KEY_DIM ** -0.5,
            k_b.reshape(b, t, B_HEADS, B_KEY_DIM),
            v_b.reshape(b, t, B_HEADS, B_VAL_DIM),
            log_alpha.reshape(b, t, B_HEADS, B_KEY_DIM),
            r_b, gate_a, gate_b)


def fox_prompt(q, k, v, logf):
    b, t, h, d = q.shape
    n_blocks = t // Q_BLOCK
    c = jnp.cumsum(logf, axis=1).transpose(0, 2, 1)
    kpos = jnp.arange(t)

    def block(i):
        start = i * Q_BLOCK
        qb = lax.dynamic_slice_in_dim(q, start, Q_BLOCK, axis=1)
        cq = lax.dynamic_slice_in_dim(c, start, Q_BLOCK, axis=2)
        qpos = start + jnp.arange(Q_BLOCK)
        logits = (jnp.einsum('bqhd,bkhd->bhqk', qb, k, preferred_element_type=jnp.float32)
                  + cq[..., :, None] - c[..., None, :])
        logits = jnp.where(kpos[None, :] <= qpos[:, None], logits, -jnp.inf)
        p = jax.nn.softmax(logits, axis=-1).astype(v.dtype)
        return jnp.einsum('bhqk,bkhd->bqhd', p, v)

    out = lax.map(block, jnp.arange(n_blocks))
    return out.transpose(1, 0, 2, 3, 4).reshape(b, t, h * d)


def fox_sample(q, k, v, logf, k_past, v_past, logf_past):
    db, t, h, d = q.shape
    past_len = k_past.shape[1]
    c_new = jnp.cumsum(logf, axis=1).transpose(0, 2, 1)
    lp = logf_past.astype(jnp.float32)
    suffix = (lax.cumsum(lp, axis=1, reverse=True) - lp).transpose(0, 2, 1)
    s_past = (jnp.einsum('bqhd,bkhd->bhqk', q, k_past, preferred_element_type=jnp.float32)
              + c_new[..., :, None] + suffix[..., None, :])
    s_new = (jnp.einsum('bqhd,bkhd->bhqk', q, k, preferred_element_type=jnp.float32)
             + c_new[..., :, None] - c_new[..., None, :])
    causal = jnp.tril(jnp.ones((t, t), dtype=bool))
    s_new = jnp.where(causal, s_new, -jnp.inf)
    p = jax.nn.softmax(jnp.concatenate([s_past, s_new], axis=-1), axis=-1).astype(v.dtype)
    out = (jnp.einsum('bhqk,bkhd->bqhd', p[..., :past_len], v_past)
           + jnp.einsum('bhqk,bkhd->bqhd', p[..., past_len:], v))
    return out.reshape(db, t, h * d)


def gla_chunk(state, xs):
    q, k, v, log_alpha = xs
    qf, kf, vf = q.astype(jnp.float32), k.astype(jnp.float32), v.astype(jnp.float32)
    c = q.shape[2]
    bcum = jnp.cumsum(log_alpha.astype(jnp.float32), axis=2)
    o_inter = jnp.einsum('bhtk,bhkv->bhtv', qf * jnp.exp(bcum), state)
    causal = jnp.tril(jnp.ones((c, c), dtype=bool))
    diff = bcum[:, :, :, None, :] - bcum[:, :, None, :, :]
    decay = jnp.exp(jnp.where(causal[:, :, None], diff, -jnp.inf))
    attn = jnp.einsum('bhtk,bhsk,bhtsk->bhts', qf, kf, decay)
    o = o_inter + jnp.einsum('bhts,bhsv->bhtv', attn, vf)
    b_last = bcum[:, :, -1:, :]
    new_state = (jnp.exp(b_last[:, :, 0, :])[..., None] * state
                 + jnp.einsum('bhsk,bhsv->bhkv', kf * jnp.exp(b_last - bcum), vf))
    return new_state, o


def gla_prompt(q, k, v, log_alpha):
    b, t, h, dk = q.shape
    dv = v.shape[-1]
    n_chunks = t // GLA_CHUNK

    def chunks(a):
        return a.reshape(b, n_chunks, GLA_CHUNK, h, a.shape[-1]).transpose(1, 0, 3, 2, 4)

    s0 = jnp.zeros((b, h, dk, dv), jnp.float32)
    s_fin, o = lax.scan(gla_chunk, s0, (chunks(q), chunks(k), chunks(v), chunks(log_alpha)))
    return o.transpose(1, 0, 3, 2, 4).reshape(b, t, h, dv).astype(v.dtype), s_fin


def gla_sample(state, q, k, v, log_alpha):
    tr = lambda a: a.transpose(0, 2, 1, 3)
    s_new, o = gla_chunk(state.astype(jnp.float32), (tr(q), tr(k), tr(v), tr(log_alpha)))
    return tr(o).astype(v.dtype), s_new


def conv_ffn(hn, conv_buf, w_up, w_conv, b_conv, w_down):
    t = hn.shape[1]
    u = hn @ w_up
    full = jnp.concatenate([conv_buf.astype(u.dtype), u], axis=1)
    conv = b_conv + sum(full[:, i:i + t] * w_conv[i] for i in range(CONV_WIDTH))
    a, g = jnp.split(conv, 2, axis=-1)
    return (jax.nn.gelu(g, approximate=True) * a) @ w_down, full[:, t:]


def layer_forward(x, attn_fn, gla_fn, conv_buf, g_pre_mix, w_in, b_f, w_alpha2, b_alpha, g_gla,
                  w_proj_a, w_proj_b, w_out, g_post_mix, g_pre_ffn, w_up, w_conv, b_conv, w_down, g_post_ffn):
    b, t, _ = x.shape
    hn = rmsnorm(x, g_pre_mix)
    q_a, k_a, v_a, logf, q_b, k_b, v_b, log_alpha, r_b, gate_a, gate_b = project_mixer_inputs(
        hn, w_in, b_f, w_alpha2, b_alpha)
    o_a = attn_fn(q_a, k_a, v_a, logf)
    o_b, gla_state = gla_fn(q_b, k_b, v_b, log_alpha)
    o_b = rmsnorm(o_b, g_gla).reshape(b, t, B_VAL_WIDTH) * jax.nn.silu(r_b)
    merged = jax.nn.sigmoid(gate_a) * (o_a @ w_proj_a) + jax.nn.sigmoid(gate_b) * (o_b @ w_proj_b)
    x = x + rmsnorm(merged @ w_out, g_post_mix)
    f, new_conv = conv_ffn(rmsnorm(x, g_pre_ffn), conv_buf, w_up, w_conv, b_conv, w_down)
    y = x + rmsnorm(f, g_post_ffn)
    return y, k_a, v_a, logf, gla_state, new_conv


def setup_inputs(seed: int = 0) -> dict:
    key = jax.random.key(seed)
    ks = jax.random.split(key, 24)
    n_pages = PAST_LEN // PAGE_SIZE
    n_pool = (DEC_BATCH * n_pages * 5) // 4

    def nrm(k, shape, scale=1.0):
        return scale * jax.random.normal(k, shape, jnp.float32)

    page_table = jax.random.permutation(ks[7], n_pool)[: DEC_BATCH * n_pages].reshape(DEC_BATCH, n_pages).astype(jnp.int32)
    return {
        'x_prompt': nrm(ks[0], (BATCH, SEQ, D_MODEL)),
        'x_sample': nrm(ks[1], (DEC_BATCH, DEC_SEQ, D_MODEL)),
        'cache_k': nrm(ks[2], (DEPTH, n_pool, PAGE_SIZE, A_HEADS, A_HEAD_DIM)),
        'cache_v': nrm(ks[3], (DEPTH, n_pool, PAGE_SIZE, A_HEADS, A_HEAD_DIM)),
        'cache_logf': jax.nn.log_sigmoid(8.0 + nrm(ks[4], (DEPTH, n_pool, PAGE_SIZE, A_HEADS), 0.5)),
        'state_gla': nrm(ks[5], (DEPTH, DEC_BATCH, B_HEADS, B_KEY_DIM, B_VAL_DIM)),
        'state_conv': nrm(ks[6], (DEPTH, DEC_BATCH, CONV_WIDTH - 1, 2 * D_FF)),
        'page_table': page_table,
        'g_pre_mix': 1.0 + nrm(ks[8], (DEPTH, D_MODEL), 0.01),
        'w_in': nrm(ks[9], (DEPTH, D_MODEL, D_IN), D_MODEL ** -0.5),
        'b_f': jnp.linspace(3.0, 8.0, A_HEADS, dtype=jnp.float32) + nrm(ks[10], (DEPTH, A_HEADS), 0.1),
        'w_alpha2': nrm(ks[11], (DEPTH, GATE_RANK, B_KEY_WIDTH), GATE_RANK ** -0.5),
        'b_alpha': nrm(ks[12], (DEPTH, B_KEY_WIDTH), 0.1),
        'g_gla': 1.0 + nrm(ks[13], (DEPTH, B_VAL_DIM), 0.01),
        'w_proj_a': nrm(ks[14], (DEPTH, A_WIDTH, D_MODEL), A_WIDTH ** -0.5),
        'w_proj_b': nrm(ks[15], (DEPTH, B_VAL_WIDTH, D_MODEL), B_VAL_WIDTH ** -0.5),
        'w_out': nrm(ks[16], (DEPTH, D_MODEL, D_MODEL), D_MODEL ** -0.5),
        'g_post_mix': 1.0 + nrm(ks[17], (DEPTH, D_MODEL), 0.01),
        'g_pre_ffn': 1.0 + nrm(ks[18], (DEPTH, D_MODEL), 0.01),
        'w_up': nrm(ks[19], (DEPTH, D_MODEL, 2 * D_FF), D_MODEL ** -0.5),
        'w_conv': nrm(ks[20], (DEPTH, CONV_WIDTH, 2 * D_FF), CONV_WIDTH ** -0.5),
        'b_conv': nrm(ks[21], (DEPTH, 2 * D_FF), 0.01),
        'w_down': nrm(ks[22], (DEPTH, D_FF, D_MODEL), D_FF ** -0.5),
        'g_post_ffn': 1.0 + nrm(ks[23], (DEPTH, D_MODEL), 0.01),
    }


def reference(x_prompt, x_sample, cache_k, cache_v, cache_logf, state_gla, state_conv, page_table,
              g_pre_mix, w_in, b_f, w_alpha2, b_alpha, g_gla, w_proj_a, w_proj_b, w_out, g_post_mix,
              g_pre_ffn, w_up, w_conv, b_conv, w_down, g_post_ffn):
    dec_batch, n_pages = page_table.shape
    past_len = n_pages * PAGE_SIZE
    conv0 = jnp.zeros((x_prompt.shape[0], CONV_WIDTH - 1, 2 * D_FF), x_prompt.dtype)
    yp, ys = x_prompt, x_sample
    kp, vp, fp, gp, cp = [], [], [], [], []
    ksm, vsm, fsm, gsm, csm = [], [], [], [], []
    for l in range(DEPTH):
        w = (g_pre_mix[l], w_in[l], b_f[l], w_alpha2[l], b_alpha[l], g_gla[l], w_proj_a[l], w_proj_b[l],
             w_out[l], g_post_mix[l], g_pre_ffn[l], w_up[l], w_conv[l], b_conv[l], w_down[l], g_post_ffn[l])
        yp, k_a, v_a, logf, st, cb = layer_forward(yp, fox_prompt, gla_prompt, conv0, *w)
        kp.append(k_a); vp.append(v_a); fp.append(logf); gp.append(st); cp.append(cb)
        k_past = cache_k[l][page_table].reshape(dec_batch, past_len, A_HEADS, A_HEAD_DIM)
        v_past = cache_v[l][page_table].reshape(dec_batch, past_len, A_HEADS, A_HEAD_DIM)
        f_past = cache_logf[l][page_table].reshape(dec_batch, past_len, A_HEADS)
        attn_s = functools.partial(fox_sample, k_past=k_past, v_past=v_past, logf_past=f_past)
        gla_s = functools.partial(gla_sample, state_gla[l])
        ys, k_a, v_a, logf, st, cb = layer_forward(ys, attn_s, gla_s, state_conv[l], *w)
        ksm.append(k_a); vsm.append(v_a); fsm.append(logf); gsm.append(st); csm.append(cb)
    new_k_prompt = jnp.stack(kp)
    new_v_prompt = jnp.stack(vp)
    new_logf_prompt = jnp.stack(fp)
    new_gla_prompt = jnp.stack(gp)
    new_conv_prompt = jnp.stack(cp)
    new_k_sample = jnp.stack(ksm)
    new_v_sample = jnp.stack(vsm)
    new_logf_sample = jnp.stack(fsm)
    new_gla_sample = jnp.stack(gsm)
    new_conv_sample = jnp.stack(csm)
    return (yp, ys, new_k_prompt, new_v_prompt, new_logf_prompt, new_gla_prompt, new_conv_prompt,
            new_k_sample, new_v_sample, new_logf_sample, new_gla_sample, new_conv_sample)
```

```python
import contextlib
import os
import numpy as np
import concourse.bass as bass
import concourse.mybir as mybir
from concourse.bass_utils import run_bass_kernel_spmd

F32 = mybir.dt.float32
BF16 = mybir.dt.bfloat16
I32 = mybir.dt.int32
AF = mybir.ActivationFunctionType
ALU = mybir.AluOpType
AX = mybir.AxisListType

D = 1024
AH, AD = 8, 64
AW = 512
BH, BK, BV = 4, 128, 256
GR = 16
DFF = 2816
NCH = DFF // 128
EPS = 1e-6
DIN = 6680
O_QA, O_KA, O_VA, O_F = 0, 512, 1024, 1536
O_QB, O_KB, O_VB, O_AL, O_RB, O_GA, O_GB = 1544, 2056, 2568, 3592, 3608, 4632, 5656
NEG = -30000.0
EPOCH = 30000
ND = 8


class Sched:
    def __init__(self, nc, es):
        self.nc, self.es = nc, es
        self.E = {'pe': nc.tensor, 'act': nc.scalar, 'dve': nc.vector, 'pool': nc.gpsimd, 'sp': nc.sync}
        self.cnt = {e: 0 for e in self.E}
        self.sems = {e: [] for e in self.E}
        self.waited = {}
        self.dsem = {q: [es.enter_context(nc.semaphore(f"d{q}{i}")) for i in range(ND)] for q in ('sp', 'pool', 'act')}
        self.dcnt = {q: 0 for q in self.dsem}
        self.dwaited = {}
        self.lastw = {}
        self.readers = {}
        self.nwaits = 0

    def _sem(self, e, n):
        ep = (n - 1) // EPOCH
        while len(self.sems[e]) <= ep:
            self.sems[e].append(self.es.enter_context(self.nc.semaphore(f"s{e}{len(self.sems[e])}")))
        return self.sems[e][ep], n - ep * EPOCH

    def need(self, E, ev):
        if ev[0] == 'e':
            _, Fe, n = ev
            if Fe == E and E == 'pe':
                return
            if self.waited.get((E, Fe), 0) >= n:
                return
            sem, val = self._sem(Fe, n)
            self.E[E].wait_ge(sem, val)
            self.waited[(E, Fe)] = n
            self.nwaits += 1
        else:
            _, q, k = ev
            slot, val = k % ND, 16 * (k // ND + 1)
            if self.dwaited.get((E, q, slot), 0) >= val:
                return
            self.E[E].wait_ge(self.dsem[q][slot], val)
            self.dwaited[(E, q, slot)] = val
            self.nwaits += 1

    def _deps(self, reads, writes):
        deps = []
        for r in reads:
            if r in self.lastw:
                deps.append(self.lastw[r])
            if isinstance(r, tuple) and r[0] == 'ps':
                deps.extend(self.readers.get(r, ()))
        for w in writes:
            if w in self.lastw:
                deps.append(self.lastw[w])
            deps.extend(self.readers.get(w, ()))
        return deps

    def _record(self, ev, reads, writes):
        for r in reads:
            lst = self.readers.setdefault(r, [])
            if ev[0] == 'e':
                lst[:] = [x for x in lst if not (x[0] == 'e' and x[1] == ev[1])]
            lst.append(ev)
        for w in writes:
            self.lastw[w] = ev
            self.readers[w] = []

    def op(self, E, fn, reads=(), writes=()):
        for d in self._deps(reads, writes):
            self.need(E, d)
        ins = fn()
        self.cnt[E] += 1
        sem, _ = self._sem(E, self.cnt[E])
        ins.then_inc(sem, 1)
        self._record(('e', E, self.cnt[E]), reads, writes)

    def dma(self, q, fn, reads=(), writes=()):
        for d in self._deps(reads, writes):
            self.need(q, d)
        k = self.dcnt[q]
        if k >= ND:
            self.need(q, ('d', q, k - ND))
        ins = fn()
        ins.then_inc(self.dsem[q][k % ND], 16)
        self.dcnt[q] = k + 1
        ev = ('d', q, k)
        self._record(ev, reads, writes)
        return ev

    def finish(self):
        for e in self.E:
            if self.cnt[e]:
                self.need('sp', ('e', e, self.cnt[e]))
        for q in self.dcnt:
            for k in range(max(0, self.dcnt[q] - ND), self.dcnt[q]):
                self.need('sp', ('d', q, k))


class Ring:
    def __init__(self, tiles, name):
        self.tiles, self.name, self.i = tiles, name, 0

    def next(self):
        t = self.tiles[self.i % len(self.tiles)]
        k = (self.name, self.i % len(self.tiles))
        self.i += 1
        return t, k


def host_consts(QG):
    tri = np.triu(np.ones((128, 128), np.float32))
    c = {
        'c_ident': np.eye(128, dtype=np.float32),
        'c_tri': tri,
        'c_trin': (-tri / 16.0).astype(np.float32),
        'c_ones': np.ones((128, 128), np.float32),
    }
    stair = np.zeros((QG, 128, QG * 128), np.float32)
    for d in range(QG):
        for qb in range(QG):
            if qb == d:
                stair[d][:, qb * 128:(qb + 1) * 128] = tri
            elif qb > d:
                stair[d][:, qb * 128:(qb + 1) * 128] = 1.0
    c['c_stair'] = stair.reshape(QG * 128, QG * 128)
    tri_s = np.zeros((32, 32), np.float32)
    seqind = np.zeros((32, 4), np.float32)
    for s in range(4):
        tri_s[s * 8:(s + 1) * 8, s * 8:(s + 1) * 8] = np.triu(np.ones((8, 8), np.float32))
        seqind[s * 8:(s + 1) * 8, s] = 1.0
    c['c_tris'] = tri_s
    c['c_trins'] = (-tri_s / 16.0).astype(np.float32)
    c['c_seqind'] = seqind
    c['c_seqindn'] = (-seqind / 16.0).astype(np.float32)
    colm = np.zeros((4, 128, 32), np.float32)
    for s in range(4):
        colm[s][:, s * 8:(s + 1) * 8] = 1.0
    c['c_colmask'] = colm.reshape(4 * 128, 32)
    c['c_iota'] = np.arange(128, dtype=np.int32).reshape(128, 1)
    c['c_trigt'] = np.tril(np.ones((128, 128), np.float32), -1)
    mn = np.zeros((32, 4, 8), np.float32)
    for k in range(32):
        for q in range(8):
            if (k % 8) <= q:
                mn[k, k // 8, q] = 1.0
    c['c_masknew'] = mn.reshape(32, 32)
    return c


class Builder:
    def __init__(self, NB, QG, NPG, NPOOL):
        self.NB, self.QG, self.NPG, self.NPOOL = NB, QG, NPG, NPOOL
        self.NQ = NB // 4
        self.OWN0 = NB - self.NQ - 1
        self.NOWN = self.NQ + 1
        self.TT = NB * 128
        self.NSQ = 4
        self.es = contextlib.ExitStack()
        self.nc = bass.Bass("TRN2", target_bir_lowering=False)
        self.S = Sched(self.nc, self.es)
        self.out_events = []
        self.cur = self.es

    def din(self, name, shape, dt=F32):
        return self.nc.dram_tensor(name, list(shape), dt, kind="ExternalInput").ap()

    def dout(self, name, shape, dt=F32):
        return self.nc.dram_tensor(name, list(shape), dt, kind="ExternalOutput").ap()

    def dscr(self, name, shape, dt):
        return self.nc.dram_tensor(name, list(shape), dt, kind="Internal").ap()

    def sb(self, name, shape, dt=F32):
        return self.cur.enter_context(self.nc.sbuf_tensor(name, list(shape), dt))

    def ring(self, name, n, shape, dt=F32):
        return Ring([self.sb(f"{name}{i}", shape, dt) for i in range(n)], name)

    def mm(self, out, lhsT, rhs, start, stop, reads, writes):
        nc = self.nc
        self.S.op('pe', lambda: nc.tensor.matmul(out, lhsT=lhsT, rhs=rhs, start=start, stop=stop), reads, writes)

    def tr(self, out, in_, ident, reads, writes):
        nc = self.nc
        self.S.op('pe', lambda: nc.tensor.transpose(out, in_, ident), reads, writes)

    def act(self, out, in_, func, reads, writes, bias=None, scale=None, accum=None):
        nc = self.nc
        kw = {}
        if bias is not None:
            kw['bias'] = bias
        if scale is not None:
            kw['scale'] = scale
        if accum is not None:
            kw['accum_out'] = accum
        self.S.op('act', lambda: nc.scalar.activation(out=out, in_=in_, func=func, **kw), reads, writes)

    def tt(self, eng, out, in0, in1, op, reads, writes):
        e = self.S.E[eng]
        self.S.op(eng, lambda: e.tensor_tensor(out=out, in0=in0, in1=in1, op=op), reads, writes)

    def ts(self, eng, out, in0, s1, s2, op0, op1, reads, writes):
        e = self.S.E[eng]
        if op1 is None:
            self.S.op(eng, lambda: e.tensor_scalar(out=out, in0=in0, scalar1=s1, scalar2=None, op0=op0), reads, writes)
        else:
            self.S.op(eng, lambda: e.tensor_scalar(out=out, in0=in0, scalar1=s1, scalar2=s2, op0=op0, op1=op1), reads, writes)

    def stt(self, eng, out, in0, scalar, in1, op0, op1, reads, writes):
        e = self.S.E[eng]
        self.S.op(eng, lambda: e.scalar_tensor_tensor(out=out, in0=in0, scalar=scalar, in1=in1, op0=op0, op1=op1), reads, writes)

    def cp(self, eng, out, in_, reads, writes):
        if eng == 'act':
            nc = self.nc
            self.S.op('act', lambda: nc.scalar.copy(out=out, in_=in_), reads, writes)
        else:
            e = self.S.E[eng]
            self.S.op(eng, lambda: e.tensor_copy(out=out, in_=in_), reads, writes)

    def dma(self, q, out, in_, reads, writes, slow=False):
        e = self.S.E[q]
        if slow:
            return self.S.dma(q, lambda: e.dma_start(out=out, in_=in_, allow_slow_non_contiguous=True), reads, writes)
        return self.S.dma(q, lambda: e.dma_start(out=out, in_=in_), reads, writes)

    def rsqrt_col(self, out, in_, key):
        nc = self.nc
        self.S.op('act', lambda: nc.scalar.sqrt(out=out, in_=in_), [key], [key])
        self.S.op('dve', lambda: nc.vector.reciprocal(out=out, in_=out), [key], [key])

    def psn(self):
        t, k = self.psring.next()
        return t, k

    @contextlib.contextmanager
    def phase(self):
        prev = self.cur
        with contextlib.ExitStack() as st:
            self.cur = st
            yield
            self.barrier()
        self.cur = prev

    def barrier(self):
        S = self.S
        for E in S.E:
            for Fe in S.E:
                if Fe != E and S.cnt[Fe]:
                    S.need(E, ('e', Fe, S.cnt[Fe]))
            for q in S.dcnt:
                for k in range(max(0, S.dcnt[q] - ND), S.dcnt[q]):
                    S.need(E, ('d', q, k))
        S.lastw.clear()
        S.readers.clear()

    def build(self):
        nc, S = self.nc, self.S
        NB, QG, NPG, NPOOL, TT, NSQ = self.NB, self.QG, self.NPG, self.NPOOL, self.TT, self.NSQ
        NOWN, OWN0, NQ = self.NOWN, self.OWN0, self.NQ
        TO = NOWN * 128
        self.cur = self.es
        STOP = int(os.environ.get("MK_STOP", "99"))

        xc = self.din("xc", [TT, D])
        kmask_d = self.din("kmask", [128, NB])
        xs = self.din("xs", [32, D])
        pt_d = self.din("pt", [1, NSQ * NPG], I32)
        cache_k = self.din("cache_k", [NPOOL * 128, AW])
        cache_v = self.din("cache_v", [NPOOL * 128, AW])
        cache_lf = self.din("cache_lf", [NPOOL * 128, AH])
        sgla_d = self.din("sgla", [NSQ * BH * 128, BV])
        sconv_d = self.din("sconv", [NSQ * 2, 2 * DFF])
        w_in = self.din("w_in", [D, DIN])
        w_a2 = self.din("w_alpha2", [GR, AW])
        b_al = self.din("b_alpha", [1, AW])
        b_f = self.din("b_f", [1, AH])
        g_pre_mix = self.din("g_pre_mix", [1, D])
        g_gla = self.din("g_gla", [1, BV])
        w_pa = self.din("w_proj_a", [AW, D])
        w_pb = self.din("w_proj_b", [D, D])
        w_out = self.din("w_out", [D, D])
        g_post_mix = self.din("g_post_mix", [1, D])
        g_pre_ffn = self.din("g_pre_ffn", [1, D])
        w_up = self.din("w_up", [D, 2 * DFF])
        w_conv = self.din("w_conv", [3, 2 * DFF])
        b_conv = self.din("b_conv", [1, 2 * DFF])
        w_down = self.din("w_down", [DFF, D])
        g_post_ffn = self.din("g_post_ffn", [1, D])
        cst = {k: self.din(k, v.shape, I32 if v.dtype == np.int32 else F32) for k, v in host_consts(QG).items()}

        o_y = self.dout("o_y", [NQ * 128, D])
        o_k = self.dout("o_k", [NQ * 128, AW])
        o_v = self.dout("o_v", [NQ * 128, AW])
        o_lf = self.dout("o_lf", [NQ * 128, AH])
        o_gla = self.dout("o_gla", [BH * 128, BV])
        o_conv = self.dout("o_conv", [2, 2 * DFF])
        o_ys = self.dout("o_ys", [32, D])
        o_ks = self.dout("o_ks", [32, AW])
        o_vs = self.dout("o_vs", [32, AW])
        o_lfs = self.dout("o_lfs", [32, AH])
        o_glas = self.dout("o_glas", [NSQ * BH * 128, BV])
        o_convs = self.dout("o_convs", [NSQ * 2, 2 * DFF])

        kt_scr = self.dscr("kt_scr", [128, 4, TT], BF16)
        vx_scr = self.dscr("vx_scr", [NB, 128, AH * 65], BF16)
        qt_scr = self.dscr("qt_scr", [128, 4, TO], BF16)
        obt_scr = self.dscr("obt_scr", [128, 8, TO], BF16)
        oat_scr = self.dscr("oat_scr", [128, 4, TO], BF16)
        xm_scr = self.dscr("xm_scr", [TO + 32, D], F32)
        wup_scr = self.dscr("wup_scr", [128, 8, 2 * DFF], BF16)

        ident = self.sb("ident", [128, 128], BF16)
        identf = self.sb("identf", [128, 128], F32)
        tri_f = self.sb("tri_f", [128, 128], F32)
        trigt_f = self.sb("trigt_f", [128, 128], F32)
        tri_b = self.sb("tri_b", [128, 128], BF16)
        trin_b = self.sb("trin_b", [128, 128], BF16)
        ones_f = self.sb("ones_f", [128, 128], F32)
        onesn_b = self.sb("onesn_b", [128, 1], BF16)
        stair = self.sb("stair", [128, QG, QG * 128], BF16)
        kmask = self.sb("kmask_t", [128, NB], F32)
        tris_f = self.sb("tris_f", [32, 32], F32)
        tris_b = self.sb("tris_b", [32, 32], BF16)
        trins_b = self.sb("trins_b", [32, 32], BF16)
        seqind_f = self.sb("seqind_f", [32, 4], F32)
        seqindn_b = self.sb("seqindn_b", [32, 4], BF16)
        colmask = self.sb("colmask", [128, 4, 32], BF16)
        masknew = self.sb("masknew", [32, 4, 8], BF16)
        iota_i = self.sb("iota_i", [128, 1], I32)
        Ccum = self.sb("Ccum", [128, NB, AH], F32)
        carry = self.sb("carry", [128, NB + 1, AH], F32)
        QTs = self.sb("QTs", [128, 4, 32], BF16)
        KTs = self.sb("KTs", [128, 4, 32], BF16)
        VXs = self.sb("VXs", [32, AH, 65], BF16)
        Csn = self.sb("Csn", [32, AH], F32)
        obTs = self.sb("obTs", [128, 8, 32], BF16)
        oaTs = self.sb("oaTs", [128, 4, 32], BF16)

        K0 = ('const',)
        cl = [
            ('pool', ident[:], cst['c_ident'][:, :]), ('sp', identf[:], cst['c_ident'][:, :]),
            ('sp', tri_f[:], cst['c_tri'][:, :]), ('pool', tri_b[:], cst['c_tri'][:, :]),
            ('sp', trigt_f[:], cst['c_trigt'][:, :]),
            ('pool', trin_b[:], cst['c_trin'][:, :]), ('sp', ones_f[:], cst['c_ones'][:, :]),
            ('pool', onesn_b[:], cst['c_trin'][0:128, 127:128]),
            ('sp', kmask[:], kmask_d[:, :]),
            ('sp', tris_f[:], cst['c_tris'][:, :]),
            ('pool', tris_b[:], cst['c_tris'][:, :]), ('pool', trins_b[:], cst['c_trins'][:, :]),
            ('sp', seqind_f[:], cst['c_seqind'][:, :]), ('pool', seqindn_b[:], cst['c_seqindn'][:, :]),
            ('sp', iota_i[:], cst['c_iota'][:, :]),
            ('pool', masknew[:], cst['c_masknew'][:, :].rearrange("k (s q) -> k s q", s=4)),
        ]
        for q, o, i in cl:
            self.dma(q, o, i, [], [K0], slow=True)
        for d in range(QG):
            self.dma('pool', stair[:, d, :], cst['c_stair'][d * 128:(d + 1) * 128, :], [], [K0])
        for s in range(4):
            self.dma('pool', colmask[:, s, :], cst['c_colmask'][s * 128:(s + 1) * 128, :], [], [K0])
        S.op('pool', lambda: nc.gpsimd.memset(carry[:, 0, :], 0.0), [], [('carry', 0)])
        S.op('pool', lambda: nc.gpsimd.memset(VXs[:], 1.0), [], [('VXs',)])
        for k in range(8):
            self.dma('pool', wup_scr[:, k, :], w_up[k * 128:(k + 1) * 128, :], [], [('wup_scr',)])

        banks = [self.es.enter_context(nc.psum_tensor(f"ps{i}", [128, 512], F32)) for i in range(8)]
        self.psring = Ring(banks, "ps")

        def bview(ps):
            return ps[:].bitcast(BF16)

        def gload(name, g_ap, cols=D):
            t = self.sb(name, [128, cols], F32)
            self.dma('sp', t[:], g_ap.partition_broadcast(128), [], [K0], slow=True)
            return t

        def wload(name, src, r0, rows, c0, cols):
            kc = rows // 128
            t = self.sb(name, [128, kc, cols], BF16)
            v = src[r0:r0 + rows, c0:c0 + cols].rearrange("(k p) c -> p k c", p=128)
            for k in range(kc):
                self.dma('pool', t[:, k, :], v[:, k, :], [], [K0])
            return t

        def norm_T(xt, xk, P, gt, dst, dkey, R):
            junk, jk = R['junk'].next()
            ss, sk = R['ss'].next()
            self.act(junk[0:P, :], xt, AF.Square, [xk], [jk, sk], accum=ss[0:P, 0:1])
            self.ts('dve', ss[0:P, 1:2], ss[0:P, 0:1], 1.0 / D, EPS, ALU.mult, ALU.add, [sk], [sk])
            self.rsqrt_col(ss[0:P, 2:3], ss[0:P, 1:2], sk)
            hn, hk = R['hn'].next()
            self.stt('dve', hn[0:P, :], xt, ss[0:P, 2:3], gt[0:P, :], ALU.mult, ALU.mult, [xk, sk, K0], [hk])
            ps, pk = self.psn()
            pv = bview(ps)
            for kc in range(8):
                self.tr(pv[:, kc * P:(kc + 1) * P], hn[0:P, kc * 128:(kc + 1) * 128], ident[0:P, 0:P], [hk, K0], [pk])
            self.cp('act', dst, pv[:, 0:8 * P].rearrange("p (k t) -> p k t", k=8), [pk], [dkey])

        def proj_tok(hnT, tk, P, W, c0, cols, ps, pk, pcol=0, nk=8):
            for kc in range(nk):
                self.mm(ps[0:P, pcol:pcol + cols], hnT[:, kc, 0:P], W[:, kc, c0:c0 + cols], kc == 0, kc == nk - 1, [tk, K0], [pk])

        def proj_feat(hnT, tk, P, W, c0, M, ps, pk, pcol=0):
            for kc in range(8):
                self.mm(ps[0:M, pcol:pcol + P], W[:, kc, c0:c0 + M], hnT[:, kc, 0:P], kc == 0, kc == 7, [tk, K0], [pk])

        def softplus_neg(dst, dk, src, sk_, tmp, tmk):
            self.act(tmp, src, AF.Exp, [sk_], [tmk], scale=-1.0)
            self.act(dst, tmp, AF.Ln, [tmk], [dk], bias=1.0)

        def transpose_into(dst, dkey, src, sk_, P, nfc):
            for g0 in range(0, nfc, 8):
                g = min(8, nfc - g0)
                ps, pk = self.psn()
                pv = bview(ps)
                for i in range(g):
                    fc = g0 + i
                    self.tr(pv[:, i * P:(i + 1) * P], src[0:P, fc * 128:(fc + 1) * 128], ident[0:P, 0:P], [sk_, K0], [pk])
                self.cp('act', dst[:, g0:g0 + g, :], pv[:, 0:g * P].rearrange("p (k t) -> p k t", k=g), [pk], [dkey])

        with self.phase():
            gpre = gload("gpre", g_pre_mix)
            ggl = gload("ggl", g_gla, BV)
            bft = gload("bft", b_f, AH)
            balt = gload("balt", b_al, AW)
            W_kv = wload("W_kv", w_in, 0, D, O_KA, 1024)
            W_f = wload("W_f", w_in, 0, D, O_F, 8)
            W_kb = wload("W_kb", w_in, 0, D, O_KB, 512)
            W_vb = wload("W_vb", w_in, 0, D, O_VB, 1024)
            W_al = wload("W_al", w_in, 0, D, O_AL, 16)
            W_qa = wload("W_qa", w_in, 0, D, O_QA, 512)
            W_qb = wload("W_qb", w_in, 0, D, O_QB, 512)
            W_rb = wload("W_rb", w_in, 0, D, O_RB, 1024)
            W_a2 = self.sb("W_a2", [GR, AW], BF16)
            self.dma('pool', W_a2[:], w_a2[:, :], [], [K0])
            Sst = [self.sb(f"Sst{h}", [128, BV], F32) for h in range(BH)]
            Sbf = [self.sb(f"Sbf{h}", [128, BV], BF16) for h in range(BH)]
            for h in range(BH):
                S.op('pool', lambda h=h: nc.gpsimd.memset(Sst[h][:], 0.0), [], [('S', h)])
            R = {
                'x': self.ring("xt", 2, [128, D], F32), 'junk': self.ring("junk", 1, [128, D], BF16),
                'ss': self.ring("ss", 4, [128, 4], F32), 'hn': self.ring("hn", 2, [128, D], BF16),
                'hnT': self.ring("hnT", 2, [128, 8, 128], BF16), 'ktb': self.ring("ktb", 2, [128, 4, 128], BF16),
                'vx': self.ring("vx", 2, [128, AH, 65], BF16), 'o512': self.ring("o512", 2, [128, 512], F32),
                'sm': self.ring("sm", 4, [128, 8], F32), 'alT': self.ring("alT", 2, [GR, 128], BF16),
                'a2s': self.ring("a2s", 2, [128, 512], F32), 'lsp': self.ring("lsp", 2, [128, 512], BF16),
                'ekd': self.ring("ekd", 2, [128, 512], F32), 'kd': self.ring("kd", 2, [128, 512], BF16),
                'vb': self.ring("vb", 2, [128, 1024], BF16), 'dl': self.ring("dl", 2, [128, 16], F32),
                'u2': self.ring("u2", 2, [128, BV], F32), 'qd': self.ring("qd", 2, [128, 512], BF16),
                'xT': self.ring("xT", 2, [128, 4, 128], BF16), 'AT': self.ring("AT", 2, [128, 4, 128], BF16),
                'f1k': self.ring("f1k", 2, [128, 1024], F32), 'b1k': self.ring("b1k", 2, [128, 1024], BF16),
                'qtb': self.ring("qtb", 2, [128, 4, 128], BF16), 'obTb': self.ring("obTb", 2, [128, 8, 128], BF16),
            }
            for t in R['vx'].tiles:
                S.op('pool', lambda t=t: nc.gpsimd.memset(t[:], 1.0), [], [('vx', R['vx'].tiles.index(t))])

            def mixer_block(hnT, tk, P, n, own, sample=False, outs=None, qa_sink=None):
                res = {}
                ps, pk = self.psn()
                for fc in range(4):
                    proj_feat(hnT, tk, P, W_kv, fc * 128, 128, ps, pk, pcol=fc * P)
                if sample:
                    self.cp('act', KTs[:, :, :], ps[:, 0:4 * P].rearrange("p (f t) -> p f t", f=4), [pk], [('KTs',)])
                else:
                    ktb, kk = R['ktb'].next()
                    self.cp('act', ktb[:], ps[:, :].rearrange("p (f t) -> p f t", f=4), [pk], [kk])
                    self.dma('sp', kt_scr[:, :, n * 128:(n + 1) * 128], ktb[:], [kk], [('kt_scr',)])
                ps, pk = self.psn()
                proj_tok(hnT, tk, P, W_kv, 512, 512, ps, pk)
                if sample:
                    self.cp('dve', VXs[0:P, :, 0:64], ps[0:P, :].rearrange("p (h d) -> p h d", h=AH), [pk], [('VXs',)])
                else:
                    vx, vk = R['vx'].next()
                    self.cp('dve', vx[0:P, :, 0:64], ps[0:P, :].rearrange("p (h d) -> p h d", h=AH), [pk], [vk])
                    self.dma('sp', vx_scr[n, :, :], vx[:].rearrange("p h d -> p (h d)"), [vk], [('vx_scr',)])
                if outs is not None:
                    o5, ok_ = R['o512'].next()
                    self.cp('act', o5[0:P, :], ps[0:P, :], [pk], [ok_])
                    self.out_events.append(self.dma('sp', outs['v'], o5[0:P, :], [ok_], []))
                    ps2, pk2 = self.psn()
                    proj_tok(hnT, tk, P, W_kv, 0, 512, ps2, pk2)
                    o5, ok_ = R['o512'].next()
                    self.cp('act', o5[0:P, :], ps2[0:P, :], [pk2], [ok_])
                    self.out_events.append(self.dma('sp', outs['k'], o5[0:P, :], [ok_], []))
                ps, pk = self.psn()
                proj_tok(hnT, tk, P, W_f, 0, 8, ps, pk)
                sm, smk = R['sm'].next()
                self.tt('dve', sm[0:P, :], ps[0:P, 0:8], bft[0:P, :], ALU.add, [pk, K0], [smk])
                sm2, smk2 = R['sm'].next()
                sp_t, spk = R['sm'].next()
                softplus_neg(sp_t[0:P, :], spk, sm[0:P, :], smk, sm2[0:P, :], smk2)
                if outs is not None:
                    lf, lfk = R['sm'].next()
                    self.ts('dve', lf[0:P, :], sp_t[0:P, :], -1.0, None, ALU.mult, None, [spk], [lfk])
                    self.out_events.append(self.dma('sp', outs['lf'], lf[0:P, :], [lfk], []))
                ps, pk = self.psn()
                proj_feat(hnT, tk, P, W_al, 0, GR, ps, pk)
                alT, ak = R['alT'].next()
                self.cp('act', alT[:, 0:P], ps[0:GR, 0:P], [pk], [ak])
                ps, pk = self.psn()
                self.mm(ps[0:P, :], alT[:, 0:P], W_a2[:, :], True, True, [ak, K0], [pk])
                a2s, a2k = R['a2s'].next()
                self.tt('dve', a2s[0:P, :], ps[0:P, :], balt[0:P, :], ALU.add, [pk, K0], [a2k])
                ekd, ek = R['ekd'].next()
                lsp, lk = R['lsp'].next()
                softplus_neg(lsp[0:P, :], lk, a2s[0:P, :], a2k, ekd[0:P, :], ek)
                vb, vbk = R['vb'].next()
                for half in range(2):
                    ps, pk = self.psn()
                    proj_tok(hnT, tk, P, W_vb, half * 512, 512, ps, pk)
                    self.cp('act' if half else 'dve', vb[0:P, half * 512:(half + 1) * 512], ps[0:P, :], [pk], [vbk])
                if own:
                    sr, srk = R['f1k'].next()
                    for half in range(2):
                        ps, pk = self.psn()
                        proj_tok(hnT, tk, P, W_rb, half * 512, 512, ps, pk)
                        self.act(sr[0:P, half * 512:(half + 1) * 512], ps[0:P, :], AF.Silu, [pk], [srk])
                    res['sr'] = (sr, srk)
                    ps, pk = self.psn()
                    for fc in range(4):
                        proj_feat(hnT, tk, P, W_qa, fc * 128, 128, ps, pk, pcol=fc * P)
                    qa_sink(ps, pk)
                ps, pk = self.psn()
                if sample:
                    self.mm(ps[0:P, 0:8], tris_f[0:P, 0:P], sp_t[0:P, :], True, True, [spk, K0], [pk])
                    self.cp('dve', Csn[:, :], ps[0:P, 0:8], [pk], [('Csn',)])
                else:
                    self.mm(ps[:, 0:8], tri_f[:], sp_t[:, :], True, True, [spk, K0], [pk])
                    self.mm(ps[:, 8:16], ones_f[:], sp_t[:, :], True, True, [spk, K0], [pk])
                    self.tt('dve', Ccum[:, n, :], ps[:, 0:8], carry[:, n, :], ALU.add, [pk, ('carry', n)], [('Ccum', n)])
                    self.tt('dve', carry[:, n + 1, :], ps[:, 8:16], carry[:, n, :], ALU.add, [pk, ('carry', n)], [('carry', n + 1)])
                psb, pbk = self.psn()
                tn = trins_b if sample else trin_b
                self.mm(psb[0:P, :], tn[0:P, 0:P], lsp[0:P, :], True, True, [lk, K0], [pbk])
                ekd, ek = R['ekd'].next()
                self.act(ekd[0:P, :], psb[0:P, :], AF.Exp, [pbk], [ek], scale=-1.0)
                ps, pk = self.psn()
                proj_tok(hnT, tk, P, W_kb, 0, 512, ps, pk)
                kd, kdk = R['kd'].next()
                self.tt('dve', kd[0:P, :], ps[0:P, :], ekd[0:P, :], ALU.mult, [pk, ek], [kdk])
                res.update(lsp=(lsp, lk), kd=(kd, kdk), vb=(vb, vbk))
                if own:
                    eq, eqk = R['a2s'].next()
                    self.act(eq[0:P, :], psb[0:P, :], AF.Exp, [pbk], [eqk])
                    ps, pk = self.psn()
                    proj_tok(hnT, tk, P, W_qb, 0, 512, ps, pk)
                    qd, qk = R['qd'].next()
                    self.stt('dve', qd[0:P, :], ps[0:P, :], BK ** -0.5, eq[0:P, :], ALU.mult, ALU.mult, [pk, eqk], [qk])
                    tiles = []
                    for src, sk_ in ((qd, qk), (kd, kdk)):
                        xT, xk_ = R['xT'].next()
                        transpose_into(xT[:, :, 0:P], xk_, src, sk_, P, 4)
                        tiles.append((xT, xk_))
                    (qdT, qtk), (kdT, ktk) = tiles
                    ps, pk = self.psn()
                    for h in range(BH):
                        self.mm(ps[0:P, h * P:(h + 1) * P], kdT[:, h, 0:P], qdT[:, h, 0:P], True, True, [ktk, qtk], [pk])
                    AT, atk = R['AT'].next()
                    msk = tris_b if sample else tri_b
                    self.tt('dve', AT[0:P, :, 0:P], ps[0:P, 0:4 * P].rearrange("p (h t) -> p h t", h=4),
                            msk[0:P, 0:P].unsqueeze(1).to_broadcast([P, 4, P]), ALU.mult, [pk, K0], [atk])
                    res.update(qdT=(qdT, qtk), AT=(AT, atk))
                return res

            def gla_out(res, P, S_list):
                AT, atk = res['AT']
                vb, vbk = res['vb']
                sr, srk = res['sr']
                ob, obk = R['f1k'].next()
                ss, sk = R['ss'].next()
                junk, jk = R['junk'].next()
                pss = []
                for h in range(BH):
                    if h % 2 == 0:
                        ps, pk = self.psn()
                        pss.append((ps, pk))
                    c0 = (h % 2) * BV
                    self.mm(ps[0:P, c0:c0 + BV], AT[0:P, h, 0:P], vb[0:P, h * BV:(h + 1) * BV], True, False, [atk, vbk], [pk])
                    terms = S_list(h)
                    for i, (lq, lqk, sbf, sbk) in enumerate(terms):
                        self.mm(ps[0:P, c0:c0 + BV], lq, sbf, False, i == len(terms) - 1, [lqk, sbk], [pk])
                    self.act(junk[0:P, 0:BV], ps[0:P, c0:c0 + BV], AF.Square, [pk], [jk, sk], accum=ss[0:P, h:h + 1])
                sq, sqk = R['ss'].next()
                self.ts('dve', sq[0:P, :], ss[0:P, :], 1.0 / BV, EPS, ALU.mult, ALU.add, [sk], [sqk])
                self.rsqrt_col(sq[0:P, :], sq[0:P, :], sqk)
                for h in range(BH):
                    ps, pk = pss[h // 2]
                    c0 = (h % 2) * BV
                    self.stt('dve', ob[0:P, h * BV:(h + 1) * BV], ps[0:P, c0:c0 + BV], sq[0:P, h:h + 1], ggl[0:P, :],
                             ALU.mult, ALU.mult, [pk, sqk, K0], [obk])
                ob2, o2k = R['b1k'].next()
                self.tt('dve', ob2[0:P, :], ob[0:P, :], sr[0:P, :], ALU.mult, [obk, srk], [o2k])
                return ob2, o2k

            def state_update(res, P, h, lhs_kd, lkk, dlcol, dlk, S_in, S_in_k, S_out, S_out_k):
                vb, vbk = res['vb']
                ps, pk = self.psn()
                self.mm(ps[:, 0:BV], lhs_kd, vb[0:P, h * BV:(h + 1) * BV], True, True, [lkk, vbk], [pk])
                u2, uk = R['u2'].next()
                self.act(u2[:], ps[:, 0:BV], AF.Copy, [pk, dlk], [uk], scale=dlcol)
                self.stt('dve', S_out, S_in, dlcol, u2[:], ALU.mult, ALU.add, [S_in_k, dlk, uk], [S_out_k])

            for n in range(NB):
                own = n >= OWN0
                xt, xk = R['x'].next()
                self.dma('sp', xt[:, :], xc[n * 128:(n + 1) * 128, :], [], [xk])
                hnT, tk = R['hnT'].next()
                norm_T(xt[:, :], xk, 128, gpre, hnT[:, :, :], tk, R)
                outs = None
                if n > OWN0:
                    r0 = (n - OWN0 - 1) * 128
                    outs = {'k': o_k[r0:r0 + 128, :], 'v': o_v[r0:r0 + 128, :], 'lf': o_lf[r0:r0 + 128, :]}
                def qa_sink(qps, qpk, n=n):
                    qtb, qbk = R['qtb'].next()
                    self.ts('dve', qtb[:], qps[:, :].rearrange("p (f t) -> p f t", f=4), AD ** -0.5, None,
                            ALU.mult, None, [qpk], [qbk])
                    c0_ = (n - OWN0) * 128
                    self.dma('sp', qt_scr[:, :, c0_:c0_ + 128], qtb[:], [qbk], [('qt_scr',)])
                res = mixer_block(hnT, tk, 128, n, own, outs=outs, qa_sink=qa_sink)
                lsp, lk = res['lsp']
                kd, kdk = res['kd']
                ps, pk = self.psn()
                for h in range(BH):
                    self.mm(ps[:, h:h + 1], lsp[:, h * 128:(h + 1) * 128], onesn_b[:, 0:1], True, True, [lk, K0], [pk])
                dl, dlk = R['dl'].next()
                self.act(dl[:, 0:4], ps[:, 0:4], AF.Exp, [pk], [dlk])
                if own:
                    col0 = (n - OWN0) * 128
                    for h in range(BH):
                        self.cp('pool', Sbf[h][:], Sst[h][:], [('S', h)], [('Sbf', h)])
                    qdT, qtk = res['qdT']
                    ob2, o2k = gla_out(res, 128, lambda h: [(qdT[:, h, :], qtk, Sbf[h][:], ('Sbf', h))])
                    obTb, obk_ = R['obTb'].next()
                    transpose_into(obTb[:, :, :], obk_, ob2, o2k, 128, 8)
                    self.dma('sp', obt_scr[:, :, col0:col0 + 128], obTb[:], [obk_], [('obt_scr',)])
                for h in range(BH):
                    state_update(res, 128, h, kd[:, h * 128:(h + 1) * 128], kdk, dl[:, h:h + 1], dlk,
                                 Sst[h][:], ('S', h), Sst[h][:], ('S', h))
            for h in range(BH):
                self.out_events.append(self.dma('sp', o_gla[h * 128:(h + 1) * 128, :], Sst[h][:], [('S', h)], []))

            if STOP >= 2:
                S0 = self.sb("S0", [128, NSQ * BH, BV], F32)
                S0b = self.sb("S0b", [128, NSQ * BH, BV], BF16)
                self.dma('sp', S0[:], sgla_d.rearrange("(g p) v -> p g v", p=128), [], [('S0',)])
                self.cp('pool', S0b[:], S0[:], [('S0',)], [('S0b',)])
                xt, xk = R['x'].next()
                self.dma('sp', xt[0:32, :], xs[:, :], [], [xk])
                hnT, tk = R['hnT'].next()
                norm_T(xt[0:32, :], xk, 32, gpre, hnT[:, :, 0:32], tk, R)
                outs = {'k': o_ks[:, :], 'v': o_vs[:, :], 'lf': o_lfs[:, :]}
                def qa_sink_s(qps, qpk):
                    self.ts('dve', QTs[:], qps[:, 0:128].rearrange("p (f t) -> p f t", f=4), AD ** -0.5, None,
                            ALU.mult, None, [qpk], [('QTs',)])
                res = mixer_block(hnT, tk, 32, None, True, sample=True, outs=outs, qa_sink=qa_sink_s)
                qdT, qtk = res['qdT']
                qdTm = self.sb("qdTm", [128, NSQ, 4, 32], BF16)
                for s in range(NSQ):
                    self.tt('dve', qdTm[:, s, :, :], qdT[:, :, 0:32], colmask[:, s, :].unsqueeze(1).to_broadcast([128, 4, 32]),
                            ALU.mult, [qtk, K0], [('qdTm',)])
                ob2, o2k = gla_out(res, 32, lambda h: [(qdTm[:, s, h, :], ('qdTm',), S0b[:, s * BH + h, :], ('S0b',)) for s in range(NSQ)])
                transpose_into(obTs[:, :, :], ('obTs',), ob2, o2k, 32, 8)
                lsp, lk = res['lsp']
                kd, kdk = res['kd']
                ps, pk = self.psn()
                for h in range(BH):
                    self.mm(ps[:, h * 4:(h + 1) * 4], lsp[0:32, h * 128:(h + 1) * 128], seqindn_b[0:32, 0:4], True, True, [lk, K0], [pk])
                dl, dlk = R['dl'].next()
                self.act(dl[:, 0:16], ps[:, 0:16], AF.Exp, [pk], [dlk])
                kdm = self.sb("kdm", [32, NSQ, 512], BF16)
                for s in range(NSQ):
                    self.ts('dve', kdm[:, s, :], kd[0:32, :], seqind_f[:, s:s + 1], None, ALU.mult, None, [kdk, K0], [('kdm',)])
                Sn = self.ring("Sn", 2, [128, BV], F32)
                for s in range(NSQ):
                    for h in range(BH):
                        sn, snk = Sn.next()
                        g = s * BH + h
                        state_update(res, 32, h, kdm[:, s, h * 128:(h + 1) * 128], ('kdm',), dl[:, h * 4 + s:h * 4 + s + 1], dlk,
                                     S0[:, g, :], ('S0',), sn[:], snk)
                        self.out_events.append(self.dma('sp', o_glas[g * 128:(g + 1) * 128, :], sn[:], [snk], []))

        if STOP <= 2:
            return self.finish_all()

        groups = [(OWN0, 1)] + [(OWN0 + 1 + g * QG, QG) for g in range(NQ // QG)]
        NG = len(groups)
        self.psring = Ring(banks[0:4], "ps")
        with self.phase():
            biasAll = self.sb("biasAll", [128, NG, NB, AH], F32)
            for gi, (n0, nq) in enumerate(groups):
                nk = n0 + nq
                self.tt('dve', biasAll[:, gi, 0:nk, :], Ccum[:, 0:nk, :], carry[:, n0, :].unsqueeze(1).to_broadcast([128, nk, AH]),
                        ALU.subtract, [], [('bias', gi)])
                self.tt('dve', biasAll[:, gi, 0:nk, :], biasAll[:, gi, 0:nk, :], kmask[:, 0:nk].unsqueeze(2).to_broadcast([128, nk, AH]),
                        ALU.add, [('bias', gi)], [('bias', gi)])
            R_kt = self.ring("KT", 2, [128, TT], BF16)
            R_vxh = self.ring("VXh", 2, [128, NB, 130], BF16)
            ones_b = self.sb("ones_b", [65, 64], BF16)
            S.op('pool', lambda: nc.gpsimd.memset(ones_b[:], 1.0), [], [('ones_b',)])
            R_qth = self.ring("QTh", 2, [128, TO], BF16)
            R_pt = self.ring("pt", 8, [128, QG * 128], BF16)
            R_rec = self.ring("rec", 2, [65, 2 * QG * 128], BF16)
            R_recf = self.ring("recf", 2, [65, QG * 128], F32)
            R_rb = self.ring("rbc", 2, [64, QG * 128], F32)
            R_oT = self.ring("oT", 3, [64, QG * 128], BF16)
            accs = [(banks[4], ('ps', 4)), (banks[5], ('ps', 5))]
            LOOK = 4

            def prompt_gen():
                acci = 0
                for hp in range(4):
                    KT, ktk = R_kt.next()
                    VX, vxk = R_vxh.next()
                    QTh, qhk = R_qth.next()
                    for c in range(0, TT, 2048):
                        self.dma('sp', KT[:, c:c + 2048], kt_scr[:, hp, c:c + 2048], [], [ktk])
                    for c in range(0, NB, 8):
                        self.dma('sp', VX[:, c:c + 8, :], vx_scr[c:c + 8, :, hp * 130:(hp + 1) * 130].rearrange("n p c -> p n c"), [], [vxk])
                    self.dma('sp', QTh[:], qt_scr[:, hp, :], [], [qhk])
                    units = []
                    for gi, (n0, nq) in enumerate(groups):
                        for hh in range(2):
                            for n in range(n0 + nq):
                                units.append((gi, n0, nq, hh, n))
                    pend = {}
                    accof = {}
                    evs = {}

                    def emit_qk(u):
                        gi, n0, nq, hh, n = units[u]
                        W = nq * 128
                        c0 = (n0 - OWN0) * 128
                        h = 2 * hp + hh
                        prs = slice(hh * 64, (hh + 1) * 64)
                        lo = max(n - n0, 0) * 128
                        psS, psk = self.psn()
                        self.mm(psS[:, lo:W], KT[prs, n * 128:(n + 1) * 128], QTh[prs, c0 + lo:c0 + W], True, True, [ktk, qhk], [psk])
                        pt, ptk = R_pt.next()
                        self.act(pt[:, lo:W], psS[:, lo:W], AF.Exp, [psk, ('bias', gi)], [ptk], bias=biasAll[:, gi, n, h:h + 1])
                        if n >= n0:
                            self.tt('dve', pt[:, lo:lo + 128], pt[:, lo:lo + 128], tri_b[:, :], ALU.mult, [ptk, K0], [ptk])
                        pend[u] = (pt, ptk)
                        evs[u] = [('e', 'act', S.cnt['act']), ('e', 'dve', S.cnt['dve']) if n >= n0 else None]

                    def emit_pv(u):
                        nonlocal acci
                        gi, n0, nq, hh, n = units[u]
                        W = nq * 128
                        c0 = (n0 - OWN0) * 128
                        lo = max(n - n0, 0) * 128
                        if n == 0:
                            accof[(gi, hh)] = accs[acci % 2]
                            acci += 1
                        psO, pok = accof[(gi, hh)]
                        pt, ptk = pend.pop(u)
                        last = (n == n0 + nq - 1)
                        self.mm(psO[0:65, lo:W], VX[:, n, hh * 65:(hh + 1) * 65], pt[:, lo:W], n == 0, last, [ptk, vxk], [pok])
                        if last:
                            rf, rfk = R_recf.next()
                            self.ts('dve', rf[64:65, 0:W], psO[64:65, 0:W], 1e-30, None, ALU.max, None, [pok], [rfk])
                            S.op('dve', lambda rf=rf, W=W: nc.vector.reciprocal(out=rf[64:65, 0:W], in_=rf[64:65, 0:W]), [rfk], [rfk])
                            rec, rk = R_rec.next()
                            self.cp('dve', rec[64:65, 0:W], rf[64:65, 0:W], [rfk], [rk])
                            self.tt('dve', rec[64:65, W:2 * W], rf[64:65, 0:W], rec[64:65, 0:W], ALU.subtract, [rfk, rk], [rk])
                            psB, pbk = self.psn()
                            self.mm(psB[0:64, 0:W], ones_b[64:65, 0:64], rec[64:65, 0:W], True, False, [rk, ('ones_b',)], [pbk])
                            self.mm(psB[0:64, 0:W], ones_b[64:65, 0:64], rec[64:65, W:2 * W], False, True, [rk, ('ones_b',)], [pbk])
                            rb, rbk = R_rb.next()
                            self.cp('act', rb[0:64, 0:W], psB[0:64, 0:W], [pbk], [rbk])
                            oT, otk = R_oT.next()
                            self.tt('dve', oT[:, 0:W], psO[0:64, 0:W], rb[0:64, 0:W], ALU.mult, [pok, rbk], [otk])
                            self.dma('sp', oat_scr[hh * 64:(hh + 1) * 64, hp, c0:c0 + W], oT[:, 0:W], [otk], [('oat_scr',)])

                    NU = len(units)
                    for i in range(0, NU + LOOK + 1, 2):
                        hi_q = min(i + 1, NU - 1)
                        if i < NU and hi_q - 4 >= 0 and (hi_q - 4) in evs:
                            S.need('pe', evs[hi_q - 4][0])
                        for u in (i, i + 1):
                            if u < NU:
                                emit_qk(u)
                        v0 = i - LOOK - 1
                        hi_v = min(v0 + 1, NU - 1)
                        if hi_v >= 0 and hi_v in evs:
                            for ev in evs[hi_v]:
                                if ev is not None:
                                    S.need('pe', ev)
                        for v in (v0, v0 + 1):
                            if 0 <= v < NU:
                                emit_pv(v)
                        yield

            n_prompt_steps = 4 * ((sum(2 * (n0 + nq) for (n0, nq) in groups) + LOOK + 2) // 2)

            def sample_gen():
                ptb = self.sb("ptb", [128, NSQ * NPG], I32)
                idx = self.sb("idx", [128, NSQ * NPG], I32)
                self.dma('sp', ptb[:], pt_d.partition_broadcast(128), [], [('ptb',)], slow=True)
                ptf = self.sb("ptf", [128, NSQ * NPG], F32)
                iof = self.sb("iof", [128, 1], F32)
                self.cp('dve', ptf[:], ptb[:], [('ptb',)], [('ptf',)])
                self.cp('dve', iof[:], iota_i[:], [K0], [('iof',)])
                self.stt('dve', ptf[:], ptf[:], 128.0, iof[:, 0:1].to_broadcast([128, NSQ * NPG]), ALU.mult, ALU.add,
                         [('ptf',), ('iof',)], [('ptf',)])
                self.cp('dve', idx[:], ptf[:], [('ptf',)], [('idx',)])
                qblk = self.sb("qblk", [128, NSQ, 4, 16], BF16)
                S.op('pool', lambda: nc.gpsimd.memset(qblk[:], 0.0), [], [('qblk',)])
                for s in range(NSQ):
                    for hh in range(2):
                        self.cp('dve', qblk[hh * 64:(hh + 1) * 64, s, :, hh * 8:(hh + 1) * 8], QTs[hh * 64:(hh + 1) * 64, :, s * 8:(s + 1) * 8],
                                [('QTs',), ('qblk',)], [('qblk',)])
                R_kr = self.ring("kraw", 3, [128, AW], F32)
                R_vr = self.ring("vraw", 3, [128, AW], F32)
                R_lr = self.ring("lraw", 4, [128, AH], F32)
                R_ktp = self.ring("ktp", 2, [128, 4, 128], BF16)
                R_kb16 = self.ring("kb16", 2, [128, AW], BF16)
                R_l2 = self.ring("l2", 3, [128, 16], BF16)
                trigt_b = self.sb("trigt_b", [128, 128], BF16)
                ones_bb = self.sb("ones_bb", [128, 128], BF16)
                self.dma('pool', trigt_b[:], cst['c_trigt'][:, :], [], [('trigt_b',)])
                S.op('pool', lambda: nc.gpsimd.memset(ones_bb[:], 1.0), [], [('ones_bb',)])
                R_vxp = self.ring("vxp", 2, [128, AH, 65], BF16)
                for t in R_vxp.tiles:
                    S.op('pool', lambda t=t: nc.gpsimd.memset(t[:], 1.0), [], [('vxp', R_vxp.tiles.index(t))])
                R_sfx = self.ring("sfx", 3, [128, AH], F32)
                R_cs = self.ring("cs", 3, [128, AH], F32)
                R_e = self.ring("e64", 3, [128, AH, 8], F32)
                R_p64 = self.ring("p64", 3, [128, 64], BF16)
                R_os = self.ring("os", 2, [8, AH, 64], BF16)
                R_rs = self.ring("rs", 2, [8, AH], F32)
                accs2 = [(banks[6], ('ps', 6)), (banks[7], ('ps', 7))]
                yield
                order = [(s, pg) for s in range(NSQ) for pg in range(NPG - 1, -1, -1)]
                raw = {}

                def gather(i):
                    s, pg = order[i]
                    col = s * NPG + pg
                    kr, krk = R_kr.next()
                    vr, vrk = R_vr.next()
                    lr, lrk = R_lr.next()
                    for (dst, dk_, src) in ((lr, lrk, cache_lf), (kr, krk, cache_k), (vr, vrk, cache_v)):
                        S.dma('pool', lambda dst=dst, src=src, col=col: nc.gpsimd.indirect_dma_start(
                            out=dst[:, :], out_offset=None, in_=src[:, :],
                            in_offset=bass.IndirectOffsetOnAxis(ap=idx[:, col:col + 1], axis=0)), [('idx',)], [dk_])
                    raw[i] = (kr, krk, vr, vrk, lr, lrk)

                gather(0)
                cs = csk = None
                for i, (s, pg) in enumerate(order):
                    if i + 1 < len(order):
                        gather(i + 1)
                    kr, krk, vr, vrk, lr, lrk = raw.pop(i)
                    first = (pg == NPG - 1)
                    if first:
                        cs, csk = R_cs.next()
                        S.op('dve', lambda cs=cs: nc.vector.memset(cs[:], 0.0), [], [csk])
                    l2, l2k = R_l2.next()
                    self.cp('dve', l2[:, 0:8], lr[:, :], [lrk], [l2k])
                    self.tt('dve', l2[:, 8:16], lr[:, :], l2[:, 0:8], ALU.subtract, [lrk, l2k], [l2k])
                    ps, pk = self.psn()
                    self.mm(ps[:, 0:8], trigt_b[:], l2[:, 0:8], True, False, [l2k, ('trigt_b',)], [pk])
                    self.mm(ps[:, 0:8], trigt_b[:], l2[:, 8:16], False, True, [l2k, ('trigt_b',)], [pk])
                    self.mm(ps[:, 8:16], ones_bb[:], l2[:, 0:8], False, False, [l2k, ('ones_bb',)], [pk])
                    self.mm(ps[:, 8:16], ones_bb[:], l2[:, 8:16], False, True, [l2k, ('ones_bb',)], [pk])
                    sfx, sxk = R_sfx.next()
                    self.tt('dve', sfx[:], ps[:, 0:8], cs[:], ALU.add, [pk, csk], [sxk])
                    cs2, csk2 = R_cs.next()
                    self.tt('dve', cs2[:], ps[:, 8:16], cs[:], ALU.add, [pk, csk], [csk2])
                    cs, csk = cs2, csk2
                    kb, kbk = R_kb16.next()
                    self.cp('dve', kb[:], kr[:], [krk], [kbk])
                    ps, pk = self.psn()
                    pvk = bview(ps)
                    for hp in range(4):
                        self.tr(pvk[:, hp * 128:(hp + 1) * 128], kb[:, hp * 128:(hp + 1) * 128], ident[:, :], [kbk, K0], [pk])
                    ktp, kpk = R_ktp.next()
                    self.cp('act', ktp[:], pvk[:, 0:512].rearrange("p (f t) -> p f t", f=4), [pk], [kpk])
                    psS, psk = self.psn()
                    for hp in range(4):
                        self.mm(psS[:, hp * 16:(hp + 1) * 16], ktp[:, hp, :], qblk[:, s, hp, :], True, True, [kpk, ('qblk',)], [psk])
                    e, ek_ = R_e.next()
                    self.tt('dve', e[:], psS[:, 0:64].rearrange("p (h q) -> p h q", h=AH), sfx[:].unsqueeze(2).to_broadcast([128, AH, 8]),
                            ALU.add, [psk, sxk], [ek_])
                    p64, p6k = R_p64.next()
                    self.act(p64[:], e[:].rearrange("p h q -> p (h q)"), AF.Exp, [ek_], [p6k])
                    vxp, vpk = R_vxp.next()
                    self.cp('dve', vxp[:, :, 0:64], vr[:, :].rearrange("p (h d) -> p h d", h=AH), [vrk], [vpk])
                    for h in range(AH):
                        psO, pok = accs2[h // 4]
                        cc = (h % 4) * 65
                        self.mm(psO[0:8, cc:cc + 65], p64[:, h * 8:(h + 1) * 8], vxp[:, h, :], first and h % 4 == 0, False, [p6k, vpk], [pok])
                    if pg == 0:
                        psS, psk = self.psn()
                        for hp in range(4):
                            self.mm(psS[0:32, hp * 16:(hp + 1) * 16], KTs[:, hp, :], qblk[:, s, hp, :], True, True, [('KTs',), ('qblk',)], [psk])
                        e, ek_ = R_e.next()
                        self.tt('dve', e[0:32], psS[0:32, 0:64].rearrange("p (h q) -> p h q", h=AH), Csn[:, :].unsqueeze(2).to_broadcast([32, AH, 8]),
                                ALU.add, [psk, ('Csn',)], [ek_])
                        p64f = R_e.next()
                        self.act(p64f[0][0:32], e[0:32], AF.Exp, [ek_], [p64f[1]])
                        p64, p6k = R_p64.next()
                        self.tt('dve', p64[0:32, :].rearrange("p (h q) -> p h q", h=AH), p64f[0][0:32],
                                masknew[:, s, :].unsqueeze(1).to_broadcast([32, AH, 8]), ALU.mult, [p64f[1], K0], [p6k])
                        for h in range(AH):
                            psO, pok = accs2[h // 4]
                            cc = (h % 4) * 65
                            self.mm(psO[0:8, cc:cc + 65], p64[0:32, h * 8:(h + 1) * 8], VXs[0:32, h, :], False, True, [p6k, ('VXs',)], [pok])
                        os_, osk = R_os.next()
                        rs, rsk = R_rs.next()
                        for half in range(2):
                            psO, pok = accs2[half]
                            pv = psO[0:8, 0:260].rearrange("p (h e) -> p h e", e=65)
                            self.ts('dve', rs[:, half * 4:(half + 1) * 4].unsqueeze(2), pv[:, :, 64:65], 1e-30, None, ALU.max, None, [pok], [rsk])
                        S.op('dve', lambda rs=rs: nc.vector.reciprocal(out=rs[:, :], in_=rs[:, :]), [rsk], [rsk])
                        for half in range(2):
                            psO, pok = accs2[half]
                            pv = psO[0:8, 0:260].rearrange("p (h e) -> p h e", e=65)
                            self.tt('dve', os_[:, half * 4:(half + 1) * 4, :], pv[:, :, 0:64],
                                    rs[:, half * 4:(half + 1) * 4].unsqueeze(2).to_broadcast([8, 4, 64]), ALU.mult, [pok, rsk], [osk])
                        ps, pk = self.psn()
                        pvb = bview(ps)
                        osf = os_[:].rearrange("p h d -> p (h d)")
                        for fc in range(4):
                            self.tr(pvb[:, fc * 8:(fc + 1) * 8], osf[:, fc * 128:(fc + 1) * 128], ident[0:8, 0:8], [osk, K0], [pk])
                        self.cp('act', oaTs[:, :, s * 8:(s + 1) * 8], pvb[:, 0:32].rearrange("p (f t) -> p f t", f=4), [pk], [('oaTs',)])
                    yield

            sg_ = sample_gen() if STOP >= 4 else iter(())
            n_s = NSQ * NPG + 1
            done_s = 0
            for i, _ in enumerate(prompt_gen()):
                target = (i + 1) * n_s / n_prompt_steps
                while done_s < target:
                    next(sg_, None)
                    done_s += 1
            for _ in sg_:
                pass
        self.psring = Ring(banks, "ps")
        if STOP <= 4:
            return self.finish_all()

        with self.phase():
            gpre = gload("gpre4", g_pre_mix)
            gpm = gload("gpm", g_post_mix)
            W_ga = wload("W_ga", w_in, 0, D, O_GA, 1024)
            W_gb = wload("W_gb", w_in, 0, D, O_GB, 1024)
            W_pa = wload("W_pa", w_pa, 0, AW, 0, 1024)
            W_pb = wload("W_pb", w_pb, 0, D, 0, 1024)
            W_o = wload("W_o", w_out, 0, D, 0, 1024)
            R = {
                'x': self.ring("xt4", 2, [128, D], F32), 'junk': self.ring("junk4", 1, [128, D], BF16),
                'ss': self.ring("ss4", 4, [128, 4], F32), 'hn': self.ring("hn4", 2, [128, D], BF16),
                'hnT': self.ring("hnT4", 2, [128, 8, 128], BF16),
            }
            R_sg = self.ring("sg", 2, [128, 2, D], F32)
            R_oa = self.ring("oa4", 2, [128, 4, 128], BF16)
            R_ob = self.ring("ob4", 2, [128, 8, 128], BF16)
            R_m = self.ring("m4", 2, [128, D], F32)
            R_mb = self.ring("mb4", 2, [128, D], BF16)
            R_mT = self.ring("mT4", 2, [128, 8, 128], BF16)
            R_xm = self.ring("xm4", 2, [128, D], F32)
            tiles = [(xc[n * 128:(n + 1) * 128, :], 128, (n - OWN0) * 128, None) for n in range(OWN0, NB)]
            tiles.append((xs[:, :], 32, TO, 'sample'))
            for (src, P, r0, kind) in tiles:
                xt, xk = R['x'].next()
                self.dma('sp', xt[0:P, :], src, [], [xk])
                hnT, tk = R['hnT'].next()
                norm_T(xt[0:P, :], xk, P, gpre, hnT[:, :, 0:P], tk, R)
                sg, sgk = R_sg.next()
                for gi, Wg in enumerate((W_ga, W_gb)):
                    for half in range(2):
                        ps, pk = self.psn()
                        proj_tok(hnT, tk, P, Wg, half * 512, 512, ps, pk)
                        self.act(sg[0:P, gi, half * 512:(half + 1) * 512], ps[0:P, :], AF.Sigmoid, [pk], [sgk])
                if kind == 'sample':
                    oa, oak, ob_, obk = oaTs, ('oaTs',), obTs, ('obTs',)
                else:
                    oa, oak = R_oa.next()
                    ob_, obk = R_ob.next()
                    self.dma('sp', oa[:], oat_scr[:, :, r0:r0 + 128], [], [oak])
                    self.dma('sp', ob_[:], obt_scr[:, :, r0:r0 + 128], [], [obk])
                m, mk = R_m.next()
                mb, mbk = R_mb.next()
                for half in range(2):
                    cs_ = slice(half * 512, (half + 1) * 512)
                    ps, pk = self.psn()
                    proj_tok(oa, oak, P, W_pa, half * 512, 512, ps, pk, nk=4)
                    self.tt('dve', m[0:P, cs_], ps[0:P, :], sg[0:P, 0, cs_], ALU.mult, [pk, sgk], [mk])
                    ps, pk = self.psn()
                    proj_tok(ob_, obk, P, W_pb, half * 512, 512, ps, pk)
                    self.tt('dve', sg[0:P, 1, cs_], ps[0:P, :], sg[0:P, 1, cs_], ALU.mult, [pk, sgk], [sgk])
                    self.tt('dve', mb[0:P, cs_], m[0:P, cs_], sg[0:P, 1, cs_], ALU.add, [mk, sgk], [mbk])
                mT, mtk = R_mT.next()
                transpose_into(mT[:, :, 0:P], mtk, mb, mbk, P, 8)
                ss, sk = R['ss'].next()
                junk, jk = R['junk'].next()
                pss = []
                for half in range(2):
                    ps, pk = self.psn()
                    proj_tok(mT, mtk, P, W_o, half * 512, 512, ps, pk)
                    self.act(junk[0:P, 0:512], ps[0:P, :], AF.Square, [pk], [jk, sk], accum=ss[0:P, half:half + 1])
                    pss.append((ps, pk))
                self.tt('dve', ss[0:P, 2:3], ss[0:P, 0:1], ss[0:P, 1:2], ALU.add, [sk], [sk])
                self.ts('dve', ss[0:P, 2:3], ss[0:P, 2:3], 1.0 / D, EPS, ALU.mult, ALU.add, [sk], [sk])
                self.rsqrt_col(ss[0:P, 3:4], ss[0:P, 2:3], sk)
                xm, xmk = R_xm.next()
                for half in range(2):
                    cs_ = slice(half * 512, (half + 1) * 512)
                    ps, pk = pss[half]
                    self.stt('dve', xm[0:P, cs_], ps[0:P, :], ss[0:P, 3:4], gpm[0:P, cs_], ALU.mult, ALU.mult, [pk, sk, K0], [xmk])
                self.tt('dve', xm[0:P, :], xm[0:P, :], xt[0:P, :], ALU.add, [xmk, xk], [xmk])
                self.dma('sp', xm_scr[r0:r0 + P, :], xm[0:P, :], [xmk], [('xm_scr',)])

        if STOP <= 5:
            return self.finish_all()

        with self.phase():
            gpf = gload("gpf", g_pre_ffn)
            gpo = gload("gpo", g_post_ffn)
            wcv = self.sb("wcv", [128, 3, 2 * NCH], F32)
            bcv = self.sb("bcv", [128, 2 * NCH], F32)
            for t in range(3):
                self.dma('sp', wcv[:, t, :], w_conv[t:t + 1, :].rearrange("o (c p) -> p (o c)", p=128), [], [K0], slow=True)
            self.dma('sp', bcv[:], b_conv.rearrange("o (c p) -> p (o c)", p=128), [], [K0], slow=True)
            W_dn = wload("W_dn", w_down, 0, DFF, 0, 1024)
            GW = QG * 128
            R = {
                'junk': self.ring("junk5", 1, [128, D], BF16), 'ss': self.ring("ss5", 4, [128, 4], F32),
                'hn': self.ring("hn5", 2, [128, D], BF16),
            }
            xmg = self.sb("xmg", [128, QG, D], F32)
            h2T = self.sb("h2T", [128, 8, GW], BF16)
            hT = self.sb("hT", [128, NCH, GW], BF16)
            lbp = self.sb("lbp", [128, 2 * NCH, 1, 2], F32)
            lbs = self.sb("lbs", [128, 2 * NCH, NSQ, 2], F32)
            S.op('pool', lambda: nc.gpsimd.memset(lbp[:], 0.0), [], [('lbp',)])
            R_wu = self.ring("wu", 3, [128, 8, 256], BF16)
            R_uc = self.ring("uc", 3, [128, GW + 8], F32)
            R_c = self.ring("cv", 4, [128, GW], F32)
            R_t = self.ring("tg", 3, [128, GW], F32)
            R_tp = self.ring("tpl", 2, [128, GW], F32)
            R_y = self.ring("y5", 2, [128, D], F32)
            R_wt = self.ring("wt", 2, [128, 8, 512], BF16)
            R_ut = self.ring("ut", 2, [128, 512], F32)
            sct = self.sb("sct", [8, 1408], F32)
            ps, pk = self.psn()
            for piece in range(4):
                self.dma('sp', sct[:], sconv_d[:, piece * 1408:(piece + 1) * 1408], [], [('sct',)])
                for c in range(11):
                    ch = piece * 11 + c
                    self.tr(ps[:, ch * 8:(ch + 1) * 8], sct[0:8, c * 128:(c + 1) * 128], identf[0:8, 0:8], [('sct',), K0], [pk])
            self.cp('act', lbs[:].rearrange("p c s l -> p (c s l)"), ps[:, 0:2 * NCH * 8], [pk], [('lbs',)])

            fgroups = [(OWN0, 1, 1, 128, 'p')] + [(OWN0 + 1 + g * QG, QG, 1, QG * 128, 'p') for g in range(NQ // QG)]
            fgroups.append((None, 1, NSQ, 8, 's'))
            for (n0, nblk, nseq, L, kind) in fgroups:
                W = nseq * L
                lb, lbk = (lbp, ('lbp',)) if kind == 'p' else (lbs, ('lbs',))
                for b in range(nblk):
                    P = 128 if kind == 'p' else 32
                    r0 = (n0 + b - OWN0) * 128 if kind == 'p' else TO
                    self.dma('sp', xmg[0:P, b, :], xm_scr[r0:r0 + P, :], [], [('xmg', b)])
                    norm_T(xmg[0:P, b, :], ('xmg', b), P, gpf, h2T[:, :, b * 128:b * 128 + P], ('h2T',), R)
                for ch in range(NCH):
                    wu, wuk = R_wu.next()
                    self.dma('sp', wu[:, :, 0:128], wup_scr[:, :, ch * 128:(ch + 1) * 128], [], [wuk])
                    self.dma('sp', wu[:, :, 128:256], wup_scr[:, :, DFF + ch * 128:DFF + (ch + 1) * 128], [], [wuk])
                    cvs = []
                    for part in range(2):
                        ci = part * NCH + ch
                        ps, pk = self.psn()
                        for kc in range(8):
                            self.mm(ps[:, 0:W], wu[:, kc, part * 128:(part + 1) * 128], h2T[:, kc, 0:W], kc == 0, kc == 7, [wuk, ('h2T',)], [pk])
                        uc, uck = R_uc.next()
                        ucv = uc[:, 0:nseq * (L + 2)].rearrange("p (s l) -> p s l", s=nseq)
                        self.cp('act', ucv[:, :, 2:2 + L], ps[:, 0:W].rearrange("p (s l) -> p s l", s=nseq), [pk], [uck])
                        self.cp('pool', ucv[:, :, 0:2], lb[:, ci, :, :], [lbk], [uck])
                        cv, cvk = R_c.next()
                        cvv = cv[:, 0:W].rearrange("p (s l) -> p s l", s=nseq)
                        self.act(cvv, ucv[:, :, 2:2 + L], AF.Identity, [uck, K0], [cvk], bias=bcv[:, ci:ci + 1], scale=wcv[:, 2, ci:ci + 1])
                        self.stt('dve', cvv, ucv[:, :, 1:1 + L], wcv[:, 1, ci:ci + 1], cvv, ALU.mult, ALU.add, [uck, cvk, K0], [cvk])
                        self.stt('dve', cvv, ucv[:, :, 0:L], wcv[:, 0, ci:ci + 1], cvv, ALU.mult, ALU.add, [uck, cvk, K0], [cvk])
                        if kind == 'p':
                            self.cp('pool', lbp[:, ci, :, :], ucv[:, :, L:L + 2], [uck], [('lbp',)])
                        cvs.append((cv, cvk))
                    (ca, cak), (cg, cgk) = cvs
                    t1, t1k = R_t.next()
                    self.act(t1[:, 0:W], cg[:, 0:W], AF.Square, [cgk], [t1k])
                    self.ts('dve', t1[:, 0:W], t1[:, 0:W], 0.044715, 1.0, ALU.mult, ALU.add, [t1k], [t1k])
                    self.tt('dve', t1[:, 0:W], t1[:, 0:W], cg[:, 0:W], ALU.mult, [t1k, cgk], [t1k])
                    t2, t2k = R_t.next()
                    self.act(t2[:, 0:W], t1[:, 0:W], AF.Sigmoid, [t1k], [t2k], scale=1.5957691216057308)
                    self.tt('dve', t2[:, 0:W], t2[:, 0:W], cg[:, 0:W], ALU.mult, [t2k, cgk], [t2k])
                    self.tt('dve', hT[:, ch, 0:W], t2[:, 0:W], ca[:, 0:W], ALU.mult, [t2k, cak], [('hT',)])
                for b in range(nblk):
                    P = 128 if kind == 'p' else 32
                    ss, sk = R['ss'].next()
                    junk, jk = R['junk'].next()
                    pss = []
                    for half in range(2):
                        ps, pk = self.psn()
                        for ch in range(NCH):
                            self.mm(ps[0:P, :], hT[:, ch, b * 128:b * 128 + P], W_dn[:, ch, half * 512:(half + 1) * 512],
                                    ch == 0, ch == NCH - 1, [('hT',), K0], [pk])
                        self.act(junk[0:P, 0:512], ps[0:P, :], AF.Square, [pk], [jk, sk], accum=ss[0:P, half:half + 1])
                        pss.append((ps, pk))
                    self.tt('dve', ss[0:P, 2:3], ss[0:P, 0:1], ss[0:P, 1:2], ALU.add, [sk], [sk])
                    self.ts('dve', ss[0:P, 2:3], ss[0:P, 2:3], 1.0 / D, EPS, ALU.mult, ALU.add, [sk], [sk])
                    self.rsqrt_col(ss[0:P, 3:4], ss[0:P, 2:3], sk)
                    y, yk = R_y.next()
                    for half in range(2):
                        cs_ = slice(half * 512, (half + 1) * 512)
                        ps, pk = pss[half]
                        self.stt('dve', y[0:P, cs_], ps[0:P, :], ss[0:P, 3:4], gpo[0:P, cs_], ALU.mult, ALU.mult, [pk, sk, K0], [yk])
                    self.tt('dve', y[0:P, :], y[0:P, :], xmg[0:P, b, :], ALU.add, [yk, ('xmg', b)], [yk])
                    if kind == 's':
                        self.out_events.append(self.dma('sp', o_ys[:, :], y[0:32, :], [yk], []))
                    elif n0 + b > OWN0:
                        ro = (n0 + b - OWN0 - 1) * 128
                        self.out_events.append(self.dma('sp', o_y[ro:ro + 128, :], y[:, :], [yk], []))
                last_p = (kind == 'p' and n0 + nblk == NB)
                if last_p or kind == 's':
                    P = 128 if kind == 'p' else 32
                    bcol = (nblk - 1) * 128
                    for cgp in range(2 * DFF // 512):
                        wt, wtk = R_wt.next()
                        self.dma('sp', wt[:], wup_scr[:, :, cgp * 512:(cgp + 1) * 512], [], [wtk])
                        ps, pk = self.psn()
                        for kc in range(8):
                            self.mm(ps[0:P, :], h2T[:, kc, bcol:bcol + P], wt[:, kc, :], kc == 0, kc == 7, [('h2T',), wtk], [pk])
                        ut, utk = R_ut.next()
                        self.cp('act', ut[0:P, :], ps[0:P, :], [pk], [utk])
                        if kind == 'p':
                            self.out_events.append(self.dma('sp', o_conv[:, cgp * 512:(cgp + 1) * 512], ut[126:128, :], [utk], []))
                        else:
                            for s in range(NSQ):
                                self.out_events.append(self.dma('sp', o_convs[2 * s:2 * s + 2, cgp * 512:(cgp + 1) * 512],
                                                                ut[8 * s + 6:8 * s + 8, :], [utk], []))
        return self.finish_all()

    def finish_all(self):
        S = self.S
        for ev in self.out_events:
            S.need('sp', ev)
        S.finish()
        self.es.close()
        return self.nc


_CACHE = {}


def _get_builder(NB, QG, NPG, NPOOL):
    key = (NB, QG, NPG, NPOOL)
    if key not in _CACHE:
        b = Builder(NB, QG, NPG, NPOOL)
        b.build()
        _CACHE[key] = b
    return _CACHE[key]


def kernel(x_prompt, x_sample, cache_k, cache_v, cache_logf, state_gla, state_conv, page_table,
           g_pre_mix, w_in, b_f, w_alpha2, b_alpha, g_gla, w_proj_a, w_proj_b, w_out, g_post_mix,
           g_pre_ffn, w_up, w_conv, b_conv, w_down, g_post_ffn):
    f32 = np.float32
    x_prompt = np.asarray(x_prompt, f32)
    B, SEQ, _ = x_prompt.shape
    DB, DS, _ = np.asarray(x_sample).shape
    NPOOL = np.asarray(cache_k).shape[1]
    NPG = np.asarray(page_table).shape[1]
    assert B == 2 and DB == 32 and DS == 8 and SEQ % 2048 == 0
    NB = SEQ // 128
    NQ = NB // 4
    QT_ = SEQ // 4
    QG = min(4, NQ)
    bld = _get_builder(NB, QG, NPG, NPOOL)
    consts = host_consts(QG)
    ck = np.ascontiguousarray(np.asarray(cache_k, f32)[0].reshape(NPOOL * 128, AW))
    cv = np.ascontiguousarray(np.asarray(cache_v, f32)[0].reshape(NPOOL * 128, AW))
    clf = np.ascontiguousarray(np.asarray(cache_logf, f32)[0].reshape(NPOOL * 128, AH))
    shared = {
        'cache_k': ck, 'cache_v': cv, 'cache_lf': clf,
        'w_in': np.ascontiguousarray(np.asarray(w_in, f32)[0]),
        'w_alpha2': np.ascontiguousarray(np.asarray(w_alpha2, f32)[0]),
        'b_alpha': np.asarray(b_alpha, f32).reshape(1, AW), 'b_f': np.asarray(b_f, f32).reshape(1, AH),
        'g_pre_mix': np.asarray(g_pre_mix, f32).reshape(1, D), 'g_gla': np.asarray(g_gla, f32).reshape(1, BV),
        'w_proj_a': np.ascontiguousarray(np.asarray(w_proj_a, f32)[0]),
        'w_proj_b': np.ascontiguousarray(np.asarray(w_proj_b, f32)[0]),
        'w_out': np.ascontiguousarray(np.asarray(w_out, f32)[0]),
        'g_post_mix': np.asarray(g_post_mix, f32).reshape(1, D), 'g_pre_ffn': np.asarray(g_pre_ffn, f32).reshape(1, D),
        'w_up': np.ascontiguousarray(np.asarray(w_up, f32)[0]),
        'w_conv': np.ascontiguousarray(np.asarray(w_conv, f32)[0]),
        'b_conv': np.asarray(b_conv, f32).reshape(1, 2 * DFF),
        'w_down': np.ascontiguousarray(np.asarray(w_down, f32)[0]),
        'g_post_ffn': np.asarray(g_post_ffn, f32).reshape(1, D),
    }
    shared.update(consts)
    xs_all = np.asarray(x_sample, f32)
    pt_all = np.asarray(page_table, np.int32)
    sg = np.asarray(state_gla, f32)[0]
    sc = np.asarray(state_conv, f32)[0]
    in_maps = []
    for c in range(8):
        b, j = c // 4, c % 4
        xc = np.zeros((SEQ, D), f32)
        xc[(3 - j) * QT_:] = x_prompt[b, :(j + 1) * QT_]
        km = np.zeros((128, NB), f32)
        km[:, :(3 - j) * NQ] = NEG
        m = dict(shared)
        m.update({
            'xc': xc, 'kmask': km,
            'xs': np.ascontiguousarray(xs_all[4 * c:4 * c + 4].reshape(32, D)),
            'pt': np.ascontiguousarray(pt_all[4 * c:4 * c + 4].reshape(1, 4 * NPG)),
            'sgla': np.ascontiguousarray(sg[4 * c:4 * c + 4].reshape(4 * BH * 128, BV)),
            'sconv': np.ascontiguousarray(sc[4 * c:4 * c + 4].reshape(8, 2 * DFF)),
        })
        in_maps.append(m)
    res = run_bass_kernel_spmd(bld.nc, in_maps, core_ids=list(range(8)))
    R = res.results
    y_p = np.zeros((B, SEQ, D), f32)
    k_p = np.zeros((1, B, SEQ, AH, AD), f32)
    v_p = np.zeros((1, B, SEQ, AH, AD), f32)
    lf_p = np.zeros((1, B, SEQ, AH), f32)
    gla_p = np.zeros((1, B, BH, BK, BV), f32)
    conv_p = np.zeros((1, B, 2, 2 * DFF), f32)
    y_s = np.zeros((DB, DS, D), f32)
    k_s = np.zeros((1, DB, DS, AH, AD), f32)
    v_s = np.zeros((1, DB, DS, AH, AD), f32)
    lf_s = np.zeros((1, DB, DS, AH), f32)
    gla_s = np.zeros((1, DB, BH, BK, BV), f32)
    conv_s = np.zeros((1, DB, 2, 2 * DFF), f32)
    for c in range(8):
        b, j = c // 4, c % 4
        r = R[c]
        sl = slice(j * QT_, (j + 1) * QT_)
        y_p[b, sl] = r['o_y']
        k_p[0, b, sl] = r['o_k'].reshape(QT_, AH, AD)
        v_p[0, b, sl] = r['o_v'].reshape(QT_, AH, AD)
        lf_p[0, b, sl] = r['o_lf']
        if j == 3:
            gla_p[0, b] = r['o_gla'].reshape(BH, BK, BV)
            conv_p[0, b] = r['o_conv']
        ss = slice(4 * c, 4 * c + 4)
        y_s[ss] = r['o_ys'].reshape(4, DS, D)
        k_s[0, ss] = r['o_ks'].reshape(4, DS, AH, AD)
        v_s[0, ss] = r['o_vs'].reshape(4, DS, AH, AD)
        lf_s[0, ss] = r['o_lfs'].reshape(4, DS, AH)
        gla_s[0, ss] = r['o_glas'].reshape(4, BH, BK, BV)
        conv_s[0, ss] = r['o_convs'].reshape(4, 2, 2 * DFF)
    return (y_p, y_s, k_p, v_p, lf_p, gla_p, conv_p, k_s, v_s, lf_s, gla_s, conv_s)
```

```python
import contextlib
import os
import numpy as np
import concourse.bass as bass
import concourse.mybir as mybir
from concourse.bass_utils import run_bass_kernel_spmd

F32 = mybir.dt.float32
BF16 = mybir.dt.bfloat16
I32 = mybir.dt.int32
AF = mybir.ActivationFunctionType
ALU = mybir.AluOpType
AX = mybir.AxisListType

D = 1024
AH, AD = 8, 64
AW = 512
BH, BK, BV = 4, 128, 256
GR = 16
DFF = 2816
NCH = DFF // 128
EPS = 1e-6
DIN = 6680
O_QA, O_KA, O_VA, O_F = 0, 512, 1024, 1536
O_QB, O_KB, O_VB, O_AL, O_RB, O_GA, O_GB = 1544, 2056, 2568, 3592, 3608, 4632, 5656
NEG = -30000.0
EPOCH = 30000
ND = 8


class Sched:
    def __init__(self, nc, es):
        self.nc, self.es = nc, es
        self.E = {'pe': nc.tensor, 'act': nc.scalar, 'dve': nc.vector, 'pool': nc.gpsimd, 'sp': nc.sync}
        self.cnt = {e: 0 for e in self.E}
        self.sems = {e: [] for e in self.E}
        self.waited = {}
        self.dsem = {q: [es.enter_context(nc.semaphore(f"d{q}{i}")) for i in range(ND)] for q in ('sp', 'pool', 'act')}
        self.dcnt = {q: 0 for q in self.dsem}
        self.dwaited = {}
        self.lastw = {}
        self.readers = {}
        self.nwaits = 0

    def _sem(self, e, n):
        ep = (n - 1) // EPOCH
        while len(self.sems[e]) <= ep:
            self.sems[e].append(self.es.enter_context(self.nc.semaphore(f"s{e}{len(self.sems[e])}")))
        return self.sems[e][ep], n - ep * EPOCH

    def need(self, E, ev):
        if ev[0] == 'e':
            _, Fe, n = ev
            if Fe == E and E == 'pe':
                return
            if self.waited.get((E, Fe), 0) >= n:
                return
            sem, val = self._sem(Fe, n)
            self.E[E].wait_ge(sem, val)
            self.waited[(E, Fe)] = n
            self.nwaits += 1
        else:
            _, q, k = ev
            slot, val = k % ND, 16 * (k // ND + 1)
            if self.dwaited.get((E, q, slot), 0) >= val:
                return
            self.E[E].wait_ge(self.dsem[q][slot], val)
            self.dwaited[(E, q, slot)] = val
            self.nwaits += 1

    def _deps(self, reads, writes):
        deps = []
        for r in reads:
            if r in self.lastw:
                deps.append(self.lastw[r])
            if isinstance(r, tuple) and r[0] == 'ps':
                deps.extend(self.readers.get(r, ()))
        for w in writes:
            if w in self.lastw:
                deps.append(self.lastw[w])
            deps.extend(self.readers.get(w, ()))
        return deps

    def _record(self, ev, reads, writes):
        for r in reads:
            lst = self.readers.setdefault(r, [])
            if ev[0] == 'e':
                lst[:] = [x for x in lst if not (x[0] == 'e' and x[1] == ev[1])]
            lst.append(ev)
        for w in writes:
            self.lastw[w] = ev
            self.readers[w] = []

    def op(self, E, fn, reads=(), writes=()):
        for d in self._deps(reads, writes):
            self.need(E, d)
        ins = fn()
        self.cnt[E] += 1
        sem, _ = self._sem(E, self.cnt[E])
        ins.then_inc(sem, 1)
        self._record(('e', E, self.cnt[E]), reads, writes)

    def dma(self, q, fn, reads=(), writes=()):
        for d in self._deps(reads, writes):
            self.need(q, d)
        k = self.dcnt[q]
        if k >= ND:
            self.need(q, ('d', q, k - ND))
        ins = fn()
        ins.then_inc(self.dsem[q][k % ND], 16)
        self.dcnt[q] = k + 1
        ev = ('d', q, k)
        self._record(ev, reads, writes)
        return ev

    def finish(self):
        for e in self.E:
            if self.cnt[e]:
                self.need('sp', ('e', e, self.cnt[e]))
        for q in self.dcnt:
            for k in range(max(0, self.dcnt[q] - ND), self.dcnt[q]):
                self.need('sp', ('d', q, k))


class Ring:
    def __init__(self, tiles, name):
        self.tiles, self.name, self.i = tiles, name, 0

    def next(self):
        t = self.tiles[self.i % len(self.tiles)]
        k = (self.name, self.i % len(self.tiles))
        self.i += 1
        return t, k


def host_consts(QG):
    tri = np.triu(np.ones((128, 128), np.float32))
    c = {
        'c_ident': np.eye(128, dtype=np.float32),
        'c_tri': tri,
        'c_trin': (-tri / 16.0).astype(np.float32),
        'c_ones': np.ones((128, 128), np.float32),
    }
    stair = np.zeros((QG, 128, QG * 128), np.float32)
    for d in range(QG):
        for qb in range(QG):
            if qb == d:
                stair[d][:, qb * 128:(qb + 1) * 128] = tri
            elif qb > d:
                stair[d][:, qb * 128:(qb + 1) * 128] = 1.0
    c['c_stair'] = stair.reshape(QG * 128, QG * 128)
    tri_s = np.zeros((32, 32), np.float32)
    seqind = np.zeros((32, 4), np.float32)
    for s in range(4):
        tri_s[s * 8:(s + 1) * 8, s * 8:(s + 1) * 8] = np.triu(np.ones((8, 8), np.float32))
        seqind[s * 8:(s + 1) * 8, s] = 1.0
    c['c_tris'] = tri_s
    c['c_trins'] = (-tri_s / 16.0).astype(np.float32)
    c['c_seqind'] = seqind
    c['c_seqindn'] = (-seqind / 16.0).astype(np.float32)
    colm = np.zeros((4, 128, 32), np.float32)
    for s in range(4):
        colm[s][:, s * 8:(s + 1) * 8] = 1.0
    c['c_colmask'] = colm.reshape(4 * 128, 32)
    c['c_iota'] = np.arange(128, dtype=np.int32).reshape(128, 1)
    c['c_trigt'] = np.tril(np.ones((128, 128), np.float32), -1)
    mn = np.zeros((32, 4, 8), np.float32)
    for k in range(32):
        for q in range(8):
            if (k % 8) <= q:
                mn[k, k // 8, q] = 1.0
    c['c_masknew'] = mn.reshape(32, 32)
    return c


class Builder:
    def __init__(self, NB, QG, NPG, NPOOL):
        self.NB, self.QG, self.NPG, self.NPOOL = NB, QG, NPG, NPOOL
        self.NQ = NB // 4
        self.OWN0 = NB - self.NQ - 1
        self.NOWN = self.NQ + 1
        self.TT = NB * 128
        self.NSQ = 4
        self.es = contextlib.ExitStack()
        self.nc = bass.Bass("TRN2", target_bir_lowering=False)
        self.S = Sched(self.nc, self.es)
        self.out_events = []
        self.cur = self.es

    def din(self, name, shape, dt=F32):
        return self.nc.dram_tensor(name, list(shape), dt, kind="ExternalInput").ap()

    def dout(self, name, shape, dt=F32):
        return self.nc.dram_tensor(name, list(shape), dt, kind="ExternalOutput").ap()

    def dscr(self, name, shape, dt):
        return self.nc.dram_tensor(name, list(shape), dt, kind="Internal").ap()

    def sb(self, name, shape, dt=F32):
        return self.cur.enter_context(self.nc.sbuf_tensor(name, list(shape), dt))

    def ring(self, name, n, shape, dt=F32):
        return Ring([self.sb(f"{name}{i}", shape, dt) for i in range(n)], name)

    def mm(self, out, lhsT, rhs, start, stop, reads, writes):
        nc = self.nc
        self.S.op('pe', lambda: nc.tensor.matmul(out, lhsT=lhsT, rhs=rhs, start=start, stop=stop), reads, writes)

    def tr(self, out, in_, ident, reads, writes):
        nc = self.nc
        self.S.op('pe', lambda: nc.tensor.transpose(out, in_, ident), reads, writes)

    def act(self, out, in_, func, reads, writes, bias=None, scale=None, accum=None):
        nc = self.nc
        kw = {}
        if bias is not None:
            kw['bias'] = bias
        if scale is not None:
            kw['scale'] = scale
        if accum is not None:
            kw['accum_out'] = accum
        self.S.op('act', lambda: nc.scalar.activation(out=out, in_=in_, func=func, **kw), reads, writes)

    def tt(self, eng, out, in0, in1, op, reads, writes):
        e = self.S.E[eng]
        self.S.op(eng, lambda: e.tensor_tensor(out=out, in0=in0, in1=in1, op=op), reads, writes)

    def ts(self, eng, out, in0, s1, s2, op0, op1, reads, writes):
        e = self.S.E[eng]
        if op1 is None:
            self.S.op(eng, lambda: e.tensor_scalar(out=out, in0=in0, scalar1=s1, scalar2=None, op0=op0), reads, writes)
        else:
            self.S.op(eng, lambda: e.tensor_scalar(out=out, in0=in0, scalar1=s1, scalar2=s2, op0=op0, op1=op1), reads, writes)

    def stt(self, eng, out, in0, scalar, in1, op0, op1, reads, writes):
        e = self.S.E[eng]
        self.S.op(eng, lambda: e.scalar_tensor_tensor(out=out, in0=in0, scalar=scalar, in1=in1, op0=op0, op1=op1), reads, writes)

    def cp(self, eng, out, in_, reads, writes):
        if eng == 'act':
            nc = self.nc
            self.S.op('act', lambda: nc.scalar.copy(out=out, in_=in_), reads, writes)
        else:
            e = self.S.E[eng]
            self.S.op(eng, lambda: e.tensor_copy(out=out, in_=in_), reads, writes)

    def dma(self, q, out, in_, reads, writes, slow=False):
        e = self.S.E[q]
        if slow:
            return self.S.dma(q, lambda: e.dma_start(out=out, in_=in_, allow_slow_non_contiguous=True), reads, writes)
        return self.S.dma(q, lambda: e.dma_start(out=out, in_=in_), reads, writes)

    def rsqrt_col(self, out, in_, key):
        nc = self.nc
        self.S.op('act', lambda: nc.scalar.sqrt(out=out, in_=in_), [key], [key])
        self.S.op('dve', lambda: nc.vector.reciprocal(out=out, in_=out), [key], [key])

    def psn(self):
        t, k = self.psring.next()
        return t, k

    @contextlib.contextmanager
    def phase(self):
        prev = self.cur
        with contextlib.ExitStack() as st:
            self.cur = st
            yield
            self.barrier()
        self.cur = prev

    def barrier(self):
        S = self.S
        for E in S.E:
            for Fe in S.E:
                if Fe != E and S.cnt[Fe]:
                    S.need(E, ('e', Fe, S.cnt[Fe]))
            for q in S.dcnt:
                for k in range(max(0, S.dcnt[q] - ND), S.dcnt[q]):
                    S.need(E, ('d', q, k))
        S.lastw.clear()
        S.readers.clear()

    def build(self):
        nc, S = self.nc, self.S
        NB, QG, NPG, NPOOL, TT, NSQ = self.NB, self.QG, self.NPG, self.NPOOL, self.TT, self.NSQ
        NOWN, OWN0, NQ = self.NOWN, self.OWN0, self.NQ
        TO = NOWN * 128
        self.cur = self.es
        STOP = int(os.environ.get("MK_STOP", "99"))

        xc = self.din("xc", [TT, D])
        kmask_d = self.din("kmask", [128, NB])
        xs = self.din("xs", [32, D])
        pt_d = self.din("pt", [1, NSQ * NPG], I32)
        cache_k = self.din("cache_k", [NPOOL * 128, AW])
        cache_v = self.din("cache_v", [NPOOL * 128, AW])
        cache_lf = self.din("cache_lf", [NPOOL * 128, AH])
        sgla_d = self.din("sgla", [NSQ * BH * 128, BV])
        sconv_d = self.din("sconv", [NSQ * 2, 2 * DFF])
        w_in = self.din("w_in", [D, DIN])
        w_a2 = self.din("w_alpha2", [GR, AW])
        b_al = self.din("b_alpha", [1, AW])
        b_f = self.din("b_f", [1, AH])
        g_pre_mix = self.din("g_pre_mix", [1, D])
        g_gla = self.din("g_gla", [1, BV])
        w_pa = self.din("w_proj_a", [AW, D])
        w_pb = self.din("w_proj_b", [D, D])
        w_out = self.din("w_out", [D, D])
        g_post_mix = self.din("g_post_mix", [1, D])
        g_pre_ffn = self.din("g_pre_ffn", [1, D])
        w_up = self.din("w_up", [D, 2 * DFF])
        w_conv = self.din("w_conv", [3, 2 * DFF])
        b_conv = self.din("b_conv", [1, 2 * DFF])
        w_down = self.din("w_down", [DFF, D])
        g_post_ffn = self.din("g_post_ffn", [1, D])
        cst = {k: self.din(k, v.shape, I32 if v.dtype == np.int32 else F32) for k, v in host_consts(QG).items()}

        o_y = self.dout("o_y", [NQ * 128, D])
        o_k = self.dout("o_k", [NQ * 128, AW])
        o_v = self.dout("o_v", [NQ * 128, AW])
        o_lf = self.dout("o_lf", [NQ * 128, AH])
        o_gla = self.dout("o_gla", [BH * 128, BV])
        o_conv = self.dout("o_conv", [2, 2 * DFF])
        o_ys = self.dout("o_ys", [32, D])
        o_ks = self.dout("o_ks", [32, AW])
        o_vs = self.dout("o_vs", [32, AW])
        o_lfs = self.dout("o_lfs", [32, AH])
        o_glas = self.dout("o_glas", [NSQ * BH * 128, BV])
        o_convs = self.dout("o_convs", [NSQ * 2, 2 * DFF])

        kt_scr = self.dscr("kt_scr", [128, 4, TT], BF16)
        vx_scr = self.dscr("vx_scr", [NB, 128, AH * 65], BF16)
        qt_scr = self.dscr("qt_scr", [128, 4, TO], BF16)
        obt_scr = self.dscr("obt_scr", [128, 8, TO], BF16)
        oat_scr = self.dscr("oat_scr", [128, 4, TO], BF16)
        xm_scr = self.dscr("xm_scr", [TO + 32, D], F32)
        wup_scr = self.dscr("wup_scr", [128, 8, 2 * DFF], BF16)

        ident = self.sb("ident", [128, 128], BF16)
        identf = self.sb("identf", [128, 128], F32)
        tri_f = self.sb("tri_f", [128, 128], F32)
        trigt_f = self.sb("trigt_f", [128, 128], F32)
        tri_b = self.sb("tri_b", [128, 128], BF16)
        trin_b = self.sb("trin_b", [128, 128], BF16)
        ones_f = self.sb("ones_f", [128, 128], F32)
        onesn_b = self.sb("onesn_b", [128, 1], BF16)
        stair = self.sb("stair", [128, QG, QG * 128], BF16)
        kmask = self.sb("kmask_t", [128, NB], F32)
        tris_f = self.sb("tris_f", [32, 32], F32)
        tris_b = self.sb("tris_b", [32, 32], BF16)
        trins_b = self.sb("trins_b", [32, 32], BF16)
        seqind_f = self.sb("seqind_f", [32, 4], F32)
        seqindn_b = self.sb("seqindn_b", [32, 4], BF16)
        colmask = self.sb("colmask", [128, 4, 32], BF16)
        masknew = self.sb("masknew", [32, 4, 8], BF16)
        iota_i = self.sb("iota_i", [128, 1], I32)
        Ccum = self.sb("Ccum", [128, NB, AH], F32)
        carry = self.sb("carry", [128, NB + 1, AH], F32)
        QTs = self.sb("QTs", [128, 4, 32], BF16)
        KTs = self.sb("KTs", [128, 4, 32], BF16)
        VXs = self.sb("VXs", [32, AH, 65], BF16)
        Csn = self.sb("Csn", [32, AH], F32)
        obTs = self.sb("obTs", [128, 8, 32], BF16)
        oaTs = self.sb("oaTs", [128, 4, 32], BF16)

        K0 = ('const',)
        cl = [
            ('pool', ident[:], cst['c_ident'][:, :]), ('sp', identf[:], cst['c_ident'][:, :]),
            ('sp', tri_f[:], cst['c_tri'][:, :]), ('pool', tri_b[:], cst['c_tri'][:, :]),
            ('sp', trigt_f[:], cst['c_trigt'][:, :]),
            ('pool', trin_b[:], cst['c_trin'][:, :]), ('sp', ones_f[:], cst['c_ones'][:, :]),
            ('pool', onesn_b[:], cst['c_trin'][0:128, 127:128]),
            ('sp', kmask[:], kmask_d[:, :]),
            ('sp', tris_f[:], cst['c_tris'][:, :]),
            ('pool', tris_b[:], cst['c_tris'][:, :]), ('pool', trins_b[:], cst['c_trins'][:, :]),
            ('sp', seqind_f[:], cst['c_seqind'][:, :]), ('pool', seqindn_b[:], cst['c_seqindn'][:, :]),
            ('sp', iota_i[:], cst['c_iota'][:, :]),
            ('pool', masknew[:], cst['c_masknew'][:, :].rearrange("k (s q) -> k s q", s=4)),
        ]
        for q, o, i in cl:
            self.dma(q, o, i, [], [K0], slow=True)
        for d in range(QG):
            self.dma('pool', stair[:, d, :], cst['c_stair'][d * 128:(d + 1) * 128, :], [], [K0])
        for s in range(4):
            self.dma('pool', colmask[:, s, :], cst['c_colmask'][s * 128:(s + 1) * 128, :], [], [K0])
        S.op('pool', lambda: nc.gpsimd.memset(carry[:, 0, :], 0.0), [], [('carry', 0)])
        S.op('pool', lambda: nc.gpsimd.memset(VXs[:], 1.0), [], [('VXs',)])
        for k in range(8):
            self.dma('pool', wup_scr[:, k, :], w_up[k * 128:(k + 1) * 128, :], [], [('wup_scr',)])

        banks = [self.es.enter_context(nc.psum_tensor(f"ps{i}", [128, 512], F32)) for i in range(8)]
        self.psring = Ring(banks, "ps")

        def bview(ps):
            return ps[:].bitcast(BF16)

        def gload(name, g_ap, cols=D):
            t = self.sb(name, [128, cols], F32)
            self.dma('sp', t[:], g_ap.partition_broadcast(128), [], [K0], slow=True)
            return t

        def wload(name, src, r0, rows, c0, cols):
            kc = rows // 128
            t = self.sb(name, [128, kc, cols], BF16)
            v = src[r0:r0 + rows, c0:c0 + cols].rearrange("(k p) c -> p k c", p=128)
            for k in range(kc):
                self.dma('pool', t[:, k, :], v[:, k, :], [], [K0])
            return t

        def norm_T(xt, xk, P, gt, dst, dkey, R):
            junk, jk = R['junk'].next()
            ss, sk = R['ss'].next()
            self.act(junk[0:P, :], xt, AF.Square, [xk], [jk, sk], accum=ss[0:P, 0:1])
            self.ts('dve', ss[0:P, 1:2], ss[0:P, 0:1], 1.0 / D, EPS, ALU.mult, ALU.add, [sk], [sk])
            self.rsqrt_col(ss[0:P, 2:3], ss[0:P, 1:2], sk)
            hn, hk = R['hn'].next()
            self.stt('dve', hn[0:P, :], xt, ss[0:P, 2:3], gt[0:P, :], ALU.mult, ALU.mult, [xk, sk, K0], [hk])
            ps, pk = self.psn()
            pv = bview(ps)
            for kc in range(8):
                self.tr(pv[:, kc * P:(kc + 1) * P], hn[0:P, kc * 128:(kc + 1) * 128], ident[0:P, 0:P], [hk, K0], [pk])
            self.cp('act', dst, pv[:, 0:8 * P].rearrange("p (k t) -> p k t", k=8), [pk], [dkey])

        def proj_tok(hnT, tk, P, W, c0, cols, ps, pk, pcol=0, nk=8):
            for kc in range(nk):
                self.mm(ps[0:P, pcol:pcol + cols], hnT[:, kc, 0:P], W[:, kc, c0:c0 + cols], kc == 0, kc == nk - 1, [tk, K0], [pk])

        def proj_feat(hnT, tk, P, W, c0, M, ps, pk, pcol=0):
            for kc in range(8):
                self.mm(ps[0:M, pcol:pcol + P], W[:, kc, c0:c0 + M], hnT[:, kc, 0:P], kc == 0, kc == 7, [tk, K0], [pk])

        def softplus_neg(dst, dk, src, sk_, tmp, tmk):
            self.act(tmp, src, AF.Exp, [sk_], [tmk], scale=-1.0)
            self.act(dst, tmp, AF.Ln, [tmk], [dk], bias=1.0)

        def transpose_into(dst, dkey, src, sk_, P, nfc):
            for g0 in range(0, nfc, 8):
                g = min(8, nfc - g0)
                ps, pk = self.psn()
                pv = bview(ps)
                for i in range(g):
                    fc = g0 + i
                    self.tr(pv[:, i * P:(i + 1) * P], src[0:P, fc * 128:(fc + 1) * 128], ident[0:P, 0:P], [sk_, K0], [pk])
                self.cp('act', dst[:, g0:g0 + g, :], pv[:, 0:g * P].rearrange("p (k t) -> p k t", k=g), [pk], [dkey])

        with self.phase():
            gpre = gload("gpre", g_pre_mix)
            ggl = gload("ggl", g_gla, BV)
            bft = gload("bft", b_f, AH)
            balt = gload("balt", b_al, AW)
            W_kv = wload("W_kv", w_in, 0, D, O_KA, 1024)
            W_f = wload("W_f", w_in, 0, D, O_F, 8)
            W_kb = wload("W_kb", w_in, 0, D, O_KB, 512)
            W_vb = wload("W_vb", w_in, 0, D, O_VB, 1024)
            W_al = wload("W_al", w_in, 0, D, O_AL, 16)
            W_qa = wload("W_qa", w_in, 0, D, O_QA, 512)
            W_qb = wload("W_qb", w_in, 0, D, O_QB, 512)
            W_rb = wload("W_rb", w_in, 0, D, O_RB, 1024)
            W_a2 = self.sb("W_a2", [GR, AW], BF16)
            self.dma('pool', W_a2[:], w_a2[:, :], [], [K0])
            Sst = [self.sb(f"Sst{h}", [128, BV], F32) for h in range(BH)]
            Sbf = [self.sb(f"Sbf{h}", [128, BV], BF16) for h in range(BH)]
            for h in range(BH):
                S.op('pool', lambda h=h: nc.gpsimd.memset(Sst[h][:], 0.0), [], [('S', h)])
            R = {
                'x': self.ring("xt", 2, [128, D], F32), 'junk': self.ring("junk", 1, [128, D], BF16),
                'ss': self.ring("ss", 4, [128, 4], F32), 'hn': self.ring("hn", 2, [128, D], BF16),
                'hnT': self.ring("hnT", 2, [128, 8, 128], BF16), 'ktb': self.ring("ktb", 2, [128, 4, 128], BF16),
                'vx': self.ring("vx", 2, [128, AH, 65], BF16), 'o512': self.ring("o512", 2, [128, 512], F32),
                'sm': self.ring("sm", 4, [128, 8], F32), 'alT': self.ring("alT", 2, [GR, 128], BF16),
                'a2s': self.ring("a2s", 2, [128, 512], F32), 'lsp': self.ring("lsp", 2, [128, 512], BF16),
                'ekd': self.ring("ekd", 2, [128, 512], F32), 'kd': self.ring("kd", 2, [128, 512], BF16),
                'vb': self.ring("vb", 2, [128, 1024], BF16), 'dl': self.ring("dl", 2, [128, 16], F32),
                'u2': self.ring("u2", 2, [128, BV], F32), 'qd': self.ring("qd", 2, [128, 512], BF16),
                'xT': self.ring("xT", 2, [128, 4, 128], BF16), 'AT': self.ring("AT", 2, [128, 4, 128], BF16),
                'f1k': self.ring("f1k", 2, [128, 1024], F32), 'b1k': self.ring("b1k", 2, [128, 1024], BF16),
                'qtb': self.ring("qtb", 2, [128, 4, 128], BF16), 'obTb': self.ring("obTb", 2, [128, 8, 128], BF16),
            }
            for t in R['vx'].tiles:
                S.op('pool', lambda t=t: nc.gpsimd.memset(t[:], 1.0), [], [('vx', R['vx'].tiles.index(t))])

            def mixer_block(hnT, tk, P, n, own, sample=False, outs=None, qa_sink=None):
                res = {}
                ps, pk = self.psn()
                for fc in range(4):
                    proj_feat(hnT, tk, P, W_kv, fc * 128, 128, ps, pk, pcol=fc * P)
                if sample:
                    self.cp('act', KTs[:, :, :], ps[:, 0:4 * P].rearrange("p (f t) -> p f t", f=4), [pk], [('KTs',)])
                else:
                    ktb, kk = R['ktb'].next()
                    self.cp('act', ktb[:], ps[:, :].rearrange("p (f t) -> p f t", f=4), [pk], [kk])
                    self.dma('sp', kt_scr[:, :, n * 128:(n + 1) * 128], ktb[:], [kk], [('kt_scr',)])
                ps, pk = self.psn()
                proj_tok(hnT, tk, P, W_kv, 512, 512, ps, pk)
                if sample:
                    self.cp('dve', VXs[0:P, :, 0:64], ps[0:P, :].rearrange("p (h d) -> p h d", h=AH), [pk], [('VXs',)])
                else:
                    vx, vk = R['vx'].next()
                    self.cp('dve', vx[0:P, :, 0:64], ps[0:P, :].rearrange("p (h d) -> p h d", h=AH), [pk], [vk])
                    self.dma('sp', vx_scr[n, :, :], vx[:].rearrange("p h d -> p (h d)"), [vk], [('vx_scr',)])
                if outs is not None:
                    o5, ok_ = R['o512'].next()
                    self.cp('act', o5[0:P, :], ps[0:P, :], [pk], [ok_])
                    self.out_events.append(self.dma('sp', outs['v'], o5[0:P, :], [ok_], []))
                    ps2, pk2 = self.psn()
                    proj_tok(hnT, tk, P, W_kv, 0, 512, ps2, pk2)
                    o5, ok_ = R['o512'].next()
                    self.cp('act', o5[0:P, :], ps2[0:P, :], [pk2], [ok_])
                    self.out_events.append(self.dma('sp', outs['k'], o5[0:P, :], [ok_], []))
                ps, pk = self.psn()
                proj_tok(hnT, tk, P, W_f, 0, 8, ps, pk)
                sm, smk = R['sm'].next()
                self.tt('dve', sm[0:P, :], ps[0:P, 0:8], bft[0:P, :], ALU.add, [pk, K0], [smk])
                sm2, smk2 = R['sm'].next()
                sp_t, spk = R['sm'].next()
                softplus_neg(sp_t[0:P, :], spk, sm[0:P, :], smk, sm2[0:P, :], smk2)
                if outs is not None:
                    lf, lfk = R['sm'].next()
                    self.ts('dve', lf[0:P, :], sp_t[0:P, :], -1.0, None, ALU.mult, None, [spk], [lfk])
                    self.out_events.append(self.dma('sp', outs['lf'], lf[0:P, :], [lfk], []))
                ps, pk = self.psn()
                proj_feat(hnT, tk, P, W_al, 0, GR, ps, pk)
                alT, ak = R['alT'].next()
                self.cp('act', alT[:, 0:P], ps[0:GR, 0:P], [pk], [ak])
                ps, pk = self.psn()
                self.mm(ps[0:P, :], alT[:, 0:P], W_a2[:, :], True, True, [ak, K0], [pk])
                a2s, a2k = R['a2s'].next()
                self.tt('dve', a2s[0:P, :], ps[0:P, :], balt[0:P, :], ALU.add, [pk, K0], [a2k])
                ekd, ek = R['ekd'].next()
                lsp, lk = R['lsp'].next()
                softplus_neg(lsp[0:P, :], lk, a2s[0:P, :], a2k, ekd[0:P, :], ek)
                vb, vbk = R['vb'].next()
                for half in range(2):
                    ps, pk = self.psn()
                    proj_tok(hnT, tk, P, W_vb, half * 512, 512, ps, pk)
                    self.cp('act' if half else 'dve', vb[0:P, half * 512:(half + 1) * 512], ps[0:P, :], [pk], [vbk])
                if own:
                    sr, srk = R['f1k'].next()
                    for half in range(2):
                        ps, pk = self.psn()
                        proj_tok(hnT, tk, P, W_rb, half * 512, 512, ps, pk)
                        self.act(sr[0:P, half * 512:(half + 1) * 512], ps[0:P, :], AF.Silu, [pk], [srk])
                    res['sr'] = (sr, srk)
                    ps, pk = self.psn()
                    for fc in range(4):
                        proj_feat(hnT, tk, P, W_qa, fc * 128, 128, ps, pk, pcol=fc * P)
                    qa_sink(ps, pk)
                ps, pk = self.psn()
                if sample:
                    self.mm(ps[0:P, 0:8], tris_f[0:P, 0:P], sp_t[0:P, :], True, True, [spk, K0], [pk])
                    self.cp('dve', Csn[:, :], ps[0:P, 0:8], [pk], [('Csn',)])
                else:
                    self.mm(ps[:, 0:8], tri_f[:], sp_t[:, :], True, True, [spk, K0], [pk])
                    self.mm(ps[:, 8:16], ones_f[:], sp_t[:, :], True, True, [spk, K0], [pk])
                    self.tt('dve', Ccum[:, n, :], ps[:, 0:8], carry[:, n, :], ALU.add, [pk, ('carry', n)], [('Ccum', n)])
                    self.tt('dve', carry[:, n + 1, :], ps[:, 8:16], carry[:, n, :], ALU.add, [pk, ('carry', n)], [('carry', n + 1)])
                psb, pbk = self.psn()
                tn = trins_b if sample else trin_b
                self.mm(psb[0:P, :], tn[0:P, 0:P], lsp[0:P, :], True, True, [lk, K0], [pbk])
                ekd, ek = R['ekd'].next()
                self.act(ekd[0:P, :], psb[0:P, :], AF.Exp, [pbk], [ek], scale=-1.0)
                ps, pk = self.psn()
                proj_tok(hnT, tk, P, W_kb, 0, 512, ps, pk)
                kd, kdk = R['kd'].next()
                self.tt('dve', kd[0:P, :], ps[0:P, :], ekd[0:P, :], ALU.mult, [pk, ek], [kdk])
                res.update(lsp=(lsp, lk), kd=(kd, kdk), vb=(vb, vbk))
                if own:
                    eq, eqk = R['a2s'].next()
                    self.act(eq[0:P, :], psb[0:P, :], AF.Exp, [pbk], [eqk])
                    ps, pk = self.psn()
                    proj_tok(hnT, tk, P, W_qb, 0, 512, ps, pk)
                    qd, qk = R['qd'].next()
                    self.stt('dve', qd[0:P, :], ps[0:P, :], BK ** -0.5, eq[0:P, :], ALU.mult, ALU.mult, [pk, eqk], [qk])
                    tiles = []
                    for src, sk_ in ((qd, qk), (kd, kdk)):
                        xT, xk_ = R['xT'].next()
                        transpose_into(xT[:, :, 0:P], xk_, src, sk_, P, 4)
                        tiles.append((xT, xk_))
                    (qdT, qtk), (kdT, ktk) = tiles
                    ps, pk = self.psn()
                    for h in range(BH):
                        self.mm(ps[0:P, h * P:(h + 1) * P], kdT[:, h, 0:P], qdT[:, h, 0:P], True, True, [ktk, qtk], [pk])
                    AT, atk = R['AT'].next()
                    msk = tris_b if sample else tri_b
                    self.tt('dve', AT[0:P, :, 0:P], ps[0:P, 0:4 * P].rearrange("p (h t) -> p h t", h=4),
                            msk[0:P, 0:P].unsqueeze(1).to_broadcast([P, 4, P]), ALU.mult, [pk, K0], [atk])
                    res.update(qdT=(qdT, qtk), AT=(AT, atk))
                return res

            def gla_out(res, P, S_list):
                AT, atk = res['AT']
                vb, vbk = res['vb']
                sr, srk = res['sr']
                ob, obk = R['f1k'].next()
                ss, sk = R['ss'].next()
                junk, jk = R['junk'].next()
                pss = []
                for h in range(BH):
                    if h % 2 == 0:
                        ps, pk = self.psn()
                        pss.append((ps, pk))
                    c0 = (h % 2) * BV
                    self.mm(ps[0:P, c0:c0 + BV], AT[0:P, h, 0:P], vb[0:P, h * BV:(h + 1) * BV], True, False, [atk, vbk], [pk])
                    terms = S_list(h)
                    for i, (lq, lqk, sbf, sbk) in enumerate(terms):
                        self.mm(ps[0:P, c0:c0 + BV], lq, sbf, False, i == len(terms) - 1, [lqk, sbk], [pk])
                    self.act(junk[0:P, 0:BV], ps[0:P, c0:c0 + BV], AF.Square, [pk], [jk, sk], accum=ss[0:P, h:h + 1])
                sq, sqk = R['ss'].next()
                self.ts('dve', sq[0:P, :], ss[0:P, :], 1.0 / BV, EPS, ALU.mult, ALU.add, [sk], [sqk])
                self.rsqrt_col(sq[0:P, :], sq[0:P, :], sqk)
                for h in range(BH):
                    ps, pk = pss[h // 2]
                    c0 = (h % 2) * BV
                    self.stt('dve', ob[0:P, h * BV:(h + 1) * BV], ps[0:P, c0:c0 + BV], sq[0:P, h:h + 1], ggl[0:P, :],
                             ALU.mult, ALU.mult, [pk, sqk, K0], [obk])
                ob2, o2k = R['b1k'].next()
                self.tt('dve', ob2[0:P, :], ob[0:P, :], sr[0:P, :], ALU.mult, [obk, srk], [o2k])
                return ob2, o2k

            def state_update(res, P, h, lhs_kd, lkk, dlcol, dlk, S_in, S_in_k, S_out, S_out_k):
                vb, vbk = res['vb']
                ps, pk = self.psn()
                self.mm(ps[:, 0:BV], lhs_kd, vb[0:P, h * BV:(h + 1) * BV], True, True, [lkk, vbk], [pk])
                u2, uk = R['u2'].next()
                self.act(u2[:], ps[:, 0:BV], AF.Copy, [pk, dlk], [uk], scale=dlcol)
                self.stt('dve', S_out, S_in, dlcol, u2[:], ALU.mult, ALU.add, [S_in_k, dlk, uk], [S_out_k])

            for n in range(NB):
                own = n >= OWN0
                xt, xk = R['x'].next()
                self.dma('sp', xt[:, :], xc[n * 128:(n + 1) * 128, :], [], [xk])
                hnT, tk = R['hnT'].next()
                norm_T(xt[:, :], xk, 128, gpre, hnT[:, :, :], tk, R)
                outs = None
                if n > OWN0:
                    r0 = (n - OWN0 - 1) * 128
                    outs = {'k': o_k[r0:r0 + 128, :], 'v': o_v[r0:r0 + 128, :], 'lf': o_lf[r0:r0 + 128, :]}
                def qa_sink(qps, qpk, n=n):
                    qtb, qbk = R['qtb'].next()
                    self.ts('dve', qtb[:], qps[:, :].rearrange("p (f t) -> p f t", f=4), AD ** -0.5, None,
                            ALU.mult, None, [qpk], [qbk])
                    c0_ = (n - OWN0) * 128
                    self.dma('sp', qt_scr[:, :, c0_:c0_ + 128], qtb[:], [qbk], [('qt_scr',)])
                res = mixer_block(hnT, tk, 128, n, own, outs=outs, qa_sink=qa_sink)
                lsp, lk = res['lsp']
                kd, kdk = res['kd']
                ps, pk = self.psn()
                for h in range(BH):
                    self.mm(ps[:, h:h + 1], lsp[:, h * 128:(h + 1) * 128], onesn_b[:, 0:1], True, True, [lk, K0], [pk])
                dl, dlk = R['dl'].next()
                self.act(dl[:, 0:4], ps[:, 0:4], AF.Exp, [pk], [dlk])
                if own:
                    col0 = (n - OWN0) * 128
                    for h in range(BH):
                        self.cp('pool', Sbf[h][:], Sst[h][:], [('S', h)], [('Sbf', h)])
                    qdT, qtk = res['qdT']
                    ob2, o2k = gla_out(res, 128, lambda h: [(qdT[:, h, :], qtk, Sbf[h][:], ('Sbf', h))])
                    obTb, obk_ = R['obTb'].next()
                    transpose_into(obTb[:, :, :], obk_, ob2, o2k, 128, 8)
                    self.dma('sp', obt_scr[:, :, col0:col0 + 128], obTb[:], [obk_], [('obt_scr',)])
                for h in range(BH):
                    state_update(res, 128, h, kd[:, h * 128:(h + 1) * 128], kdk, dl[:, h:h + 1], dlk,
                                 Sst[h][:], ('S', h), Sst[h][:], ('S', h))
            for h in range(BH):
                self.out_events.append(self.dma('sp', o_gla[h * 128:(h + 1) * 128, :], Sst[h][:], [('S', h)], []))

            if STOP >= 2:
                S0 = self.sb("S0", [128, NSQ * BH, BV], F32)
                S0b = self.sb("S0b", [128, NSQ * BH, BV], BF16)
                self.dma('sp', S0[:], sgla_d.rearrange("(g p) v -> p g v", p=128), [], [('S0',)])
                self.cp('pool', S0b[:], S0[:], [('S0',)], [('S0b',)])
                xt, xk = R['x'].next()
                self.dma('sp', xt[0:32, :], xs[:, :], [], [xk])
                hnT, tk = R['hnT'].next()
                norm_T(xt[0:32, :], xk, 32, gpre, hnT[:, :, 0:32], tk, R)
                outs = {'k': o_ks[:, :], 'v': o_vs[:, :], 'lf': o_lfs[:, :]}
                def qa_sink_s(qps, qpk):
                    self.ts('dve', QTs[:], qps[:, 0:128].rearrange("p (f t) -> p f t", f=4), AD ** -0.5, None,
                            ALU.mult, None, [qpk], [('QTs',)])
                res = mixer_block(hnT, tk, 32, None, True, sample=True, outs=outs, qa_sink=qa_sink_s)
                qdT, qtk = res['qdT']
                qdTm = self.sb("qdTm", [128, NSQ, 4, 32], BF16)
                for s in range(NSQ):
                    self.tt('dve', qdTm[:, s, :, :], qdT[:, :, 0:32], colmask[:, s, :].unsqueeze(1).to_broadcast([128, 4, 32]),
                            ALU.mult, [qtk, K0], [('qdTm',)])
                ob2, o2k = gla_out(res, 32, lambda h: [(qdTm[:, s, h, :], ('qdTm',), S0b[:, s * BH + h, :], ('S0b',)) for s in range(NSQ)])
                transpose_into(obTs[:, :, :], ('obTs',), ob2, o2k, 32, 8)
                lsp, lk = res['lsp']
                kd, kdk = res['kd']
                ps, pk = self.psn()
                for h in range(BH):
                    self.mm(ps[:, h * 4:(h + 1) * 4], lsp[0:32, h * 128:(h + 1) * 128], seqindn_b[0:32, 0:4], True, True, [lk, K0], [pk])
                dl, dlk = R['dl'].next()
                self.act(dl[:, 0:16], ps[:, 0:16], AF.Exp, [pk], [dlk])
                kdm = self.sb("kdm", [32, NSQ, 512], BF16)
                for s in range(NSQ):
                    self.ts('dve', kdm[:, s, :], kd[0:32, :], seqind_f[:, s:s + 1], None, ALU.mult, None, [kdk, K0], [('kdm',)])
                Sn = self.ring("Sn", 2, [128, BV], F32)
                for s in range(NSQ):
                    for h in range(BH):
                        sn, snk = Sn.next()
                        g = s * BH + h
                        state_update(res, 32, h, kdm[:, s, h * 128:(h + 1) * 128], ('kdm',), dl[:, h * 4 + s:h * 4 + s + 1], dlk,
                                     S0[:, g, :], ('S0',), sn[:], snk)
                        self.out_events.append(self.dma('sp', o_glas[g * 128:(g + 1) * 128, :], sn[:], [snk], []))

        if STOP <= 2:
            return self.finish_all()

        groups = [(OWN0, 1)] + [(OWN0 + 1 + g * QG, QG) for g in range(NQ // QG)]
        NG = len(groups)
        self.psring = Ring(banks[0:4], "ps")
        with self.phase():
            biasAll = self.sb("biasAll", [128, NG, NB, AH], F32)
            for gi, (n0, nq) in enumerate(groups):
                nk = n0 + nq
                self.tt('dve', biasAll[:, gi, 0:nk, :], Ccum[:, 0:nk, :], carry[:, n0, :].unsqueeze(1).to_broadcast([128, nk, AH]),
                        ALU.subtract, [], [('bias', gi)])
                self.tt('dve', biasAll[:, gi, 0:nk, :], biasAll[:, gi, 0:nk, :], kmask[:, 0:nk].unsqueeze(2).to_broadcast([128, nk, AH]),
                        ALU.add, [('bias', gi)], [('bias', gi)])
            R_kt = self.ring("KT", 2, [128, TT], BF16)
            R_vxh = self.ring("VXh", 2, [128, NB, 2, 128], BF16)
            for t in R_vxh.tiles:
                ti = R_vxh.tiles.index(t)
                S.op('pool', lambda t=t: nc.gpsimd.memset(t[:], 0.0), [], [('VXh', ti)])
                S.op('pool', lambda t=t: nc.gpsimd.memset(t[:, :, :, 64:65], 1.0), [('VXh', ti)], [('VXh', ti)])
            ones_b = self.sb("ones_b", [65, 64], BF16)
            S.op('pool', lambda: nc.gpsimd.memset(ones_b[:], 1.0), [], [('ones_b',)])
            R_qth = self.ring("QTh", 2, [128, 2, TO], BF16)
            for t in R_qth.tiles:
                S.op('pool', lambda t=t: nc.gpsimd.memset(t[:], 0.0), [], [('QTh', R_qth.tiles.index(t))])
            R_pt = self.ring("pt", 8, [128, QG * 128], BF16)
            R_rec = self.ring("rec", 2, [65, 2 * QG * 128], BF16)
            R_recf = self.ring("recf", 2, [65, QG * 128], F32)
            R_rb = self.ring("rbc", 2, [64, QG * 128], F32)
            R_oT = self.ring("oT", 3, [64, QG * 128], BF16)
            accs = [(banks[4], ('ps', 4)), (banks[5], ('ps', 5))]
            LOOK = 4

            def prompt_gen():
                acci = 0
                for hp in range(4):
                    KT, ktk = R_kt.next()
                    VX, vxk = R_vxh.next()
                    QTh, qhk = R_qth.next()
                    for c in range(0, TT, 2048):
                        self.dma('sp', KT[:, c:c + 2048], kt_scr[:, hp, c:c + 2048], [], [ktk])
                    for c in range(0, NB, 8):
                        for hh in range(2):
                            a = (2 * hp + hh) * 65
                            self.dma('sp', VX[:, c:c + 8, hh, 0:64], vx_scr[c:c + 8, :, a:a + 64].rearrange("n p c -> p n c"), [], [vxk])
                    for hh in range(2):
                        self.dma('sp', QTh[hh * 64:(hh + 1) * 64, hh, :], qt_scr[hh * 64:(hh + 1) * 64, hp, :], [], [qhk])
                    units = []
                    for gi, (n0, nq) in enumerate(groups):
                        for hh in range(2):
                            for n in range(n0 + nq):
                                units.append((gi, n0, nq, hh, n))
                    pend = {}
                    accof = {}
                    evs = {}

                    def emit_qk(u):
                        gi, n0, nq, hh, n = units[u]
                        W = nq * 128
                        c0 = (n0 - OWN0) * 128
                        h = 2 * hp + hh
                        prs = slice(hh * 64, (hh + 1) * 64)
                        lo = max(n - n0, 0) * 128
                        psS, psk = self.psn()
                        self.mm(psS[:, lo:W], KT[:, n * 128:(n + 1) * 128], QTh[:, hh, c0 + lo:c0 + W], True, True, [ktk, qhk], [psk])
                        pt, ptk = R_pt.next()
                        self.act(pt[:, lo:W], psS[:, lo:W], AF.Exp, [psk, ('bias', gi)], [ptk], bias=biasAll[:, gi, n, h:h + 1])
                        if n >= n0:
                            self.tt('dve', pt[:, lo:lo + 128], pt[:, lo:lo + 128], tri_b[:, :], ALU.mult, [ptk, K0], [ptk])
                        pend[u] = (pt, ptk)
                        evs[u] = [('e', 'act', S.cnt['act']), ('e', 'dve', S.cnt['dve']) if n >= n0 else None]

                    def emit_pv(u):
                        nonlocal acci
                        gi, n0, nq, hh, n = units[u]
                        W = nq * 128
                        c0 = (n0 - OWN0) * 128
                        lo = max(n - n0, 0) * 128
                        if n == 0:
                            accof[(gi, hh)] = accs[acci % 2]
                            acci += 1
                        psO, pok = accof[(gi, hh)]
                        pt, ptk = pend.pop(u)
                        last = (n == n0 + nq - 1)
                        self.mm(psO[:, lo:W], VX[:, n, hh, :], pt[:, lo:W], n == 0, last, [ptk, vxk], [pok])
                        if last:
                            rf, rfk = R_recf.next()
                            self.ts('dve', rf[64:65, 0:W], psO[64:65, 0:W], 1e-30, None, ALU.max, None, [pok], [rfk])
                            S.op('dve', lambda rf=rf, W=W: nc.vector.reciprocal(out=rf[64:65, 0:W], in_=rf[64:65, 0:W]), [rfk], [rfk])
                            rec, rk = R_rec.next()
                            self.cp('dve', rec[64:65, 0:W], rf[64:65, 0:W], [rfk], [rk])
                            self.tt('dve', rec[64:65, W:2 * W], rf[64:65, 0:W], rec[64:65, 0:W], ALU.subtract, [rfk, rk], [rk])
                            psB, pbk = self.psn()
                            self.mm(psB[0:64, 0:W], ones_b[64:65, 0:64], rec[64:65, 0:W], True, False, [rk, ('ones_b',)], [pbk])
                            self.mm(psB[0:64, 0:W], ones_b[64:65, 0:64], rec[64:65, W:2 * W], False, True, [rk, ('ones_b',)], [pbk])
                            rb, rbk = R_rb.next()
                            self.cp('act', rb[0:64, 0:W], psB[0:64, 0:W], [pbk], [rbk])
                            oT, otk = R_oT.next()
                            self.tt('dve', oT[:, 0:W], psO[0:64, 0:W], rb[0:64, 0:W], ALU.mult, [pok, rbk], [otk])
                            self.dma('sp', oat_scr[hh * 64:(hh + 1) * 64, hp, c0:c0 + W], oT[:, 0:W], [otk], [('oat_scr',)])

                    NU = len(units)
                    for i in range(0, NU + LOOK + 1, 2):
                        hi_q = min(i + 1, NU - 1)
                        if i < NU and hi_q - 4 >= 0 and (hi_q - 4) in evs:
                            S.need('pe', evs[hi_q - 4][0])
                        for u in (i, i + 1):
                            if u < NU:
                                emit_qk(u)
                        v0 = i - LOOK - 1
                        hi_v = min(v0 + 1, NU - 1)
                        if hi_v >= 0 and hi_v in evs:
                            for ev in evs[hi_v]:
                                if ev is not None:
                                    S.need('pe', ev)
                        for v in (v0, v0 + 1):
                            if 0 <= v < NU:
                                emit_pv(v)
                        yield

            n_prompt_steps = 4 * ((sum(2 * (n0 + nq) for (n0, nq) in groups) + LOOK + 2) // 2)

            def sample_gen():
                ptb = self.sb("ptb", [128, NSQ * NPG], I32)
                idx = self.sb("idx", [128, NSQ * NPG], I32)
                self.dma('sp', ptb[:], pt_d.partition_broadcast(128), [], [('ptb',)], slow=True)
                ptf = self.sb("ptf", [128, NSQ * NPG], F32)
                iof = self.sb("iof", [128, 1], F32)
                self.cp('dve', ptf[:], ptb[:], [('ptb',)], [('ptf',)])
                self.cp('dve', iof[:], iota_i[:], [K0], [('iof',)])
                self.stt('dve', ptf[:], ptf[:], 128.0, iof[:, 0:1].to_broadcast([128, NSQ * NPG]), ALU.mult, ALU.add,
                         [('ptf',), ('iof',)], [('ptf',)])
                self.cp('dve', idx[:], ptf[:], [('ptf',)], [('idx',)])
                qblk = self.sb("qblk", [128, NSQ, 4, 16], BF16)
                S.op('pool', lambda: nc.gpsimd.memset(qblk[:], 0.0), [], [('qblk',)])
                for s in range(NSQ):
                    for hh in range(2):
                        self.cp('dve', qblk[hh * 64:(hh + 1) * 64, s, :, hh * 8:(hh + 1) * 8], QTs[hh * 64:(hh + 1) * 64, :, s * 8:(s + 1) * 8],
                                [('QTs',), ('qblk',)], [('qblk',)])
                R_kr = self.ring("kraw", 3, [128, AW], F32)
                R_vr = self.ring("vraw", 3, [128, AW], F32)
                R_lr = self.ring("lraw", 4, [128, AH], F32)
                R_ktp = self.ring("ktp", 2, [128, 4, 128], BF16)
                R_vxp = self.ring("vxp", 2, [128, AH, 65], BF16)
                for t in R_vxp.tiles:
                    S.op('pool', lambda t=t: nc.gpsimd.memset(t[:], 1.0), [], [('vxp', R_vxp.tiles.index(t))])
                R_sfx = self.ring("sfx", 3, [128, AH], F32)
                R_cs = self.ring("cs", 3, [128, AH], F32)
                R_e = self.ring("e64", 3, [128, AH, 8], F32)
                R_p64 = self.ring("p64", 3, [128, 64], BF16)
                R_os = self.ring("os", 2, [8, AH, 64], BF16)
                R_rs = self.ring("rs", 2, [8, AH], F32)
                accs2 = [(banks[6], ('ps', 6)), (banks[7], ('ps', 7))]
                yield
                order = [(s, pg) for s in range(NSQ) for pg in range(NPG - 1, -1, -1)]
                raw = {}

                def gather(i):
                    s, pg = order[i]
                    col = s * NPG + pg
                    kr, krk = R_kr.next()
                    vr, vrk = R_vr.next()
                    lr, lrk = R_lr.next()
                    for (dst, dk_, src) in ((lr, lrk, cache_lf), (kr, krk, cache_k), (vr, vrk, cache_v)):
                        S.dma('pool', lambda dst=dst, src=src, col=col: nc.gpsimd.indirect_dma_start(
                            out=dst[:, :], out_offset=None, in_=src[:, :],
                            in_offset=bass.IndirectOffsetOnAxis(ap=idx[:, col:col + 1], axis=0)), [('idx',)], [dk_])
                    raw[i] = (kr, krk, vr, vrk, lr, lrk)

                gather(0)
                cs = csk = None
                for i, (s, pg) in enumerate(order):
                    if i + 1 < len(order):
                        gather(i + 1)
                    kr, krk, vr, vrk, lr, lrk = raw.pop(i)
                    first = (pg == NPG - 1)
                    if first:
                        cs, csk = R_cs.next()
                        S.op('dve', lambda cs=cs: nc.vector.memset(cs[:], 0.0), [], [csk])
                    ps, pk = self.psn()
                    self.mm(ps[:, 0:8], trigt_f[:], lr[:, :], True, True, [lrk, K0], [pk])
                    self.mm(ps[:, 8:16], ones_f[:], lr[:, :], True, True, [lrk, K0], [pk])
                    sfx, sxk = R_sfx.next()
                    self.tt('dve', sfx[:], ps[:, 0:8], cs[:], ALU.add, [pk, csk], [sxk])
                    cs2, csk2 = R_cs.next()
                    self.tt('dve', cs2[:], ps[:, 8:16], cs[:], ALU.add, [pk, csk], [csk2])
                    cs, csk = cs2, csk2
                    ps, pk = self.psn()
                    for hp in range(4):
                        self.tr(ps[:, hp * 128:(hp + 1) * 128], kr[:, hp * 128:(hp + 1) * 128], identf[:, :], [krk, K0], [pk])
                    ktp, kpk = R_ktp.next()
                    self.cp('act', ktp[:], ps[:, :].rearrange("p (f t) -> p f t", f=4), [pk], [kpk])
                    psS, psk = self.psn()
                    for hp in range(4):
                        self.mm(psS[:, hp * 16:(hp + 1) * 16], ktp[:, hp, :], qblk[:, s, hp, :], True, True, [kpk, ('qblk',)], [psk])
                    e, ek_ = R_e.next()
                    self.tt('dve', e[:], psS[:, 0:64].rearrange("p (h q) -> p h q", h=AH), sfx[:].unsqueeze(2).to_broadcast([128, AH, 8]),
                            ALU.add, [psk, sxk], [ek_])
                    p64, p6k = R_p64.next()
                    self.act(p64[:], e[:].rearrange("p h q -> p (h q)"), AF.Exp, [ek_], [p6k])
                    vxp, vpk = R_vxp.next()
                    self.cp('dve', vxp[:, :, 0:64], vr[:, :].rearrange("p (h d) -> p h d", h=AH), [vrk], [vpk])
                    for h in range(AH):
                        psO, pok = accs2[h // 4]
                        cc = (h % 4) * 65
                        self.mm(psO[0:8, cc:cc + 65], p64[:, h * 8:(h + 1) * 8], vxp[:, h, :], first and h % 4 == 0, False, [p6k, vpk], [pok])
                    if pg == 0:
                        psS, psk = self.psn()
                        for hp in range(4):
                            self.mm(psS[0:32, hp * 16:(hp + 1) * 16], KTs[:, hp, :], qblk[:, s, hp, :], True, True, [('KTs',), ('qblk',)], [psk])
                        e, ek_ = R_e.next()
                        self.tt('dve', e[0:32], psS[0:32, 0:64].rearrange("p (h q) -> p h q", h=AH), Csn[:, :].unsqueeze(2).to_broadcast([32, AH, 8]),
                                ALU.add, [psk, ('Csn',)], [ek_])
                        p64f = R_e.next()
                        self.act(p64f[0][0:32], e[0:32], AF.Exp, [ek_], [p64f[1]])
                        p64, p6k = R_p64.next()
                        self.tt('dve', p64[0:32, :].rearrange("p (h q) -> p h q", h=AH), p64f[0][0:32],
                                masknew[:, s, :].unsqueeze(1).to_broadcast([32, AH, 8]), ALU.mult, [p64f[1], K0], [p6k])
                        for h in range(AH):
                            psO, pok = accs2[h // 4]
                            cc = (h % 4) * 65
                            self.mm(psO[0:8, cc:cc + 65], p64[0:32, h * 8:(h + 1) * 8], VXs[0:32, h, :], False, True, [p6k, ('VXs',)], [pok])
                        os_, osk = R_os.next()
                        rs, rsk = R_rs.next()
                        for half in range(2):
                            psO, pok = accs2[half]
                            pv = psO[0:8, 0:260].rearrange("p (h e) -> p h e", e=65)
                            self.ts('dve', rs[:, half * 4:(half + 1) * 4].unsqueeze(2), pv[:, :, 64:65], 1e-30, None, ALU.max, None, [pok], [rsk])
                        S.op('dve', lambda rs=rs: nc.vector.reciprocal(out=rs[:, :], in_=rs[:, :]), [rsk], [rsk])
                        for half in range(2):
                            psO, pok = accs2[half]
                            pv = psO[0:8, 0:260].rearrange("p (h e) -> p h e", e=65)
                            self.tt('dve', os_[:, half * 4:(half + 1) * 4, :], pv[:, :, 0:64],
                                    rs[:, half * 4:(half + 1) * 4].unsqueeze(2).to_broadcast([8, 4, 64]), ALU.mult, [pok, rsk], [osk])
                        ps, pk = self.psn()
                        pvb = bview(ps)
                        osf = os_[:].rearrange("p h d -> p (h d)")
                        for fc in range(4):
                            self.tr(pvb[:, fc * 8:(fc + 1) * 8], osf[:, fc * 128:(fc + 1) * 128], ident[0:8, 0:8], [osk, K0], [pk])
                        self.cp('act', oaTs[:, :, s * 8:(s + 1) * 8], pvb[:, 0:32].rearrange("p (f t) -> p f t", f=4), [pk], [('oaTs',)])
                    yield

            sg_ = sample_gen() if STOP >= 4 else iter(())
            n_s = NSQ * NPG + 1
            done_s = 0
            for i, _ in enumerate(prompt_gen()):
                target = (i + 1) * n_s / n_prompt_steps
                while done_s < target:
                    next(sg_, None)
                    done_s += 1
            for _ in sg_:
                pass
        self.psring = Ring(banks, "ps")
        if STOP <= 4:
            return self.finish_all()

        with self.phase():
            gpre = gload("gpre4", g_pre_mix)
            gpm = gload("gpm", g_post_mix)
            W_ga = wload("W_ga", w_in, 0, D, O_GA, 1024)
            W_gb = wload("W_gb", w_in, 0, D, O_GB, 1024)
            W_pa = wload("W_pa", w_pa, 0, AW, 0, 1024)
            W_pb = wload("W_pb", w_pb, 0, D, 0, 1024)
            W_o = wload("W_o", w_out, 0, D, 0, 1024)
            R = {
                'x': self.ring("xt4", 2, [128, D], F32), 'junk': self.ring("junk4", 1, [128, D], BF16),
                'ss': self.ring("ss4", 4, [128, 4], F32), 'hn': self.ring("hn4", 2, [128, D], BF16),
                'hnT': self.ring("hnT4", 2, [128, 8, 128], BF16),
            }
            R_sg = self.ring("sg", 2, [128, 2, D], F32)
            R_oa = self.ring("oa4", 2, [128, 4, 128], BF16)
            R_ob = self.ring("ob4", 2, [128, 8, 128], BF16)
            R_m = self.ring("m4", 2, [128, D], F32)
            R_mb = self.ring("mb4", 2, [128, D], BF16)
            R_mT = self.ring("mT4", 2, [128, 8, 128], BF16)
            R_xm = self.ring("xm4", 2, [128, D], F32)
            tiles = [(xc[n * 128:(n + 1) * 128, :], 128, (n - OWN0) * 128, None) for n in range(OWN0, NB)]
            tiles.append((xs[:, :], 32, TO, 'sample'))
            for (src, P, r0, kind) in tiles:
                xt, xk = R['x'].next()
                self.dma('sp', xt[0:P, :], src, [], [xk])
                hnT, tk = R['hnT'].next()
                norm_T(xt[0:P, :], xk, P, gpre, hnT[:, :, 0:P], tk, R)
                sg, sgk = R_sg.next()
                for gi, Wg in enumerate((W_ga, W_gb)):
                    for half in range(2):
                        ps, pk = self.psn()
                        proj_tok(hnT, tk, P, Wg, half * 512, 512, ps, pk)
                        self.act(sg[0:P, gi, half * 512:(half + 1) * 512], ps[0:P, :], AF.Sigmoid, [pk], [sgk])
                if kind == 'sample':
                    oa, oak, ob_, obk = oaTs, ('oaTs',), obTs, ('obTs',)
                else:
                    oa, oak = R_oa.next()
                    ob_, obk = R_ob.next()
                    self.dma('sp', oa[:], oat_scr[:, :, r0:r0 + 128], [], [oak])
                    self.dma('sp', ob_[:], obt_scr[:, :, r0:r0 + 128], [], [obk])
                m, mk = R_m.next()
                mb, mbk = R_mb.next()
                for half in range(2):
                    cs_ = slice(half * 512, (half + 1) * 512)
                    ps, pk = self.psn()
                    proj_tok(oa, oak, P, W_pa, half * 512, 512, ps, pk, nk=4)
                    self.tt('dve', m[0:P, cs_], ps[0:P, :], sg[0:P, 0, cs_], ALU.mult, [pk, sgk], [mk])
                    ps, pk = self.psn()
                    proj_tok(ob_, obk, P, W_pb, half * 512, 512, ps, pk)
                    self.tt('dve', sg[0:P, 1, cs_], ps[0:P, :], sg[0:P, 1, cs_], ALU.mult, [pk, sgk], [sgk])
                    self.tt('dve', mb[0:P, cs_], m[0:P, cs_], sg[0:P, 1, cs_], ALU.add, [mk, sgk], [mbk])
                mT, mtk = R_mT.next()
                transpose_into(mT[:, :, 0:P], mtk, mb, mbk, P, 8)
                ss, sk = R['ss'].next()
                junk, jk = R['junk'].next()
                pss = []
                for half in range(2):
                    ps, pk = self.psn()
                    proj_tok(mT, mtk, P, W_o, half * 512, 512, ps, pk)
                    self.act(junk[0:P, 0:512], ps[0:P, :], AF.Square, [pk], [jk, sk], accum=ss[0:P, half:half + 1])
                    pss.append((ps, pk))
                self.tt('dve', ss[0:P, 2:3], ss[0:P, 0:1], ss[0:P, 1:2], ALU.add, [sk], [sk])
                self.ts('dve', ss[0:P, 2:3], ss[0:P, 2:3], 1.0 / D, EPS, ALU.mult, ALU.add, [sk], [sk])
                self.rsqrt_col(ss[0:P, 3:4], ss[0:P, 2:3], sk)
                xm, xmk = R_xm.next()
                for half in range(2):
                    cs_ = slice(half * 512, (half + 1) * 512)
                    ps, pk = pss[half]
                    self.stt('dve', xm[0:P, cs_], ps[0:P, :], ss[0:P, 3:4], gpm[0:P, cs_], ALU.mult, ALU.mult, [pk, sk, K0], [xmk])
                self.tt('dve', xm[0:P, :], xm[0:P, :], xt[0:P, :], ALU.add, [xmk, xk], [xmk])
                self.dma('sp', xm_scr[r0:r0 + P, :], xm[0:P, :], [xmk], [('xm_scr',)])

        if STOP <= 5:
            return self.finish_all()

        with self.phase():
            gpf = gload("gpf", g_pre_ffn)
            gpo = gload("gpo", g_post_ffn)
            wcv = self.sb("wcv", [128, 3, 2 * NCH], F32)
            bcv = self.sb("bcv", [128, 2 * NCH], F32)
            for t in range(3):
                self.dma('sp', wcv[:, t, :], w_conv[t:t + 1, :].rearrange("o (c p) -> p (o c)", p=128), [], [K0], slow=True)
            self.dma('sp', bcv[:], b_conv.rearrange("o (c p) -> p (o c)", p=128), [], [K0], slow=True)
            W_dn = wload("W_dn", w_down, 0, DFF, 0, 1024)
            GW = QG * 128
            R = {
                'junk': self.ring("junk5", 1, [128, D], BF16), 'ss': self.ring("ss5", 4, [128, 4], F32),
                'hn': self.ring("hn5", 2, [128, D], BF16),
            }
            xmg = self.sb("xmg", [128, QG, D], F32)
            h2T = self.sb("h2T", [128, 8, GW], BF16)
            hT = self.sb("hT", [128, NCH, GW], BF16)
            lbp = self.sb("lbp", [128, 2 * NCH, 1, 2], F32)
            lbs = self.sb("lbs", [128, 2 * NCH, NSQ, 2], F32)
            S.op('pool', lambda: nc.gpsimd.memset(lbp[:], 0.0), [], [('lbp',)])
            R_wu = self.ring("wu", 3, [128, 8, 256], BF16)
            R_uc = self.ring("uc", 3, [128, GW + 8], F32)
            R_c = self.ring("cv", 4, [128, GW], F32)
            R_t = self.ring("tg", 3, [128, GW], F32)
            R_tp = self.ring("tpl", 2, [128, GW], F32)
            R_y = self.ring("y5", 2, [128, D], F32)
            R_wt = self.ring("wt", 2, [128, 8, 512], BF16)
            R_ut = self.ring("ut", 2, [128, 512], F32)
            sct = self.sb("sct", [8, 1408], F32)
            ps, pk = self.psn()
            for piece in range(4):
                self.dma('sp', sct[:], sconv_d[:, piece * 1408:(piece + 1) * 1408], [], [('sct',)])
                for c in range(11):
                    ch = piece * 11 + c
                    self.tr(ps[:, ch * 8:(ch + 1) * 8], sct[0:8, c * 128:(c + 1) * 128], identf[0:8, 0:8], [('sct',), K0], [pk])
            self.cp('act', lbs[:].rearrange("p c s l -> p (c s l)"), ps[:, 0:2 * NCH * 8], [pk], [('lbs',)])

            fgroups = [(OWN0, 1, 1, 128, 'p')] + [(OWN0 + 1 + g * QG, QG, 1, QG * 128, 'p') for g in range(NQ // QG)]
            fgroups.append((None, 1, NSQ, 8, 's'))
            for (n0, nblk, nseq, L, kind) in fgroups:
                W = nseq * L
                lb, lbk = (lbp, ('lbp',)) if kind == 'p' else (lbs, ('lbs',))
                for b in range(nblk):
                    P = 128 if kind == 'p' else 32
                    r0 = (n0 + b - OWN0) * 128 if kind == 'p' else TO
                    self.dma('sp', xmg[0:P, b, :], xm_scr[r0:r0 + P, :], [], [('xmg', b)])
                    norm_T(xmg[0:P, b, :], ('xmg', b), P, gpf, h2T[:, :, b * 128:b * 128 + P], ('h2T',), R)
                for ch in range(NCH):
                    wu, wuk = R_wu.next()
                    self.dma('sp', wu[:, :, 0:128], wup_scr[:, :, ch * 128:(ch + 1) * 128], [], [wuk])
                    self.dma('sp', wu[:, :, 128:256], wup_scr[:, :, DFF + ch * 128:DFF + (ch + 1) * 128], [], [wuk])
                    cvs = []
                    for part in range(2):
                        ci = part * NCH + ch
                        ps, pk = self.psn()
                        for kc in range(8):
                            self.mm(ps[:, 0:W], wu[:, kc, part * 128:(part + 1) * 128], h2T[:, kc, 0:W], kc == 0, kc == 7, [wuk, ('h2T',)], [pk])
                        uc, uck = R_uc.next()
                        ucv = uc[:, 0:nseq * (L + 2)].rearrange("p (s l) -> p s l", s=nseq)
                        self.cp('act', ucv[:, :, 2:2 + L], ps[:, 0:W].rearrange("p (s l) -> p s l", s=nseq), [pk], [uck])
                        self.cp('pool', ucv[:, :, 0:2], lb[:, ci, :, :], [lbk], [uck])
                        cv, cvk = R_c.next()
                        cvv = cv[:, 0:W].rearrange("p (s l) -> p s l", s=nseq)
                        self.act(cvv, ucv[:, :, 2:2 + L], AF.Identity, [uck, K0], [cvk], bias=bcv[:, ci:ci + 1], scale=wcv[:, 2, ci:ci + 1])
                        self.stt('dve', cvv, ucv[:, :, 1:1 + L], wcv[:, 1, ci:ci + 1], cvv, ALU.mult, ALU.add, [uck, cvk, K0], [cvk])
                        self.stt('dve', cvv, ucv[:, :, 0:L], wcv[:, 0, ci:ci + 1], cvv, ALU.mult, ALU.add, [uck, cvk, K0], [cvk])
                        if kind == 'p':
                            self.cp('pool', lbp[:, ci, :, :], ucv[:, :, L:L + 2], [uck], [('lbp',)])
                        cvs.append((cv, cvk))
                    (ca, cak), (cg, cgk) = cvs
                    t1, t1k = R_t.next()
                    self.act(t1[:, 0:W], cg[:, 0:W], AF.Square, [cgk], [t1k])
                    self.ts('dve', t1[:, 0:W], t1[:, 0:W], 0.044715, 1.0, ALU.mult, ALU.add, [t1k], [t1k])
                    self.tt('dve', t1[:, 0:W], t1[:, 0:W], cg[:, 0:W], ALU.mult, [t1k, cgk], [t1k])
                    t2, t2k = R_t.next()
                    self.act(t2[:, 0:W], t1[:, 0:W], AF.Sigmoid, [t1k], [t2k], scale=1.5957691216057308)
                    self.tt('dve', t2[:, 0:W], t2[:, 0:W], cg[:, 0:W], ALU.mult, [t2k, cgk], [t2k])
                    self.tt('dve', hT[:, ch, 0:W], t2[:, 0:W], ca[:, 0:W], ALU.mult, [t2k, cak], [('hT',)])
                for b in range(nblk):
                    P = 128 if kind == 'p' else 32
                    ss, sk = R['ss'].next()
                    junk, jk = R['junk'].next()
                    pss = []
                    for half in range(2):
                        ps, pk = self.psn()
                        for ch in range(NCH):
                            self.mm(ps[0:P, :], hT[:, ch, b * 128:b * 128 + P], W_dn[:, ch, half * 512:(half + 1) * 512],
                                    ch == 0, ch == NCH - 1, [('hT',), K0], [pk])
                        self.act(junk[0:P, 0:512], ps[0:P, :], AF.Square, [pk], [jk, sk], accum=ss[0:P, half:half + 1])
                        pss.append((ps, pk))
                    self.tt('dve', ss[0:P, 2:3], ss[0:P, 0:1], ss[0:P, 1:2], ALU.add, [sk], [sk])
                    self.ts('dve', ss[0:P, 2:3], ss[0:P, 2:3], 1.0 / D, EPS, ALU.mult, ALU.add, [sk], [sk])
                    self.rsqrt_col(ss[0:P, 3:4], ss[0:P, 2:3], sk)
                    y, yk = R_y.next()
                    for half in range(2):
                        cs_ = slice(half * 512, (half + 1) * 512)
                        ps, pk = pss[half]
                        self.stt('dve', y[0:P, cs_], ps[0:P, :], ss[0:P, 3:4], gpo[0:P, cs_], ALU.mult, ALU.mult, [pk, sk, K0], [yk])
                    self.tt('dve', y[0:P, :], y[0:P, :], xmg[0:P, b, :], ALU.add, [yk, ('xmg', b)], [yk])
                    if kind == 's':
                        self.out_events.append(self.dma('sp', o_ys[:, :], y[0:32, :], [yk], []))
                    elif n0 + b > OWN0:
                        ro = (n0 + b - OWN0 - 1) * 128
                        self.out_events.append(self.dma('sp', o_y[ro:ro + 128, :], y[:, :], [yk], []))
                last_p = (kind == 'p' and n0 + nblk == NB)
                if last_p or kind == 's':
                    P = 128 if kind == 'p' else 32
                    bcol = (nblk - 1) * 128
                    for cgp in range(2 * DFF // 512):
                        wt, wtk = R_wt.next()
                        self.dma('sp', wt[:], wup_scr[:, :, cgp * 512:(cgp + 1) * 512], [], [wtk])
                        ps, pk = self.psn()
                        for kc in range(8):
                            self.mm(ps[0:P, :], h2T[:, kc, bcol:bcol + P], wt[:, kc, :], kc == 0, kc == 7, [('h2T',), wtk], [pk])
                        ut, utk = R_ut.next()
                        self.cp('act', ut[0:P, :], ps[0:P, :], [pk], [utk])
                        if kind == 'p':
                            self.out_events.append(self.dma('sp', o_conv[:, cgp * 512:(cgp + 1) * 512], ut[126:128, :], [utk], []))
                        else:
                            for s in range(NSQ):
                                self.out_events.append(self.dma('sp', o_convs[2 * s:2 * s + 2, cgp * 512:(cgp + 1) * 512],
                                                                ut[8 * s + 6:8 * s + 8, :], [utk], []))
        return self.finish_all()

    def finish_all(self):
        S = self.S
        for ev in self.out_events:
            S.need('sp', ev)
        S.finish()
        self.es.close()
        return self.nc


_CACHE = {}


def _get_builder(NB, QG, NPG, NPOOL):
    key = (NB, QG, NPG, NPOOL)
    if key not in _CACHE:
        b = Builder(NB, QG, NPG, NPOOL)
        b.build()
        _CACHE[key] = b
    return _CACHE[key]


def kernel(x_prompt, x_sample, cache_k, cache_v, cache_logf, state_gla, state_conv, page_table,
           g_pre_mix, w_in, b_f, w_alpha2, b_alpha, g_gla, w_proj_a, w_proj_b, w_out, g_post_mix,
           g_pre_ffn, w_up, w_conv, b_conv, w_down, g_post_ffn):
    f32 = np.float32
    x_prompt = np.asarray(x_prompt, f32)
    B, SEQ, _ = x_prompt.shape
    DB, DS, _ = np.asarray(x_sample).shape
    NPOOL = np.asarray(cache_k).shape[1]
    NPG = np.asarray(page_table).shape[1]
    assert B == 2 and DB == 32 and DS == 8 and SEQ % 2048 == 0
    NB = SEQ // 128
    NQ = NB // 4
    QT_ = SEQ // 4
    QG = min(4, NQ)
    bld = _get_builder(NB, QG, NPG, NPOOL)
    consts = host_consts(QG)
    ck = np.ascontiguousarray(np.asarray(cache_k, f32)[0].reshape(NPOOL * 128, AW))
    cv = np.ascontiguousarray(np.asarray(cache_v, f32)[0].reshape(NPOOL * 128, AW))
    clf = np.ascontiguousarray(np.asarray(cache_logf, f32)[0].reshape(NPOOL * 128, AH))
    shared = {
        'cache_k': ck, 'cache_v': cv, 'cache_lf': clf,
        'w_in': np.ascontiguousarray(np.asarray(w_in, f32)[0]),
        'w_alpha2': np.ascontiguousarray(np.asarray(w_alpha2, f32)[0]),
        'b_alpha': np.asarray(b_alpha, f32).reshape(1, AW), 'b_f': np.asarray(b_f, f32).reshape(1, AH),
        'g_pre_mix': np.asarray(g_pre_mix, f32).reshape(1, D), 'g_gla': np.asarray(g_gla, f32).reshape(1, BV),
        'w_proj_a': np.ascontiguousarray(np.asarray(w_proj_a, f32)[0]),
        'w_proj_b': np.ascontiguousarray(np.asarray(w_proj_b, f32)[0]),
        'w_out': np.ascontiguousarray(np.asarray(w_out, f32)[0]),
        'g_post_mix': np.asarray(g_post_mix, f32).reshape(1, D), 'g_pre_ffn': np.asarray(g_pre_ffn, f32).reshape(1, D),
        'w_up': np.ascontiguousarray(np.asarray(w_up, f32)[0]),
        'w_conv': np.ascontiguousarray(np.asarray(w_conv, f32)[0]),
        'b_conv': np.asarray(b_conv, f32).reshape(1, 2 * DFF),
        'w_down': np.ascontiguousarray(np.asarray(w_down, f32)[0]),
        'g_post_ffn': np.asarray(g_post_ffn, f32).reshape(1, D),
    }
    shared.update(consts)
    xs_all = np.asarray(x_sample, f32)
    pt_all = np.asarray(page_table, np.int32)
    sg = np.asarray(state_gla, f32)[0]
    sc = np.asarray(state_conv, f32)[0]
    in_maps = []
    for c in range(8):
        b, j = c // 4, c % 4
        xc = np.zeros((SEQ, D), f32)
        xc[(3 - j) * QT_:] = x_prompt[b, :(j + 1) * QT_]
        km = np.zeros((128, NB), f32)
        km[:, :(3 - j) * NQ] = NEG
        m = dict(shared)
        m.update({
            'xc': xc, 'kmask': km,
            'xs': np.ascontiguousarray(xs_all[4 * c:4 * c + 4].reshape(32, D)),
            'pt': np.ascontiguousarray(pt_all[4 * c:4 * c + 4].reshape(1, 4 * NPG)),
            'sgla': np.ascontiguousarray(sg[4 * c:4 * c + 4].reshape(4 * BH * 128, BV)),
            'sconv': np.ascontiguousarray(sc[4 * c:4 * c + 4].reshape(8, 2 * DFF)),
        })
        in_maps.append(m)
    res = run_bass_kernel_spmd(bld.nc, in_maps, core_ids=list(range(8)))
    R = res.results
    y_p = np.zeros((B, SEQ, D), f32)
    k_p = np.zeros((1, B, SEQ, AH, AD), f32)
    v_p = np.zeros((1, B, SEQ, AH, AD), f32)
    lf_p = np.zeros((1, B, SEQ, AH), f32)
    gla_p = np.zeros((1, B, BH, BK, BV), f32)
    conv_p = np.zeros((1, B, 2, 2 * DFF), f32)
    y_s = np.zeros((DB, DS, D), f32)
    k_s = np.zeros((1, DB, DS, AH, AD), f32)
    v_s = np.zeros((1, DB, DS, AH, AD), f32)
    lf_s = np.zeros((1, DB, DS, AH), f32)
    gla_s = np.zeros((1, DB, BH, BK, BV), f32)
    conv_s = np.zeros((1, DB, 2, 2 * DFF), f32)
    for c in range(8):
        b, j = c // 4, c % 4
        r = R[c]
        sl = slice(j * QT_, (j + 1) * QT_)
        y_p[b, sl] = r['o_y']
        k_p[0, b, sl] = r['o_k'].reshape(QT_, AH, AD)
        v_p[0, b, sl] = r['o_v'].reshape(QT_, AH, AD)
        lf_p[0, b, sl] = r['o_lf']
        if j == 3:
            gla_p[0, b] = r['o_gla'].reshape(BH, BK, BV)
            conv_p[0, b] = r['o_conv']
        ss = slice(4 * c, 4 * c + 4)
        y_s[ss] = r['o_ys'].reshape(4, DS, D)
        k_s[0, ss] = r['o_ks'].reshape(4, DS, AH, AD)
        v_s[0, ss] = r['o_vs'].reshape(4, DS, AH, AD)
        lf_s[0, ss] = r['o_lfs'].reshape(4, DS, AH)
        gla_s[0, ss] = r['o_glas'].reshape(4, BH, BK, BV)
        conv_s[0, ss] = r['o_convs'].reshape(4, 2, 2 * DFF)
    return (y_p, y_s, k_p, v_p, lf_p, gla_p, conv_p, k_s, v_s, lf_s, gla_s, conv_s)
```

```python
import contextlib
import os
import numpy as np
import concourse.bass as bass
import concourse.mybir as mybir
from concourse.bass_utils import run_bass_kernel_spmd

F32 = mybir.dt.float32
BF16 = mybir.dt.bfloat16
I32 = mybir.dt.int32
AF = mybir.ActivationFunctionType
ALU = mybir.AluOpType
AX = mybir.AxisListType

D = 1024
AH, AD = 8, 64
AW = 512
BH, BK, BV = 4, 128, 256
GR = 16
DFF = 2816
NCH = DFF // 128
EPS = 1e-6
DIN = 6680
O_QA, O_KA, O_VA, O_F = 0, 512, 1024, 1536
O_QB, O_KB, O_VB, O_AL, O_RB, O_GA, O_GB = 1544, 2056, 2568, 3592, 3608, 4632, 5656
NEG = -30000.0
EPOCH = 30000
ND = 8


class Sched:
    def __init__(self, nc, es):
        self.nc, self.es = nc, es
        self.E = {'pe': nc.tensor, 'act': nc.scalar, 'dve': nc.vector, 'pool': nc.gpsimd, 'sp': nc.sync}
        self.cnt = {e: 0 for e in self.E}
        self.sems = {e: [] for e in self.E}
        self.waited = {}
        self.dsem = {q: [es.enter_context(nc.semaphore(f"d{q}{i}")) for i in range(ND)] for q in ('sp', 'pool', 'act')}
        self.dcnt = {q: 0 for q in self.dsem}
        self.dwaited = {}
        self.lastw = {}
        self.readers = {}
        self.nwaits = 0

    def _sem(self, e, n):
        ep = (n - 1) // EPOCH
        while len(self.sems[e]) <= ep:
            self.sems[e].append(self.es.enter_context(self.nc.semaphore(f"s{e}{len(self.sems[e])}")))
        return self.sems[e][ep], n - ep * EPOCH

    def need(self, E, ev):
        if ev[0] == 'e':
            _, Fe, n = ev
            if Fe == E and E == 'pe':
                return
            if self.waited.get((E, Fe), 0) >= n:
                return
            sem, val = self._sem(Fe, n)
            self.E[E].wait_ge(sem, val)
            self.waited[(E, Fe)] = n
            self.nwaits += 1
        else:
            _, q, k = ev
            slot, val = k % ND, 16 * (k // ND + 1)
            if self.dwaited.get((E, q, slot), 0) >= val:
                return
            self.E[E].wait_ge(self.dsem[q][slot], val)
            self.dwaited[(E, q, slot)] = val
            self.nwaits += 1

    def _deps(self, reads, writes):
        deps = []
        for r in reads:
            if r in self.lastw:
                deps.append(self.lastw[r])
            if isinstance(r, tuple) and r[0] == 'ps':
                deps.extend(self.readers.get(r, ()))
        for w in writes:
            if w in self.lastw:
                deps.append(self.lastw[w])
            deps.extend(self.readers.get(w, ()))
        return deps

    def _record(self, ev, reads, writes):
        for r in reads:
            lst = self.readers.setdefault(r, [])
            if ev[0] == 'e':
                lst[:] = [x for x in lst if not (x[0] == 'e' and x[1] == ev[1])]
            lst.append(ev)
        for w in writes:
            self.lastw[w] = ev
            self.readers[w] = []

    def op(self, E, fn, reads=(), writes=()):
        for d in self._deps(reads, writes):
            self.need(E, d)
        ins = fn()
        self.cnt[E] += 1
        sem, _ = self._sem(E, self.cnt[E])
        ins.then_inc(sem, 1)
        self._record(('e', E, self.cnt[E]), reads, writes)

    def dma(self, q, fn, reads=(), writes=()):
        for d in self._deps(reads, writes):
            self.need(q, d)
        k = self.dcnt[q]
        if k >= ND:
            self.need(q, ('d', q, k - ND))
        ins = fn()
        ins.then_inc(self.dsem[q][k % ND], 16)
        self.dcnt[q] = k + 1
        ev = ('d', q, k)
        self._record(ev, reads, writes)
        return ev

    def finish(self):
        for e in self.E:
            if self.cnt[e]:
                self.need('sp', ('e', e, self.cnt[e]))
        for q in self.dcnt:
            for k in range(max(0, self.dcnt[q] - ND), self.dcnt[q]):
                self.need('sp', ('d', q, k))


class Ring:
    def __init__(self, tiles, name):
        self.tiles, self.name, self.i = tiles, name, 0

    def next(self):
        t = self.tiles[self.i % len(self.tiles)]
        k = (self.name, self.i % len(self.tiles))
        self.i += 1
        return t, k


def host_consts(QG):
    tri = np.triu(np.ones((128, 128), np.float32))
    c = {
        'c_ident': np.eye(128, dtype=np.float32),
        'c_tri': tri,
        'c_trin': (-tri / 16.0).astype(np.float32),
        'c_ones': np.ones((128, 128), np.float32),
    }
    stair = np.zeros((QG, 128, QG * 128), np.float32)
    for d in range(QG):
        for qb in range(QG):
            if qb == d:
                stair[d][:, qb * 128:(qb + 1) * 128] = tri
            elif qb > d:
                stair[d][:, qb * 128:(qb + 1) * 128] = 1.0
    c['c_stair'] = stair.reshape(QG * 128, QG * 128)
    tri_s = np.zeros((32, 32), np.float32)
    seqind = np.zeros((32, 4), np.float32)
    for s in range(4):
        tri_s[s * 8:(s + 1) * 8, s * 8:(s + 1) * 8] = np.triu(np.ones((8, 8), np.float32))
        seqind[s * 8:(s + 1) * 8, s] = 1.0
    c['c_tris'] = tri_s
    c['c_trins'] = (-tri_s / 16.0).astype(np.float32)
    c['c_seqind'] = seqind
    c['c_seqindn'] = (-seqind / 16.0).astype(np.float32)
    colm = np.zeros((4, 128, 32), np.float32)
    for s in range(4):
        colm[s][:, s * 8:(s + 1) * 8] = 1.0
    c['c_colmask'] = colm.reshape(4 * 128, 32)
    c['c_iota'] = np.arange(128, dtype=np.int32).reshape(128, 1)
    c['c_trigt'] = np.tril(np.ones((128, 128), np.float32), -1)
    mn = np.zeros((32, 4, 8), np.float32)
    for k in range(32):
        for q in range(8):
            if (k % 8) <= q:
                mn[k, k // 8, q] = 1.0
    c['c_masknew'] = mn.reshape(32, 32)
    return c


class Builder:
    def __init__(self, NB, QG, NPG, NPOOL):
        self.NB, self.QG, self.NPG, self.NPOOL = NB, QG, NPG, NPOOL
        self.NQ = NB // 4
        self.OWN0 = NB - self.NQ - 1
        self.NOWN = self.NQ + 1
        self.TT = NB * 128
        self.NSQ = 4
        self.es = contextlib.ExitStack()
        self.nc = bass.Bass("TRN2", target_bir_lowering=False)
        self.S = Sched(self.nc, self.es)
        self.out_events = []
        self.cur = self.es

    def din(self, name, shape, dt=F32):
        return self.nc.dram_tensor(name, list(shape), dt, kind="ExternalInput").ap()

    def dout(self, name, shape, dt=F32):
        return self.nc.dram_tensor(name, list(shape), dt, kind="ExternalOutput").ap()

    def dscr(self, name, shape, dt):
        return self.nc.dram_tensor(name, list(shape), dt, kind="Internal").ap()

    def sb(self, name, shape, dt=F32):
        return self.cur.enter_context(self.nc.sbuf_tensor(name, list(shape), dt))

    def ring(self, name, n, shape, dt=F32):
        return Ring([self.sb(f"{name}{i}", shape, dt) for i in range(n)], name)

    def mm(self, out, lhsT, rhs, start, stop, reads, writes):
        nc = self.nc
        self.S.op('pe', lambda: nc.tensor.matmul(out, lhsT=lhsT, rhs=rhs, start=start, stop=stop), reads, writes)

    def tr(self, out, in_, ident, reads, writes):
        nc = self.nc
        self.S.op('pe', lambda: nc.tensor.transpose(out, in_, ident), reads, writes)

    def act(self, out, in_, func, reads, writes, bias=None, scale=None, accum=None):
        nc = self.nc
        kw = {}
        if bias is not None:
            kw['bias'] = bias
        if scale is not None:
            kw['scale'] = scale
        if accum is not None:
            kw['accum_out'] = accum
        self.S.op('act', lambda: nc.scalar.activation(out=out, in_=in_, func=func, **kw), reads, writes)

    def tt(self, eng, out, in0, in1, op, reads, writes):
        e = self.S.E[eng]
        self.S.op(eng, lambda: e.tensor_tensor(out=out, in0=in0, in1=in1, op=op), reads, writes)

    def ts(self, eng, out, in0, s1, s2, op0, op1, reads, writes):
        e = self.S.E[eng]
        if op1 is None:
            self.S.op(eng, lambda: e.tensor_scalar(out=out, in0=in0, scalar1=s1, scalar2=None, op0=op0), reads, writes)
        else:
            self.S.op(eng, lambda: e.tensor_scalar(out=out, in0=in0, scalar1=s1, scalar2=s2, op0=op0, op1=op1), reads, writes)

    def stt(self, eng, out, in0, scalar, in1, op0, op1, reads, writes):
        e = self.S.E[eng]
        self.S.op(eng, lambda: e.scalar_tensor_tensor(out=out, in0=in0, scalar=scalar, in1=in1, op0=op0, op1=op1), reads, writes)

    def cp(self, eng, out, in_, reads, writes):
        if eng == 'act':
            nc = self.nc
            self.S.op('act', lambda: nc.scalar.copy(out=out, in_=in_), reads, writes)
        else:
            e = self.S.E[eng]
            self.S.op(eng, lambda: e.tensor_copy(out=out, in_=in_), reads, writes)

    def dma(self, q, out, in_, reads, writes, slow=False):
        e = self.S.E[q]
        if slow:
            return self.S.dma(q, lambda: e.dma_start(out=out, in_=in_, allow_slow_non_contiguous=True), reads, writes)
        return self.S.dma(q, lambda: e.dma_start(out=out, in_=in_), reads, writes)

    def rsqrt_col(self, out, in_, key):
        nc = self.nc
        self.S.op('act', lambda: nc.scalar.sqrt(out=out, in_=in_), [key], [key])
        self.S.op('dve', lambda: nc.vector.reciprocal(out=out, in_=out), [key], [key])

    def psn(self):
        t, k = self.psring.next()
        return t, k

    @contextlib.contextmanager
    def phase(self):
        prev = self.cur
        with contextlib.ExitStack() as st:
            self.cur = st
            yield
            self.barrier()
        self.cur = prev

    def barrier(self):
        S = self.S
        for E in S.E:
            for Fe in S.E:
                if Fe != E and S.cnt[Fe]:
                    S.need(E, ('e', Fe, S.cnt[Fe]))
            for q in S.dcnt:
                for k in range(max(0, S.dcnt[q] - ND), S.dcnt[q]):
                    S.need(E, ('d', q, k))
        S.lastw.clear()
        S.readers.clear()

    def build(self):
        nc, S = self.nc, self.S
        NB, QG, NPG, NPOOL, TT, NSQ = self.NB, self.QG, self.NPG, self.NPOOL, self.TT, self.NSQ
        NOWN, OWN0, NQ = self.NOWN, self.OWN0, self.NQ
        TO = NOWN * 128
        self.cur = self.es
        STOP = int(os.environ.get("MK_STOP", "99"))

        xc = self.din("xc", [TT, D])
        kmask_d = self.din("kmask", [128, NB])
        xs = self.din("xs", [32, D])
        pt_d = self.din("pt", [1, NSQ * NPG], I32)
        cache_k = self.din("cache_k", [NPOOL * 128, AW])
        cache_v = self.din("cache_v", [NPOOL * 128, AW])
        cache_lf = self.din("cache_lf", [NPOOL * 128, AH])
        sgla_d = self.din("sgla", [NSQ * BH * 128, BV])
        sconv_d = self.din("sconv", [NSQ * 2, 2 * DFF])
        w_in = self.din("w_in", [D, DIN])
        w_a2 = self.din("w_alpha2", [GR, AW])
        b_al = self.din("b_alpha", [1, AW])
        b_f = self.din("b_f", [1, AH])
        g_pre_mix = self.din("g_pre_mix", [1, D])
        g_gla = self.din("g_gla", [1, BV])
        w_pa = self.din("w_proj_a", [AW, D])
        w_pb = self.din("w_proj_b", [D, D])
        w_out = self.din("w_out", [D, D])
        g_post_mix = self.din("g_post_mix", [1, D])
        g_pre_ffn = self.din("g_pre_ffn", [1, D])
        w_up = self.din("w_up", [D, 2 * DFF])
        w_conv = self.din("w_conv", [3, 2 * DFF])
        b_conv = self.din("b_conv", [1, 2 * DFF])
        w_down = self.din("w_down", [DFF, D])
        g_post_ffn = self.din("g_post_ffn", [1, D])
        cst = {k: self.din(k, v.shape, I32 if v.dtype == np.int32 else F32) for k, v in host_consts(QG).items()}

        o_y = self.dout("o_y", [NQ * 128, D])
        o_k = self.dout("o_k", [NQ * 128, AW])
        o_v = self.dout("o_v", [NQ * 128, AW])
        o_lf = self.dout("o_lf", [NQ * 128, AH])
        o_gla = self.dout("o_gla", [BH * 128, BV])
        o_conv = self.dout("o_conv", [2, 2 * DFF])
        o_ys = self.dout("o_ys", [32, D])
        o_ks = self.dout("o_ks", [32, AW])
        o_vs = self.dout("o_vs", [32, AW])
        o_lfs = self.dout("o_lfs", [32, AH])
        o_glas = self.dout("o_glas", [NSQ * BH * 128, BV])
        o_convs = self.dout("o_convs", [NSQ * 2, 2 * DFF])

        kt_scr = self.dscr("kt_scr", [128, 4, TT], BF16)
        vx_scr = self.dscr("vx_scr", [NB, 128, AH * 65], BF16)
        qt_scr = self.dscr("qt_scr", [128, 4, TO], BF16)
        obt_scr = self.dscr("obt_scr", [128, 8, TO], BF16)
        oat_scr = self.dscr("oat_scr", [128, 4, TO], BF16)
        xm_scr = self.dscr("xm_scr", [TO + 32, D], F32)
        wup_scr = self.dscr("wup_scr", [128, 8, 2 * DFF], BF16)

        ident = self.sb("ident", [128, 128], BF16)
        identf = self.sb("identf", [128, 128], F32)
        tri_f = self.sb("tri_f", [128, 128], F32)
        trigt_f = self.sb("trigt_f", [128, 128], F32)
        tri_b = self.sb("tri_b", [128, 128], BF16)
        trin_b = self.sb("trin_b", [128, 128], BF16)
        ones_f = self.sb("ones_f", [128, 128], F32)
        onesn_b = self.sb("onesn_b", [128, 1], BF16)
        stair = self.sb("stair", [128, QG, QG * 128], BF16)
        kmask = self.sb("kmask_t", [128, NB], F32)
        tris_f = self.sb("tris_f", [32, 32], F32)
        tris_b = self.sb("tris_b", [32, 32], BF16)
        trins_b = self.sb("trins_b", [32, 32], BF16)
        seqind_f = self.sb("seqind_f", [32, 4], F32)
        seqindn_b = self.sb("seqindn_b", [32, 4], BF16)
        colmask = self.sb("colmask", [128, 4, 32], BF16)
        masknew = self.sb("masknew", [32, 4, 8], BF16)
        iota_i = self.sb("iota_i", [128, 1], I32)
        Ccum = self.sb("Ccum", [128, NB, AH], F32)
        carry = self.sb("carry", [128, NB + 1, AH], F32)
        QTs = self.sb("QTs", [128, 4, 32], BF16)
        KTs = self.sb("KTs", [128, 4, 32], BF16)
        VXs = self.sb("VXs", [32, AH, 65], BF16)
        Csn = self.sb("Csn", [32, AH], F32)
        obTs = self.sb("obTs", [128, 8, 32], BF16)
        oaTs = self.sb("oaTs", [128, 4, 32], BF16)

        K0 = ('const',)
        cl = [
            ('pool', ident[:], cst['c_ident'][:, :]), ('sp', identf[:], cst['c_ident'][:, :]),
            ('sp', tri_f[:], cst['c_tri'][:, :]), ('pool', tri_b[:], cst['c_tri'][:, :]),
            ('sp', trigt_f[:], cst['c_trigt'][:, :]),
            ('pool', trin_b[:], cst['c_trin'][:, :]), ('sp', ones_f[:], cst['c_ones'][:, :]),
            ('pool', onesn_b[:], cst['c_trin'][0:128, 127:128]),
            ('sp', kmask[:], kmask_d[:, :]),
            ('sp', tris_f[:], cst['c_tris'][:, :]),
            ('pool', tris_b[:], cst['c_tris'][:, :]), ('pool', trins_b[:], cst['c_trins'][:, :]),
            ('sp', seqind_f[:], cst['c_seqind'][:, :]), ('pool', seqindn_b[:], cst['c_seqindn'][:, :]),
            ('sp', iota_i[:], cst['c_iota'][:, :]),
            ('pool', masknew[:], cst['c_masknew'][:, :].rearrange("k (s q) -> k s q", s=4)),
        ]
        for q, o, i in cl:
            self.dma(q, o, i, [], [K0], slow=True)
        for d in range(QG):
            self.dma('pool', stair[:, d, :], cst['c_stair'][d * 128:(d + 1) * 128, :], [], [K0])
        for s in range(4):
            self.dma('pool', colmask[:, s, :], cst['c_colmask'][s * 128:(s + 1) * 128, :], [], [K0])
        S.op('pool', lambda: nc.gpsimd.memset(carry[:, 0, :], 0.0), [], [('carry', 0)])
        S.op('pool', lambda: nc.gpsimd.memset(VXs[:], 1.0), [], [('VXs',)])
        for k in range(8):
            self.dma('pool', wup_scr[:, k, :], w_up[k * 128:(k + 1) * 128, :], [], [('wup_scr',)])

        banks = [self.es.enter_context(nc.psum_tensor(f"ps{i}", [128, 512], F32)) for i in range(8)]
        self.psring = Ring(banks, "ps")

        def bview(ps):
            return ps[:].bitcast(BF16)

        def gload(name, g_ap, cols=D):
            t = self.sb(name, [128, cols], F32)
            self.dma('sp', t[:], g_ap.partition_broadcast(128), [], [K0], slow=True)
            return t

        def wload(name, src, r0, rows, c0, cols):
            kc = rows // 128
            t = self.sb(name, [128, kc, cols], BF16)
            v = src[r0:r0 + rows, c0:c0 + cols].rearrange("(k p) c -> p k c", p=128)
            for k in range(kc):
                self.dma('pool', t[:, k, :], v[:, k, :], [], [K0])
            return t

        def norm_T(xt, xk, P, gt, dst, dkey, R):
            junk, jk = R['junk'].next()
            ss, sk = R['ss'].next()
            self.act(junk[0:P, :], xt, AF.Square, [xk], [jk, sk], accum=ss[0:P, 0:1])
            self.ts('dve', ss[0:P, 1:2], ss[0:P, 0:1], 1.0 / D, EPS, ALU.mult, ALU.add, [sk], [sk])
            self.rsqrt_col(ss[0:P, 2:3], ss[0:P, 1:2], sk)
            hn, hk = R['hn'].next()
            self.stt('dve', hn[0:P, :], xt, ss[0:P, 2:3], gt[0:P, :], ALU.mult, ALU.mult, [xk, sk, K0], [hk])
            ps, pk = self.psn()
            pv = bview(ps)
            for kc in range(8):
                self.tr(pv[:, kc * P:(kc + 1) * P], hn[0:P, kc * 128:(kc + 1) * 128], ident[0:P, 0:P], [hk, K0], [pk])
            self.cp('act', dst, pv[:, 0:8 * P].rearrange("p (k t) -> p k t", k=8), [pk], [dkey])

        def proj_tok(hnT, tk, P, W, c0, cols, ps, pk, pcol=0, nk=8):
            for kc in range(nk):
                self.mm(ps[0:P, pcol:pcol + cols], hnT[:, kc, 0:P], W[:, kc, c0:c0 + cols], kc == 0, kc == nk - 1, [tk, K0], [pk])

        def proj_feat(hnT, tk, P, W, c0, M, ps, pk, pcol=0):
            for kc in range(8):
                self.mm(ps[0:M, pcol:pcol + P], W[:, kc, c0:c0 + M], hnT[:, kc, 0:P], kc == 0, kc == 7, [tk, K0], [pk])

        def softplus_neg(dst, dk, src, sk_, tmp, tmk):
            self.act(tmp, src, AF.Exp, [sk_], [tmk], scale=-1.0)
            self.act(dst, tmp, AF.Ln, [tmk], [dk], bias=1.0)

        def transpose_into(dst, dkey, src, sk_, P, nfc):
            for g0 in range(0, nfc, 8):
                g = min(8, nfc - g0)
                ps, pk = self.psn()
                pv = bview(ps)
                for i in range(g):
                    fc = g0 + i
                    self.tr(pv[:, i * P:(i + 1) * P], src[0:P, fc * 128:(fc + 1) * 128], ident[0:P, 0:P], [sk_, K0], [pk])
                self.cp('act', dst[:, g0:g0 + g, :], pv[:, 0:g * P].rearrange("p (k t) -> p k t", k=g), [pk], [dkey])

        with self.phase():
            gpre = gload("gpre", g_pre_mix)
            ggl = gload("ggl", g_gla, BV)
            bft = gload("bft", b_f, AH)
            balt = gload("balt", b_al, AW)
            W_kv = wload("W_kv", w_in, 0, D, O_KA, 1024)
            W_f = wload("W_f", w_in, 0, D, O_F, 8)
            W_kb = wload("W_kb", w_in, 0, D, O_KB, 512)
            W_vb = wload("W_vb", w_in, 0, D, O_VB, 1024)
            W_al = wload("W_al", w_in, 0, D, O_AL, 16)
            W_qa = wload("W_qa", w_in, 0, D, O_QA, 512)
            W_qb = wload("W_qb", w_in, 0, D, O_QB, 512)
            W_rb = wload("W_rb", w_in, 0, D, O_RB, 1024)
            W_a2 = self.sb("W_a2", [GR, AW], BF16)
            self.dma('pool', W_a2[:], w_a2[:, :], [], [K0])
            Sst = [self.sb(f"Sst{h}", [128, BV], F32) for h in range(BH)]
            Sbf = [self.sb(f"Sbf{h}", [128, BV], BF16) for h in range(BH)]
            for h in range(BH):
                S.op('pool', lambda h=h: nc.gpsimd.memset(Sst[h][:], 0.0), [], [('S', h)])
            R = {
                'x': self.ring("xt", 2, [128, D], F32), 'junk': self.ring("junk", 1, [128, D], BF16),
                'ss': self.ring("ss", 4, [128, 4], F32), 'hn': self.ring("hn", 2, [128, D], BF16),
                'hnT': self.ring("hnT", 2, [128, 8, 128], BF16), 'ktb': self.ring("ktb", 2, [128, 4, 128], BF16),
                'vx': self.ring("vx", 2, [128, AH, 65], BF16), 'o512': self.ring("o512", 2, [128, 512], F32),
                'sm': self.ring("sm", 4, [128, 8], F32), 'alT': self.ring("alT", 2, [GR, 128], BF16),
                'a2s': self.ring("a2s", 2, [128, 512], F32), 'lsp': self.ring("lsp", 2, [128, 512], BF16),
                'ekd': self.ring("ekd", 2, [128, 512], F32), 'kd': self.ring("kd", 2, [128, 512], BF16),
                'vb': self.ring("vb", 2, [128, 1024], BF16), 'dl': self.ring("dl", 2, [128, 16], F32),
                'u2': self.ring("u2", 2, [128, BV], F32), 'qd': self.ring("qd", 2, [128, 512], BF16),
                'xT': self.ring("xT", 2, [128, 4, 128], BF16), 'AT': self.ring("AT", 2, [128, 4, 128], BF16),
                'f1k': self.ring("f1k", 2, [128, 1024], F32), 'b1k': self.ring("b1k", 2, [128, 1024], BF16),
                'qtb': self.ring("qtb", 2, [128, 4, 128], BF16), 'obTb': self.ring("obTb", 2, [128, 8, 128], BF16),
            }
            for t in R['vx'].tiles:
                S.op('pool', lambda t=t: nc.gpsimd.memset(t[:], 1.0), [], [('vx', R['vx'].tiles.index(t))])

            def mixer_block(hnT, tk, P, n, own, sample=False, outs=None, qa_sink=None):
                res = {}
                ps, pk = self.psn()
                for fc in range(4):
                    proj_feat(hnT, tk, P, W_kv, fc * 128, 128, ps, pk, pcol=fc * P)
                if sample:
                    self.cp('act', KTs[:, :, :], ps[:, 0:4 * P].rearrange("p (f t) -> p f t", f=4), [pk], [('KTs',)])
                else:
                    ktb, kk = R['ktb'].next()
                    self.cp('act', ktb[:], ps[:, :].rearrange("p (f t) -> p f t", f=4), [pk], [kk])
                    self.dma('sp', kt_scr[:, :, n * 128:(n + 1) * 128], ktb[:], [kk], [('kt_scr',)])
                ps, pk = self.psn()
                proj_tok(hnT, tk, P, W_kv, 512, 512, ps, pk)
                if sample:
                    self.cp('dve', VXs[0:P, :, 0:64], ps[0:P, :].rearrange("p (h d) -> p h d", h=AH), [pk], [('VXs',)])
                else:
                    vx, vk = R['vx'].next()
                    self.cp('dve', vx[0:P, :, 0:64], ps[0:P, :].rearrange("p (h d) -> p h d", h=AH), [pk], [vk])
                    self.dma('sp', vx_scr[n, :, :], vx[:].rearrange("p h d -> p (h d)"), [vk], [('vx_scr',)])
                if outs is not None:
                    o5, ok_ = R['o512'].next()
                    self.cp('act', o5[0:P, :], ps[0:P, :], [pk], [ok_])
                    self.out_events.append(self.dma('sp', outs['v'], o5[0:P, :], [ok_], []))
                    ps2, pk2 = self.psn()
                    proj_tok(hnT, tk, P, W_kv, 0, 512, ps2, pk2)
                    o5, ok_ = R['o512'].next()
                    self.cp('act', o5[0:P, :], ps2[0:P, :], [pk2], [ok_])
                    self.out_events.append(self.dma('sp', outs['k'], o5[0:P, :], [ok_], []))
                ps, pk = self.psn()
                proj_tok(hnT, tk, P, W_f, 0, 8, ps, pk)
                sm, smk = R['sm'].next()
                self.tt('dve', sm[0:P, :], ps[0:P, 0:8], bft[0:P, :], ALU.add, [pk, K0], [smk])
                sm2, smk2 = R['sm'].next()
                sp_t, spk = R['sm'].next()
                softplus_neg(sp_t[0:P, :], spk, sm[0:P, :], smk, sm2[0:P, :], smk2)
                if outs is not None:
                    lf, lfk = R['sm'].next()
                    self.ts('dve', lf[0:P, :], sp_t[0:P, :], -1.0, None, ALU.mult, None, [spk], [lfk])
                    self.out_events.append(self.dma('sp', outs['lf'], lf[0:P, :], [lfk], []))
                ps, pk = self.psn()
                proj_feat(hnT, tk, P, W_al, 0, GR, ps, pk)
                alT, ak = R['alT'].next()
                self.cp('act', alT[:, 0:P], ps[0:GR, 0:P], [pk], [ak])
                ps, pk = self.psn()
                self.mm(ps[0:P, :], alT[:, 0:P], W_a2[:, :], True, True, [ak, K0], [pk])
                a2s, a2k = R['a2s'].next()
                self.tt('dve', a2s[0:P, :], ps[0:P, :], balt[0:P, :], ALU.add, [pk, K0], [a2k])
                ekd, ek = R['ekd'].next()
                lsp, lk = R['lsp'].next()
                softplus_neg(lsp[0:P, :], lk, a2s[0:P, :], a2k, ekd[0:P, :], ek)
                vb, vbk = R['vb'].next()
                for half in range(2):
                    ps, pk = self.psn()
                    proj_tok(hnT, tk, P, W_vb, half * 512, 512, ps, pk)
                    self.cp('act' if half else 'dve', vb[0:P, half * 512:(half + 1) * 512], ps[0:P, :], [pk], [vbk])
                if own:
                    sr, srk = R['f1k'].next()
                    for half in range(2):
                        ps, pk = self.psn()
                        proj_tok(hnT, tk, P, W_rb, half * 512, 512, ps, pk)
                        self.act(sr[0:P, half * 512:(half + 1) * 512], ps[0:P, :], AF.Silu, [pk], [srk])
                    res['sr'] = (sr, srk)
                    ps, pk = self.psn()
                    for fc in range(4):
                        proj_feat(hnT, tk, P, W_qa, fc * 128, 128, ps, pk, pcol=fc * P)
                    qa_sink(ps, pk)
                ps, pk = self.psn()
                if sample:
                    self.mm(ps[0:P, 0:8], tris_f[0:P, 0:P], sp_t[0:P, :], True, True, [spk, K0], [pk])
                    self.cp('dve', Csn[:, :], ps[0:P, 0:8], [pk], [('Csn',)])
                else:
                    self.mm(ps[:, 0:8], tri_f[:], sp_t[:, :], True, True, [spk, K0], [pk])
                    self.mm(ps[:, 8:16], ones_f[:], sp_t[:, :], True, True, [spk, K0], [pk])
                    self.tt('dve', Ccum[:, n, :], ps[:, 0:8], carry[:, n, :], ALU.add, [pk, ('carry', n)], [('Ccum', n)])
                    self.tt('dve', carry[:, n + 1, :], ps[:, 8:16], carry[:, n, :], ALU.add, [pk, ('carry', n)], [('carry', n + 1)])
                psb, pbk = self.psn()
                tn = trins_b if sample else trin_b
                self.mm(psb[0:P, :], tn[0:P, 0:P], lsp[0:P, :], True, True, [lk, K0], [pbk])
                ekd, ek = R['ekd'].next()
                self.act(ekd[0:P, :], psb[0:P, :], AF.Exp, [pbk], [ek], scale=-1.0)
                ps, pk = self.psn()
                proj_tok(hnT, tk, P, W_kb, 0, 512, ps, pk)
                kd, kdk = R['kd'].next()
                self.tt('dve', kd[0:P, :], ps[0:P, :], ekd[0:P, :], ALU.mult, [pk, ek], [kdk])
                res.update(lsp=(lsp, lk), kd=(kd, kdk), vb=(vb, vbk))
                if own:
                    eq, eqk = R['a2s'].next()
                    self.act(eq[0:P, :], psb[0:P, :], AF.Exp, [pbk], [eqk])
                    ps, pk = self.psn()
                    proj_tok(hnT, tk, P, W_qb, 0, 512, ps, pk)
                    qd, qk = R['qd'].next()
                    self.stt('dve', qd[0:P, :], ps[0:P, :], BK ** -0.5, eq[0:P, :], ALU.mult, ALU.mult, [pk, eqk], [qk])
                    tiles = []
                    for src, sk_ in ((qd, qk), (kd, kdk)):
                        xT, xk_ = R['xT'].next()
                        transpose_into(xT[:, :, 0:P], xk_, src, sk_, P, 4)
                        tiles.append((xT, xk_))
                    (qdT, qtk), (kdT, ktk) = tiles
                    ps, pk = self.psn()
                    for h in range(BH):
                        self.mm(ps[0:P, h * P:(h + 1) * P], kdT[:, h, 0:P], qdT[:, h, 0:P], True, True, [ktk, qtk], [pk])
                    AT, atk = R['AT'].next()
                    msk = tris_b if sample else tri_b
                    self.tt('dve', AT[0:P, :, 0:P], ps[0:P, 0:4 * P].rearrange("p (h t) -> p h t", h=4),
                            msk[0:P, 0:P].unsqueeze(1).to_broadcast([P, 4, P]), ALU.mult, [pk, K0], [atk])
                    res.update(qdT=(qdT, qtk), AT=(AT, atk))
                return res

            def gla_out(res, P, S_list):
                AT, atk = res['AT']
                vb, vbk = res['vb']
                sr, srk = res['sr']
                ob, obk = R['f1k'].next()
                ss, sk = R['ss'].next()
                junk, jk = R['junk'].next()
                pss = []
                for h in range(BH):
                    if h % 2 == 0:
                        ps, pk = self.psn()
                        pss.append((ps, pk))
                    c0 = (h % 2) * BV
                    self.mm(ps[0:P, c0:c0 + BV], AT[0:P, h, 0:P], vb[0:P, h * BV:(h + 1) * BV], True, False, [atk, vbk], [pk])
                    terms = S_list(h)
                    for i, (lq, lqk, sbf, sbk) in enumerate(terms):
                        self.mm(ps[0:P, c0:c0 + BV], lq, sbf, False, i == len(terms) - 1, [lqk, sbk], [pk])
                    self.act(junk[0:P, 0:BV], ps[0:P, c0:c0 + BV], AF.Square, [pk], [jk, sk], accum=ss[0:P, h:h + 1])
                sq, sqk = R['ss'].next()
                self.ts('dve', sq[0:P, :], ss[0:P, :], 1.0 / BV, EPS, ALU.mult, ALU.add, [sk], [sqk])
                self.rsqrt_col(sq[0:P, :], sq[0:P, :], sqk)
                for h in range(BH):
                    ps, pk = pss[h // 2]
                    c0 = (h % 2) * BV
                    self.stt('dve', ob[0:P, h * BV:(h + 1) * BV], ps[0:P, c0:c0 + BV], sq[0:P, h:h + 1], ggl[0:P, :],
                             ALU.mult, ALU.mult, [pk, sqk, K0], [obk])
                ob2, o2k = R['b1k'].next()
                self.tt('dve', ob2[0:P, :], ob[0:P, :], sr[0:P, :], ALU.mult, [obk, srk], [o2k])
                return ob2, o2k

            def state_update(res, P, h, lhs_kd, lkk, dlcol, dlk, S_in, S_in_k, S_out, S_out_k):
                vb, vbk = res['vb']
                ps, pk = self.psn()
                self.mm(ps[:, 0:BV], lhs_kd, vb[0:P, h * BV:(h + 1) * BV], True, True, [lkk, vbk], [pk])
                u2, uk = R['u2'].next()
                self.act(u2[:], ps[:, 0:BV], AF.Copy, [pk, dlk], [uk], scale=dlcol)
                self.stt('dve', S_out, S_in, dlcol, u2[:], ALU.mult, ALU.add, [S_in_k, dlk, uk], [S_out_k])

            for n in range(NB):
                own = n >= OWN0
                xt, xk = R['x'].next()
                self.dma('sp', xt[:, :], xc[n * 128:(n + 1) * 128, :], [], [xk])
                hnT, tk = R['hnT'].next()
                norm_T(xt[:, :], xk, 128, gpre, hnT[:, :, :], tk, R)
                outs = None
                if n > OWN0:
                    r0 = (n - OWN0 - 1) * 128
                    outs = {'k': o_k[r0:r0 + 128, :], 'v': o_v[r0:r0 + 128, :], 'lf': o_lf[r0:r0 + 128, :]}
                def qa_sink(qps, qpk, n=n):
                    qtb, qbk = R['qtb'].next()
                    self.ts('dve', qtb[:], qps[:, :].rearrange("p (f t) -> p f t", f=4), AD ** -0.5, None,
                            ALU.mult, None, [qpk], [qbk])
                    c0_ = (n - OWN0) * 128
                    self.dma('sp', qt_scr[:, :, c0_:c0_ + 128], qtb[:], [qbk], [('qt_scr',)])
                res = mixer_block(hnT, tk, 128, n, own, outs=outs, qa_sink=qa_sink)
                lsp, lk = res['lsp']
                kd, kdk = res['kd']
                ps, pk = self.psn()
                for h in range(BH):
                    self.mm(ps[:, h:h + 1], lsp[:, h * 128:(h + 1) * 128], onesn_b[:, 0:1], True, True, [lk, K0], [pk])
                dl, dlk = R['dl'].next()
                self.act(dl[:, 0:4], ps[:, 0:4], AF.Exp, [pk], [dlk])
                if own:
                    col0 = (n - OWN0) * 128
                    for h in range(BH):
                        self.cp('pool', Sbf[h][:], Sst[h][:], [('S', h)], [('Sbf', h)])
                    qdT, qtk = res['qdT']
                    ob2, o2k = gla_out(res, 128, lambda h: [(qdT[:, h, :], qtk, Sbf[h][:], ('Sbf', h))])
                    obTb, obk_ = R['obTb'].next()
                    transpose_into(obTb[:, :, :], obk_, ob2, o2k, 128, 8)
                    self.dma('sp', obt_scr[:, :, col0:col0 + 128], obTb[:], [obk_], [('obt_scr',)])
                for h in range(BH):
                    state_update(res, 128, h, kd[:, h * 128:(h + 1) * 128], kdk, dl[:, h:h + 1], dlk,
                                 Sst[h][:], ('S', h), Sst[h][:], ('S', h))
            for h in range(BH):
                self.out_events.append(self.dma('sp', o_gla[h * 128:(h + 1) * 128, :], Sst[h][:], [('S', h)], []))

            if STOP >= 2:
                S0 = self.sb("S0", [128, NSQ * BH, BV], F32)
                S0b = self.sb("S0b", [128, NSQ * BH, BV], BF16)
                self.dma('sp', S0[:], sgla_d.rearrange("(g p) v -> p g v", p=128), [], [('S0',)])
                self.cp('pool', S0b[:], S0[:], [('S0',)], [('S0b',)])
                xt, xk = R['x'].next()
                self.dma('sp', xt[0:32, :], xs[:, :], [], [xk])
                hnT, tk = R['hnT'].next()
                norm_T(xt[0:32, :], xk, 32, gpre, hnT[:, :, 0:32], tk, R)
                outs = {'k': o_ks[:, :], 'v': o_vs[:, :], 'lf': o_lfs[:, :]}
                def qa_sink_s(qps, qpk):
                    self.ts('dve', QTs[:], qps[:, 0:128].rearrange("p (f t) -> p f t", f=4), AD ** -0.5, None,
                            ALU.mult, None, [qpk], [('QTs',)])
                res = mixer_block(hnT, tk, 32, None, True, sample=True, outs=outs, qa_sink=qa_sink_s)
                qdT, qtk = res['qdT']
                qdTm = self.sb("qdTm", [128, NSQ, 4, 32], BF16)
                for s in range(NSQ):
                    self.tt('dve', qdTm[:, s, :, :], qdT[:, :, 0:32], colmask[:, s, :].unsqueeze(1).to_broadcast([128, 4, 32]),
                            ALU.mult, [qtk, K0], [('qdTm',)])
                ob2, o2k = gla_out(res, 32, lambda h: [(qdTm[:, s, h, :], ('qdTm',), S0b[:, s * BH + h, :], ('S0b',)) for s in range(NSQ)])
                transpose_into(obTs[:, :, :], ('obTs',), ob2, o2k, 32, 8)
                lsp, lk = res['lsp']
                kd, kdk = res['kd']
                ps, pk = self.psn()
                for h in range(BH):
                    self.mm(ps[:, h * 4:(h + 1) * 4], lsp[0:32, h * 128:(h + 1) * 128], seqindn_b[0:32, 0:4], True, True, [lk, K0], [pk])
                dl, dlk = R['dl'].next()
                self.act(dl[:, 0:16], ps[:, 0:16], AF.Exp, [pk], [dlk])
                kdm = self.sb("kdm", [32, NSQ, 512], BF16)
                for s in range(NSQ):
                    self.ts('dve', kdm[:, s, :], kd[0:32, :], seqind_f[:, s:s + 1], None, ALU.mult, None, [kdk, K0], [('kdm',)])
                Sn = self.ring("Sn", 2, [128, BV], F32)
                for s in range(NSQ):
                    for h in range(BH):
                        sn, snk = Sn.next()
                        g = s * BH + h
                        state_update(res, 32, h, kdm[:, s, h * 128:(h + 1) * 128], ('kdm',), dl[:, h * 4 + s:h * 4 + s + 1], dlk,
                                     S0[:, g, :], ('S0',), sn[:], snk)
                        self.out_events.append(self.dma('sp', o_glas[g * 128:(g + 1) * 128, :], sn[:], [snk], []))

        if STOP <= 2:
            return self.finish_all()

        groups = [(OWN0, 1)] + [(OWN0 + 1 + g * QG, QG) for g in range(NQ // QG)]
        NG = len(groups)
        self.psring = Ring(banks[0:4], "ps")
        with self.phase():
            biasAll = self.sb("biasAll", [128, NG, NB, AH], F32)
            for gi, (n0, nq) in enumerate(groups):
                nk = n0 + nq
                self.tt('dve', biasAll[:, gi, 0:nk, :], Ccum[:, 0:nk, :], carry[:, n0, :].unsqueeze(1).to_broadcast([128, nk, AH]),
                        ALU.subtract, [], [('bias', gi)])
                self.tt('dve', biasAll[:, gi, 0:nk, :], biasAll[:, gi, 0:nk, :], kmask[:, 0:nk].unsqueeze(2).to_broadcast([128, nk, AH]),
                        ALU.add, [('bias', gi)], [('bias', gi)])
            R_kt = self.ring("KT", 2, [128, TT], BF16)
            R_vxh = self.ring("VXh", 2, [128, NB, 2, 128], BF16)
            for t in R_vxh.tiles:
                ti = R_vxh.tiles.index(t)
                S.op('pool', lambda t=t: nc.gpsimd.memset(t[:], 0.0), [], [('VXh', ti)])
                S.op('pool', lambda t=t: nc.gpsimd.memset(t[:, :, :, 64:65], 1.0), [('VXh', ti)], [('VXh', ti)])
            ones_b = self.sb("ones_b", [65, 64], BF16)
            S.op('pool', lambda: nc.gpsimd.memset(ones_b[:], 1.0), [], [('ones_b',)])
            R_qth = self.ring("QTh", 2, [128, 2, TO], BF16)
            for t in R_qth.tiles:
                S.op('pool', lambda t=t: nc.gpsimd.memset(t[:], 0.0), [], [('QTh', R_qth.tiles.index(t))])
            R_pt = self.ring("pt", 8, [128, QG * 128], BF16)
            R_rec = self.ring("rec", 2, [65, 2 * QG * 128], BF16)
            R_recf = self.ring("recf", 2, [65, QG * 128], F32)
            R_rb = self.ring("rbc", 2, [64, QG * 128], F32)
            R_oT = self.ring("oT", 3, [64, QG * 128], BF16)
            accs = [(banks[4], ('ps', 4)), (banks[5], ('ps', 5))]
            LOOK = 4

            def prompt_gen():
                acci = 0
                for hp in range(4):
                    KT, ktk = R_kt.next()
                    VX, vxk = R_vxh.next()
                    QTh, qhk = R_qth.next()
                    for c in range(0, TT, 2048):
                        self.dma('sp', KT[:, c:c + 2048], kt_scr[:, hp, c:c + 2048], [], [ktk])
                    for c in range(0, NB, 8):
                        for hh in range(2):
                            a = (2 * hp + hh) * 65
                            self.dma('sp', VX[:, c:c + 8, hh, 0:64], vx_scr[c:c + 8, :, a:a + 64].rearrange("n p c -> p n c"), [], [vxk])
                    for hh in range(2):
                        self.dma('sp', QTh[hh * 64:(hh + 1) * 64, hh, :], qt_scr[hh * 64:(hh + 1) * 64, hp, :], [], [qhk])
                    units = []
                    for gi, (n0, nq) in enumerate(groups):
                        for hh in range(2):
                            for n in range(n0 + nq):
                                units.append((gi, n0, nq, hh, n))
                    pend = {}
                    accof = {}
                    evs = {}

                    def emit_qk(u):
                        gi, n0, nq, hh, n = units[u]
                        W = nq * 128
                        c0 = (n0 - OWN0) * 128
                        h = 2 * hp + hh
                        prs = slice(hh * 64, (hh + 1) * 64)
                        lo = max(n - n0, 0) * 128
                        psS, psk = self.psn()
                        self.mm(psS[:, lo:W], KT[:, n * 128:(n + 1) * 128], QTh[:, hh, c0 + lo:c0 + W], True, True, [ktk, qhk], [psk])
                        pt, ptk = R_pt.next()
                        self.act(pt[:, lo:W], psS[:, lo:W], AF.Exp, [psk, ('bias', gi)], [ptk], bias=biasAll[:, gi, n, h:h + 1])
                        if n >= n0:
                            self.tt('dve', pt[:, lo:lo + 128], pt[:, lo:lo + 128], tri_b[:, :], ALU.mult, [ptk, K0], [ptk])
                        pend[u] = (pt, ptk)
                        evs[u] = [('e', 'act', S.cnt['act']), ('e', 'dve', S.cnt['dve']) if n >= n0 else None]

                    def emit_pv(u):
                        nonlocal acci
                        gi, n0, nq, hh, n = units[u]
                        W = nq * 128
                        c0 = (n0 - OWN0) * 128
                        lo = max(n - n0, 0) * 128
                        if n == 0:
                            accof[(gi, hh)] = accs[acci % 2]
                            acci += 1
                        psO, pok = accof[(gi, hh)]
                        pt, ptk = pend.pop(u)
                        last = (n == n0 + nq - 1)
                        self.mm(psO[:, lo:W], VX[:, n, hh, :], pt[:, lo:W], n == 0, last, [ptk, vxk], [pok])
                        if last:
                            rf, rfk = R_recf.next()
                            self.ts('dve', rf[64:65, 0:W], psO[64:65, 0:W], 1e-30, None, ALU.max, None, [pok], [rfk])
                            S.op('dve', lambda rf=rf, W=W: nc.vector.reciprocal(out=rf[64:65, 0:W], in_=rf[64:65, 0:W]), [rfk], [rfk])
                            rec, rk = R_rec.next()
                            self.cp('dve', rec[64:65, 0:W], rf[64:65, 0:W], [rfk], [rk])
                            self.tt('dve', rec[64:65, W:2 * W], rf[64:65, 0:W], rec[64:65, 0:W], ALU.subtract, [rfk, rk], [rk])
                            psB, pbk = self.psn()
                            self.mm(psB[0:64, 0:W], ones_b[64:65, 0:64], rec[64:65, 0:W], True, False, [rk, ('ones_b',)], [pbk])
                            self.mm(psB[0:64, 0:W], ones_b[64:65, 0:64], rec[64:65, W:2 * W], False, True, [rk, ('ones_b',)], [pbk])
                            rb, rbk = R_rb.next()
                            self.cp('act', rb[0:64, 0:W], psB[0:64, 0:W], [pbk], [rbk])
                            oT, otk = R_oT.next()
                            self.tt('dve', oT[:, 0:W], psO[0:64, 0:W], rb[0:64, 0:W], ALU.mult, [pok, rbk], [otk])
                            self.dma('sp', oat_scr[hh * 64:(hh + 1) * 64, hp, c0:c0 + W], oT[:, 0:W], [otk], [('oat_scr',)])

                    NU = len(units)
                    for i in range(0, NU + LOOK + 1, 2):
                        hi_q = min(i + 1, NU - 1)
                        if i < NU and hi_q - 4 >= 0 and (hi_q - 4) in evs:
                            S.need('pe', evs[hi_q - 4][0])
                        for u in (i, i + 1):
                            if u < NU:
                                emit_qk(u)
                        v0 = i - LOOK - 1
                        hi_v = min(v0 + 1, NU - 1)
                        if hi_v >= 0 and hi_v in evs:
                            for ev in evs[hi_v]:
                                if ev is not None:
                                    S.need('pe', ev)
                        for v in (v0, v0 + 1):
                            if 0 <= v < NU:
                                emit_pv(v)
                        yield

            n_prompt_steps = 4 * ((sum(2 * (n0 + nq) for (n0, nq) in groups) + LOOK + 2) // 2)

            def sample_gen():
                ptb = self.sb("ptb", [128, NSQ * NPG], I32)
                idx = self.sb("idx", [128, NSQ * NPG], I32)
                self.dma('sp', ptb[:], pt_d.partition_broadcast(128), [], [('ptb',)], slow=True)
                ptf = self.sb("ptf", [128, NSQ * NPG], F32)
                iof = self.sb("iof", [128, 1], F32)
                self.cp('dve', ptf[:], ptb[:], [('ptb',)], [('ptf',)])
                self.cp('dve', iof[:], iota_i[:], [K0], [('iof',)])
                self.stt('dve', ptf[:], ptf[:], 128.0, iof[:, 0:1].to_broadcast([128, NSQ * NPG]), ALU.mult, ALU.add,
                         [('ptf',), ('iof',)], [('ptf',)])
                self.cp('dve', idx[:], ptf[:], [('ptf',)], [('idx',)])
                qblk = self.sb("qblk", [128, NSQ, 4, 16], BF16)
                S.op('pool', lambda: nc.gpsimd.memset(qblk[:], 0.0), [], [('qblk',)])
                for s in range(NSQ):
                    for hh in range(2):
                        self.cp('dve', qblk[hh * 64:(hh + 1) * 64, s, :, hh * 8:(hh + 1) * 8], QTs[hh * 64:(hh + 1) * 64, :, s * 8:(s + 1) * 8],
                                [('QTs',), ('qblk',)], [('qblk',)])
                R_kr = self.ring("kraw", 3, [128, AW], F32)
                R_vr = self.ring("vraw", 3, [128, AW], F32)
                R_lr = self.ring("lraw", 4, [128, AH], F32)
                R_ktp = self.ring("ktp", 2, [128, 4, 128], BF16)
                R_vxp = self.ring("vxp", 2, [128, AH, 65], BF16)
                for t in R_vxp.tiles:
                    S.op('pool', lambda t=t: nc.gpsimd.memset(t[:], 1.0), [], [('vxp', R_vxp.tiles.index(t))])
                R_sfx = self.ring("sfx", 3, [128, AH], F32)
                R_cs = self.ring("cs", 3, [128, AH], F32)
                R_e = self.ring("e64", 3, [128, AH, 8], F32)
                R_p64 = self.ring("p64", 3, [128, 64], BF16)
                R_os = self.ring("os", 2, [8, AH, 64], BF16)
                R_rs = self.ring("rs", 2, [8, AH], F32)
                accs2 = [(banks[6], ('ps', 6)), (banks[7], ('ps', 7))]
                yield
                order = [(s, pg) for s in range(NSQ) for pg in range(NPG - 1, -1, -1)]
                raw = {}

                def gather(i):
                    s, pg = order[i]
                    col = s * NPG + pg
                    kr, krk = R_kr.next()
                    vr, vrk = R_vr.next()
                    lr, lrk = R_lr.next()
                    for (dst, dk_, src) in ((lr, lrk, cache_lf), (kr, krk, cache_k), (vr, vrk, cache_v)):
                        S.dma('pool', lambda dst=dst, src=src, col=col: nc.gpsimd.indirect_dma_start(
                            out=dst[:, :], out_offset=None, in_=src[:, :],
                            in_offset=bass.IndirectOffsetOnAxis(ap=idx[:, col:col + 1], axis=0)), [('idx',)], [dk_])
                    raw[i] = (kr, krk, vr, vrk, lr, lrk)

                gather(0)
                cs = csk = None
                for i, (s, pg) in enumerate(order):
                    if i + 1 < len(order):
                        gather(i + 1)
                    kr, krk, vr, vrk, lr, lrk = raw.pop(i)
                    first = (pg == NPG - 1)
                    if first:
                        cs, csk = R_cs.next()
                        S.op('dve', lambda cs=cs: nc.vector.memset(cs[:], 0.0), [], [csk])
                    ps, pk = self.psn()
                    self.mm(ps[:, 0:8], trigt_f[:], lr[:, :], True, True, [lrk, K0], [pk])
                    self.mm(ps[:, 8:16], ones_f[:], lr[:, :], True, True, [lrk, K0], [pk])
                    sfx, sxk = R_sfx.next()
                    self.tt('dve', sfx[:], ps[:, 0:8], cs[:], ALU.add, [pk, csk], [sxk])
                    cs2, csk2 = R_cs.next()
                    self.tt('dve', cs2[:], ps[:, 8:16], cs[:], ALU.add, [pk, csk], [csk2])
                    cs, csk = cs2, csk2
                    ps, pk = self.psn()
                    for hp in range(4):
                        self.tr(ps[:, hp * 128:(hp + 1) * 128], kr[:, hp * 128:(hp + 1) * 128], identf[:, :], [krk, K0], [pk])
                    ktp, kpk = R_ktp.next()
                    self.cp('act', ktp[:], ps[:, :].rearrange("p (f t) -> p f t", f=4), [pk], [kpk])
                    psS, psk = self.psn()
                    for hp in range(4):
                        self.mm(psS[:, hp * 16:(hp + 1) * 16], ktp[:, hp, :], qblk[:, s, hp, :], True, True, [kpk, ('qblk',)], [psk])
                    e, ek_ = R_e.next()
                    self.tt('dve', e[:], psS[:, 0:64].rearrange("p (h q) -> p h q", h=AH), sfx[:].unsqueeze(2).to_broadcast([128, AH, 8]),
                            ALU.add, [psk, sxk], [ek_])
                    p64, p6k = R_p64.next()
                    self.act(p64[:], e[:].rearrange("p h q -> p (h q)"), AF.Exp, [ek_], [p6k])
                    vxp, vpk = R_vxp.next()
                    self.cp('dve', vxp[:, :, 0:64], vr[:, :].rearrange("p (h d) -> p h d", h=AH), [vrk], [vpk])
                    for h in range(AH):
                        psO, pok = accs2[h // 4]
                        cc = (h % 4) * 65
                        self.mm(psO[0:8, cc:cc + 65], p64[:, h * 8:(h + 1) * 8], vxp[:, h, :], first and h % 4 == 0, False, [p6k, vpk], [pok])
                    if pg == 0:
                        psS, psk = self.psn()
                        for hp in range(4):
                            self.mm(psS[0:32, hp * 16:(hp + 1) * 16], KTs[:, hp, :], qblk[:, s, hp, :], True, True, [('KTs',), ('qblk',)], [psk])
                        e, ek_ = R_e.next()
                        self.tt('dve', e[0:32], psS[0:32, 0:64].rearrange("p (h q) -> p h q", h=AH), Csn[:, :].unsqueeze(2).to_broadcast([32, AH, 8]),
                                ALU.add, [psk, ('Csn',)], [ek_])
                        p64f = R_e.next()
                        self.act(p64f[0][0:32], e[0:32], AF.Exp, [ek_], [p64f[1]])
                        p64, p6k = R_p64.next()
                        self.tt('dve', p64[0:32, :].rearrange("p (h q) -> p h q", h=AH), p64f[0][0:32],
                                masknew[:, s, :].unsqueeze(1).to_broadcast([32, AH, 8]), ALU.mult, [p64f[1], K0], [p6k])
                        for h in range(AH):
                            psO, pok = accs2[h // 4]
                            cc = (h % 4) * 65
                            self.mm(psO[0:8, cc:cc + 65], p64[0:32, h * 8:(h + 1) * 8], VXs[0:32, h, :], False, True, [p6k, ('VXs',)], [pok])
                        os_, osk = R_os.next()
                        rs, rsk = R_rs.next()
                        for half in range(2):
                            psO, pok = accs2[half]
                            pv = psO[0:8, 0:260].rearrange("p (h e) -> p h e", e=65)
                            self.ts('dve', rs[:, half * 4:(half + 1) * 4].unsqueeze(2), pv[:, :, 64:65], 1e-30, None, ALU.max, None, [pok], [rsk])
                        S.op('dve', lambda rs=rs: nc.vector.reciprocal(out=rs[:, :], in_=rs[:, :]), [rsk], [rsk])
                        for half in range(2):
                            psO, pok = accs2[half]
                            pv = psO[0:8, 0:260].rearrange("p (h e) -> p h e", e=65)
                            self.tt('dve', os_[:, half * 4:(half + 1) * 4, :], pv[:, :, 0:64],
                                    rs[:, half * 4:(half + 1) * 4].unsqueeze(2).to_broadcast([8, 4, 64]), ALU.mult, [pok, rsk], [osk])
                        ps, pk = self.psn()
                        pvb = bview(ps)
                        osf = os_[:].rearrange("p h d -> p (h d)")
                        for fc in range(4):
                            self.tr(pvb[:, fc * 8:(fc + 1) * 8], osf[:, fc * 128:(fc + 1) * 128], ident[0:8, 0:8], [osk, K0], [pk])
                        self.cp('act', oaTs[:, :, s * 8:(s + 1) * 8], pvb[:, 0:32].rearrange("p (f t) -> p f t", f=4), [pk], [('oaTs',)])
                    yield

            sg_ = sample_gen() if STOP >= 4 else iter(())
            n_s = NSQ * NPG + 1
            done_s = 0
            for i, _ in enumerate(prompt_gen()):
                target = (i + 1) * n_s / n_prompt_steps
                while done_s < target:
                    next(sg_, None)
                    done_s += 1
            for _ in sg_:
                pass
        self.psring = Ring(banks, "ps")
        if STOP <= 4:
            return self.finish_all()

        with self.phase():
            gpre = gload("gpre4", g_pre_mix)
            gpm = gload("gpm", g_post_mix)
            W_ga = wload("W_ga", w_in, 0, D, O_GA, 1024)
            W_gb = wload("W_gb", w_in, 0, D, O_GB, 1024)
            W_pa = wload("W_pa", w_pa, 0, AW, 0, 1024)
            W_pb = wload("W_pb", w_pb, 0, D, 0, 1024)
            W_o = wload("W_o", w_out, 0, D, 0, 1024)
            R = {
                'x': self.ring("xt4", 2, [128, D], F32), 'junk': self.ring("junk4", 1, [128, D], BF16),
                'ss': self.ring("ss4", 4, [128, 4], F32), 'hn': self.ring("hn4", 2, [128, D], BF16),
                'hnT': self.ring("hnT4", 2, [128, 8, 128], BF16),
            }
            R_sg = self.ring("sg", 2, [128, 2, D], F32)
            R_oa = self.ring("oa4", 2, [128, 4, 128], BF16)
            R_ob = self.ring("ob4", 2, [128, 8, 128], BF16)
            R_m = self.ring("m4", 2, [128, D], F32)
            R_mb = self.ring("mb4", 2, [128, D], BF16)
            R_mT = self.ring("mT4", 2, [128, 8, 128], BF16)
            R_xm = self.ring("xm4", 2, [128, D], F32)
            tiles = [(xc[n * 128:(n + 1) * 128, :], 128, (n - OWN0) * 128, None) for n in range(OWN0, NB)]
            tiles.append((xs[:, :], 32, TO, 'sample'))
            def p4_front(src, P, r0, kind):
                xt, xk = R['x'].next()
                self.dma('sp', xt[0:P, :], src, [], [xk])
                hnT, tk = R['hnT'].next()
                norm_T(xt[0:P, :], xk, P, gpre, hnT[:, :, 0:P], tk, R)
                sg, sgk = R_sg.next()
                for gi, Wg in enumerate((W_ga, W_gb)):
                    for half in range(2):
                        ps, pk = self.psn()
                        proj_tok(hnT, tk, P, Wg, half * 512, 512, ps, pk)
                        self.act(sg[0:P, gi, half * 512:(half + 1) * 512], ps[0:P, :], AF.Sigmoid, [pk], [sgk])
                return (P, r0, kind, xt, xk, sg, sgk)

            def p4_back(st):
                P, r0, kind, xt, xk, sg, sgk = st
                if kind == 'sample':
                    oa, oak, ob_, obk = oaTs, ('oaTs',), obTs, ('obTs',)
                else:
                    oa, oak = R_oa.next()
                    ob_, obk = R_ob.next()
                    self.dma('sp', oa[:], oat_scr[:, :, r0:r0 + 128], [], [oak])
                    self.dma('sp', ob_[:], obt_scr[:, :, r0:r0 + 128], [], [obk])
                m, mk = R_m.next()
                mb, mbk = R_mb.next()
                for half in range(2):
                    cs_ = slice(half * 512, (half + 1) * 512)
                    ps, pk = self.psn()
                    proj_tok(oa, oak, P, W_pa, half * 512, 512, ps, pk, nk=4)
                    self.tt('dve', m[0:P, cs_], ps[0:P, :], sg[0:P, 0, cs_], ALU.mult, [pk, sgk], [mk])
                    ps, pk = self.psn()
                    proj_tok(ob_, obk, P, W_pb, half * 512, 512, ps, pk)
                    self.tt('dve', sg[0:P, 1, cs_], ps[0:P, :], sg[0:P, 1, cs_], ALU.mult, [pk, sgk], [sgk])
                    self.tt('dve', mb[0:P, cs_], m[0:P, cs_], sg[0:P, 1, cs_], ALU.add, [mk, sgk], [mbk])
                mT, mtk = R_mT.next()
                transpose_into(mT[:, :, 0:P], mtk, mb, mbk, P, 8)
                ss, sk = R['ss'].next()
                junk, jk = R['junk'].next()
                pss = []
                for half in range(2):
                    ps, pk = self.psn()
                    proj_tok(mT, mtk, P, W_o, half * 512, 512, ps, pk)
                    self.act(junk[0:P, 0:512], ps[0:P, :], AF.Square, [pk], [jk, sk], accum=ss[0:P, half:half + 1])
                    pss.append((ps, pk))
                self.tt('dve', ss[0:P, 2:3], ss[0:P, 0:1], ss[0:P, 1:2], ALU.add, [sk], [sk])
                self.ts('dve', ss[0:P, 2:3], ss[0:P, 2:3], 1.0 / D, EPS, ALU.mult, ALU.add, [sk], [sk])
                self.rsqrt_col(ss[0:P, 3:4], ss[0:P, 2:3], sk)
                xm, xmk = R_xm.next()
                for half in range(2):
                    cs_ = slice(half * 512, (half + 1) * 512)
                    ps, pk = pss[half]
                    self.stt('dve', xm[0:P, cs_], ps[0:P, :], ss[0:P, 3:4], gpm[0:P, cs_], ALU.mult, ALU.mult, [pk, sk, K0], [xmk])
                self.tt('dve', xm[0:P, :], xm[0:P, :], xt[0:P, :], ALU.add, [xmk, xk], [xmk])
                self.dma('sp', xm_scr[r0:r0 + P, :], xm[0:P, :], [xmk], [('xm_scr',)])

            prev4 = None
            for tl in tiles:
                st4 = p4_front(*tl)
                if prev4 is not None:
                    p4_back(prev4)
                prev4 = st4
            p4_back(prev4)

        if STOP <= 5:
            return self.finish_all()

        with self.phase():
            gpf = gload("gpf", g_pre_ffn)
            gpo = gload("gpo", g_post_ffn)
            wcv = self.sb("wcv", [128, 3, 2 * NCH], F32)
            bcv = self.sb("bcv", [128, 2 * NCH], F32)
            for t in range(3):
                self.dma('sp', wcv[:, t, :], w_conv[t:t + 1, :].rearrange("o (c p) -> p (o c)", p=128), [], [K0], slow=True)
            self.dma('sp', bcv[:], b_conv.rearrange("o (c p) -> p (o c)", p=128), [], [K0], slow=True)
            W_dn = wload("W_dn", w_down, 0, DFF, 0, 1024)
            GW = QG * 128
            R = {
                'junk': self.ring("junk5", 1, [128, D], BF16), 'ss': self.ring("ss5", 4, [128, 4], F32),
                'hn': self.ring("hn5", 2, [128, D], BF16),
            }
            xmg = self.sb("xmg", [128, QG, D], F32)
            h2T = self.sb("h2T", [128, 8, GW], BF16)
            hT = self.sb("hT", [128, NCH, GW], BF16)
            lbp = self.sb("lbp", [128, 2 * NCH, 1, 2], F32)
            lbs = self.sb("lbs", [128, 2 * NCH, NSQ, 2], F32)
            S.op('pool', lambda: nc.gpsimd.memset(lbp[:], 0.0), [], [('lbp',)])
            R_wu = self.ring("wu", 3, [128, 8, 256], BF16)
            R_uc = self.ring("uc", 3, [128, GW + 8], F32)
            R_c = self.ring("cv", 4, [128, GW], F32)
            R_t = self.ring("tg", 3, [128, GW], F32)
            R_tp = self.ring("tpl", 2, [128, GW], F32)
            R_y = self.ring("y5", 2, [128, D], F32)
            R_wt = self.ring("wt", 2, [128, 8, 512], BF16)
            R_ut = self.ring("ut", 2, [128, 512], F32)
            sct = self.sb("sct", [8, 1408], F32)
            ps, pk = self.psn()
            for piece in range(4):
                self.dma('sp', sct[:], sconv_d[:, piece * 1408:(piece + 1) * 1408], [], [('sct',)])
                for c in range(11):
                    ch = piece * 11 + c
                    self.tr(ps[:, ch * 8:(ch + 1) * 8], sct[0:8, c * 128:(c + 1) * 128], identf[0:8, 0:8], [('sct',), K0], [pk])
            self.cp('act', lbs[:].rearrange("p c s l -> p (c s l)"), ps[:, 0:2 * NCH * 8], [pk], [('lbs',)])

            fgroups = [(OWN0, 1, 1, 128, 'p')] + [(OWN0 + 1 + g * QG, QG, 1, QG * 128, 'p') for g in range(NQ // QG)]
            fgroups.append((None, 1, NSQ, 8, 's'))
            for (n0, nblk, nseq, L, kind) in fgroups:
                W = nseq * L
                lb, lbk = (lbp, ('lbp',)) if kind == 'p' else (lbs, ('lbs',))
                for b in range(nblk):
                    P = 128 if kind == 'p' else 32
                    r0 = (n0 + b - OWN0) * 128 if kind == 'p' else TO
                    self.dma('sp', xmg[0:P, b, :], xm_scr[r0:r0 + P, :], [], [('xmg', b)])
                    norm_T(xmg[0:P, b, :], ('xmg', b), P, gpf, h2T[:, :, b * 128:b * 128 + P], ('h2T',), R)
                for ch in range(NCH):
                    wu, wuk = R_wu.next()
                    self.dma('sp', wu[:, :, 0:128], wup_scr[:, :, ch * 128:(ch + 1) * 128], [], [wuk])
                    self.dma('sp', wu[:, :, 128:256], wup_scr[:, :, DFF + ch * 128:DFF + (ch + 1) * 128], [], [wuk])
                    cvs = []
                    for part in range(2):
                        ci = part * NCH + ch
                        ps, pk = self.psn()
                        for kc in range(8):
                            self.mm(ps[:, 0:W], wu[:, kc, part * 128:(part + 1) * 128], h2T[:, kc, 0:W], kc == 0, kc == 7, [wuk, ('h2T',)], [pk])
                        uc, uck = R_uc.next()
                        ucv = uc[:, 0:nseq * (L + 2)].rearrange("p (s l) -> p s l", s=nseq)
                        self.cp('act', ucv[:, :, 2:2 + L], ps[:, 0:W].rearrange("p (s l) -> p s l", s=nseq), [pk], [uck])
                        self.cp('pool', ucv[:, :, 0:2], lb[:, ci, :, :], [lbk], [uck])
                        cv, cvk = R_c.next()
                        cvv = cv[:, 0:W].rearrange("p (s l) -> p s l", s=nseq)
                        self.act(cvv, ucv[:, :, 2:2 + L], AF.Identity, [uck, K0], [cvk], bias=bcv[:, ci:ci + 1], scale=wcv[:, 2, ci:ci + 1])
                        self.stt('dve', cvv, ucv[:, :, 1:1 + L], wcv[:, 1, ci:ci + 1], cvv, ALU.mult, ALU.add, [uck, cvk, K0], [cvk])
                        self.stt('dve', cvv, ucv[:, :, 0:L], wcv[:, 0, ci:ci + 1], cvv, ALU.mult, ALU.add, [uck, cvk, K0], [cvk])
                        if kind == 'p':
                            self.cp('pool', lbp[:, ci, :, :], ucv[:, :, L:L + 2], [uck], [('lbp',)])
                        cvs.append((cv, cvk))
                    (ca, cak), (cg, cgk) = cvs
                    t1, t1k = R_t.next()
                    self.act(t1[:, 0:W], cg[:, 0:W], AF.Square, [cgk], [t1k])
                    self.ts('dve', t1[:, 0:W], t1[:, 0:W], 0.044715, 1.0, ALU.mult, ALU.add, [t1k], [t1k])
                    self.tt('dve', t1[:, 0:W], t1[:, 0:W], cg[:, 0:W], ALU.mult, [t1k, cgk], [t1k])
                    t2, t2k = R_t.next()
                    self.act(t2[:, 0:W], t1[:, 0:W], AF.Sigmoid, [t1k], [t2k], scale=1.5957691216057308)
                    self.tt('dve', t2[:, 0:W], t2[:, 0:W], cg[:, 0:W], ALU.mult, [t2k, cgk], [t2k])
                    self.tt('dve', hT[:, ch, 0:W], t2[:, 0:W], ca[:, 0:W], ALU.mult, [t2k, cak], [('hT',)])
                for b in range(nblk):
                    P = 128 if kind == 'p' else 32
                    ss, sk = R['ss'].next()
                    junk, jk = R['junk'].next()
                    pss = []
                    for half in range(2):
                        ps, pk = self.psn()
                        for ch in range(NCH):
                            self.mm(ps[0:P, :], hT[:, ch, b * 128:b * 128 + P], W_dn[:, ch, half * 512:(half + 1) * 512],
                                    ch == 0, ch == NCH - 1, [('hT',), K0], [pk])
                        self.act(junk[0:P, 0:512], ps[0:P, :], AF.Square, [pk], [jk, sk], accum=ss[0:P, half:half + 1])
                        pss.append((ps, pk))
                    self.tt('dve', ss[0:P, 2:3], ss[0:P, 0:1], ss[0:P, 1:2], ALU.add, [sk], [sk])
                    self.ts('dve', ss[0:P, 2:3], ss[0:P, 2:3], 1.0 / D, EPS, ALU.mult, ALU.add, [sk], [sk])
                    self.rsqrt_col(ss[0:P, 3:4], ss[0:P, 2:3], sk)
                    y, yk = R_y.next()
                    for half in range(2):
                        cs_ = slice(half * 512, (half + 1) * 512)
                        ps, pk = pss[half]
                        self.stt('dve', y[0:P, cs_], ps[0:P, :], ss[0:P, 3:4], gpo[0:P, cs_], ALU.mult, ALU.mult, [pk, sk, K0], [yk])
                    self.tt('dve', y[0:P, :], y[0:P, :], xmg[0:P, b, :], ALU.add, [yk, ('xmg', b)], [yk])
                    if kind == 's':
                        self.out_events.append(self.dma('sp', o_ys[:, :], y[0:32, :], [yk], []))
                    elif n0 + b > OWN0:
                        ro = (n0 + b - OWN0 - 1) * 128
                        self.out_events.append(self.dma('sp', o_y[ro:ro + 128, :], y[:, :], [yk], []))
                last_p = (kind == 'p' and n0 + nblk == NB)
                if last_p or kind == 's':
                    P = 128 if kind == 'p' else 32
                    bcol = (nblk - 1) * 128
                    for cgp in range(2 * DFF // 512):
                        wt, wtk = R_wt.next()
                        self.dma('sp', wt[:], wup_scr[:, :, cgp * 512:(cgp + 1) * 512], [], [wtk])
                        ps, pk = self.psn()
                        for kc in range(8):
                            self.mm(ps[0:P, :], h2T[:, kc, bcol:bcol + P], wt[:, kc, :], kc == 0, kc == 7, [('h2T',), wtk], [pk])
                        ut, utk = R_ut.next()
                        self.cp('act', ut[0:P, :], ps[0:P, :], [pk], [utk])
                        if kind == 'p':
                            self.out_events.append(self.dma('sp', o_conv[:, cgp * 512:(cgp + 1) * 512], ut[126:128, :], [utk], []))
                        else:
                            for s in range(NSQ):
                                self.out_events.append(self.dma('sp', o_convs[2 * s:2 * s + 2, cgp * 512:(cgp + 1) * 512],
                                                                ut[8 * s + 6:8 * s + 8, :], [utk], []))
        return self.finish_all()

    def finish_all(self):
        S = self.S
        for ev in self.out_events:
            S.need('sp', ev)
        S.finish()
        self.es.close()
        return self.nc


_CACHE = {}


def _get_builder(NB, QG, NPG, NPOOL):
    key = (NB, QG, NPG, NPOOL)
    if key not in _CACHE:
        b = Builder(NB, QG, NPG, NPOOL)
        b.build()
        _CACHE[key] = b
    return _CACHE[key]


def kernel(x_prompt, x_sample, cache_k, cache_v, cache_logf, state_gla, state_conv, page_table,
           g_pre_mix, w_in, b_f, w_alpha2, b_alpha, g_gla, w_proj_a, w_proj_b, w_out, g_post_mix,
           g_pre_ffn, w_up, w_conv, b_conv, w_down, g_post_ffn):
    f32 = np.float32
    x_prompt = np.asarray(x_prompt, f32)
    B, SEQ, _ = x_prompt.shape
    DB, DS, _ = np.asarray(x_sample).shape
    NPOOL = np.asarray(cache_k).shape[1]
    NPG = np.asarray(page_table).shape[1]
    assert B == 2 and DB == 32 and DS == 8 and SEQ % 2048 == 0
    NB = SEQ // 128
    NQ = NB // 4
    QT_ = SEQ // 4
    QG = min(4, NQ)
    bld = _get_builder(NB, QG, NPG, NPOOL)
    consts = host_consts(QG)
    ck = np.ascontiguousarray(np.asarray(cache_k, f32)[0].reshape(NPOOL * 128, AW))
    cv = np.ascontiguousarray(np.asarray(cache_v, f32)[0].reshape(NPOOL * 128, AW))
    clf = np.ascontiguousarray(np.asarray(cache_logf, f32)[0].reshape(NPOOL * 128, AH))
    shared = {
        'cache_k': ck, 'cache_v': cv, 'cache_lf': clf,
        'w_in': np.ascontiguousarray(np.asarray(w_in, f32)[0]),
        'w_alpha2': np.ascontiguousarray(np.asarray(w_alpha2, f32)[0]),
        'b_alpha': np.asarray(b_alpha, f32).reshape(1, AW), 'b_f': np.asarray(b_f, f32).reshape(1, AH),
        'g_pre_mix': np.asarray(g_pre_mix, f32).reshape(1, D), 'g_gla': np.asarray(g_gla, f32).reshape(1, BV),
        'w_proj_a': np.ascontiguousarray(np.asarray(w_proj_a, f32)[0]),
        'w_proj_b': np.ascontiguousarray(np.asarray(w_proj_b, f32)[0]),
        'w_out': np.ascontiguousarray(np.asarray(w_out, f32)[0]),
        'g_post_mix': np.asarray(g_post_mix, f32).reshape(1, D), 'g_pre_ffn': np.asarray(g_pre_ffn, f32).reshape(1, D),
        'w_up': np.ascontiguousarray(np.asarray(w_up, f32)[0]),
        'w_conv': np.ascontiguousarray(np.asarray(w_conv, f32)[0]),
        'b_conv': np.asarray(b_conv, f32).reshape(1, 2 * DFF),
        'w_down': np.ascontiguousarray(np.asarray(w_down, f32)[0]),
        'g_post_ffn': np.asarray(g_post_ffn, f32).reshape(1, D),
    }
    shared.update(consts)
    xs_all = np.asarray(x_sample, f32)
    pt_all = np.asarray(page_table, np.int32)
    sg = np.asarray(state_gla, f32)[0]
    sc = np.asarray(state_conv, f32)[0]
    in_maps = []
    for c in range(8):
        b, j = c // 4, c % 4
        xc = np.zeros((SEQ, D), f32)
        xc[(3 - j) * QT_:] = x_prompt[b, :(j + 1) * QT_]
        km = np.zeros((128, NB), f32)
        km[:, :(3 - j) * NQ] = NEG
        m = dict(shared)
        m.update({
            'xc': xc, 'kmask': km,
            'xs': np.ascontiguousarray(xs_all[4 * c:4 * c + 4].reshape(32, D)),
            'pt': np.ascontiguousarray(pt_all[4 * c:4 * c + 4].reshape(1, 4 * NPG)),
            'sgla': np.ascontiguousarray(sg[4 * c:4 * c + 4].reshape(4 * BH * 128, BV)),
            'sconv': np.ascontiguousarray(sc[4 * c:4 * c + 4].reshape(8, 2 * DFF)),
        })
        in_maps.append(m)
    res = run_bass_kernel_spmd(bld.nc, in_maps, core_ids=list(range(8)))
    R = res.results
    y_p = np.zeros((B, SEQ, D), f32)
    k_p = np.zeros((1, B, SEQ, AH, AD), f32)
    v_p = np.zeros((1, B, SEQ, AH, AD), f32)
    lf_p = np.zeros((1, B, SEQ, AH), f32)
    gla_p = np.zeros((1, B, BH, BK, BV), f32)
    conv_p = np.zeros((1, B, 2, 2 * DFF), f32)
    y_s = np.zeros((DB, DS, D), f32)
    k_s = np.zeros((1, DB, DS, AH, AD), f32)
    v_s = np.zeros((1, DB, DS, AH, AD), f32)
    lf_s = np.zeros((1, DB, DS, AH), f32)
    gla_s = np.zeros((1, DB, BH, BK, BV), f32)
    conv_s = np.zeros((1, DB, 2, 2 * DFF), f32)
    for c in range(8):
        b, j = c // 4, c % 4
        r = R[c]
        sl = slice(j * QT_, (j + 1) * QT_)
        y_p[b, sl] = r['o_y']
        k_p[0, b, sl] = r['o_k'].reshape(QT_, AH, AD)
        v_p[0, b, sl] = r['o_v'].reshape(QT_, AH, AD)
        lf_p[0, b, sl] = r['o_lf']
        if j == 3:
            gla_p[0, b] = r['o_gla'].reshape(BH, BK, BV)
            conv_p[0, b] = r['o_conv']
        ss = slice(4 * c, 4 * c + 4)
        y_s[ss] = r['o_ys'].reshape(4, DS, D)
        k_s[0, ss] = r['o_ks'].reshape(4, DS, AH, AD)
        v_s[0, ss] = r['o_vs'].reshape(4, DS, AH, AD)
        lf_s[0, ss] = r['o_lfs'].reshape(4, DS, AH)
        gla_s[0, ss] = r['o_glas'].reshape(4, BH, BK, BV)
        conv_s[0, ss] = r['o_convs'].reshape(4, 2, 2 * DFF)
    return (y_p, y_s, k_p, v_p, lf_p, gla_p, conv_p, k_s, v_s, lf_s, gla_s, conv_s)
```
